# Optimizing a Trainium2 kernel written in Bass

```python
import math
import jax, jax.numpy as jnp
from jax import lax
import numpy as np

D_MODEL = 1024
BATCH = 2
SEQ = 8192
DEPTH = 2
DEC_BATCH = 32
DEC_SEQ = 4
PAST_LEN = 8192
PAGE_SIZE = 128

D_MIX = D_MODEL
ATT_HEADS = 8
ATT_HEAD_DIM = 64
ATT_WIDTH = ATT_HEADS * ATT_HEAD_DIM
DN_WIDTH = D_MIX - ATT_WIDTH
DN_HEADS = 4
DN_HEAD_DIM = DN_WIDTH // DN_HEADS
DN_QKV = 3 * DN_WIDTH
D_IN = 4 * ATT_WIDTH + DN_QKV + DN_WIDTH + 2 * DN_HEADS
MOBA_BLOCK = 256
MOBA_TOPK = 3
Q_BLOCK = 64
DN_CHUNK = 64
CONV_WIDTH = 4
ROPE_THETA = 10000.0
DEEPNORM_ALPHA = (2 * DEPTH) ** 0.25
DEEPNORM_BETA = (8 * DEPTH) ** -0.25
LN_EPS = 1e-5
RMS_EPS = 1e-6
L2_EPS = 1e-6
ADA_SCALE = 0.5

kernel_name = 'hymba_moba_gdn_deepnorm_adaln_step'


def _rope(x, pos):
    half = x.shape[-1] // 2
    inv_freq = ROPE_THETA ** (-jnp.arange(half, dtype=jnp.float32) / half)
    ang = pos.astype(jnp.float32)[:, None] * inv_freq[None, :]
    cos = jnp.cos(ang)[:, None, :]
    sin = jnp.sin(ang)[:, None, :]
    xf = x.astype(jnp.float32)
    x1, x2 = xf[..., :half], xf[..., half:]
    return jnp.concatenate([x1 * cos - x2 * sin, x1 * sin + x2 * cos], axis=-1).astype(x.dtype)


def _l2norm(a):
    return a * lax.rsqrt(jnp.sum(a * a, axis=-1, keepdims=True) + L2_EPS)


def _moba_attention(q, k, v, q_pos):
    N, T, H, Dh = q.shape
    L = k.shape[1]
    nb = -(-L // MOBA_BLOCK)
    pad = nb * MOBA_BLOCK - L
    kb = jnp.pad(k, ((0, 0), (0, pad), (0, 0), (0, 0))).reshape(N, nb, MOBA_BLOCK, H, Dh)
    vb = jnp.pad(v, ((0, 0), (0, pad), (0, 0), (0, 0))).reshape(N, nb, MOBA_BLOCK, H, Dh)
    k_mean = jnp.mean(kb.astype(jnp.float32), axis=2)
    kb = kb.transpose(0, 3, 1, 2, 4)
    vb = vb.transpose(0, 3, 1, 2, 4)
    n_sel = min(MOBA_TOPK, nb)
    scale = Dh ** -0.5
    n_ix = jnp.arange(N)[:, None, None, None]
    h_ix = jnp.arange(H)[None, None, :, None]
    blk = jnp.arange(nb, dtype=jnp.int32)
    offs = jnp.arange(MOBA_BLOCK, dtype=jnp.int32)
    cq = Q_BLOCK if T % Q_BLOCK == 0 else T
    nq = T // cq

    def one_block(args):
        qc, pc = args
        qf = qc.astype(jnp.float32)
        own = pc // MOBA_BLOCK
        gate = jnp.einsum('nchd,njhd->nchj', qf, k_mean)
        past_ok = blk[None, :] < own[:, None]
        gate = jnp.where(past_ok[None, :, None, :], gate, -jnp.inf)
        _, top = lax.top_k(gate, n_sel)
        own_b = jnp.broadcast_to(own[None, :, None, None], (N, cq, H, 1)).astype(top.dtype)
        sel = jnp.concatenate([top, own_b], axis=-1)
        kg = kb[n_ix, h_ix, sel].astype(jnp.float32)
        vg = vb[n_ix, h_ix, sel].astype(jnp.float32)
        logits = jnp.einsum('nchd,nchsbd->nchsb', qf, kg) * scale
        slot_ok = jnp.concatenate(
            [jnp.arange(n_sel)[None, :] < own[:, None], jnp.ones((cq, 1), dtype=bool)], axis=-1)
        kpos = sel[..., None] * MOBA_BLOCK + offs
        ok = slot_ok[None, :, None, :, None] & (kpos <= pc[None, :, None, None, None])
        logits = jnp.where(ok, logits, -jnp.inf)
        p = jax.nn.softmax(logits.reshape(N, cq, H, -1), axis=-1).reshape(logits.shape)
        out = jnp.einsum('nchsb,nchsbd->nchd', p, vg)
        return out.astype(q.dtype)

    qs = q.reshape(N, nq, cq, H, Dh).transpose(1, 0, 2, 3, 4)
    ps = q_pos.reshape(nq, cq)
    out = lax.map(one_block, (qs, ps))
    return out.transpose(1, 0, 2, 3, 4).reshape(N, T, H, Dh)


def _gated_delta_rule(q, k, v, g, beta, S0):
    N, T, H, Dk = q.shape
    Dv = v.shape[-1]
    C = min(DN_CHUNK, T)
    nc = -(-T // C)
    pad = nc * C - T

    def prep(a):
        a = jnp.pad(a, [(0, 0), (0, pad)] + [(0, 0)] * (a.ndim - 2))
        a = a.reshape((N, nc, C) + a.shape[2:])
        a = jnp.moveaxis(a, 3, 2)
        return jnp.moveaxis(a, 1, 0)

    qc, kc, vc, gc, bc = prep(q), prep(k), prep(v), prep(g), prep(beta)
    gcum = jnp.cumsum(gc, axis=-1)
    tril = jnp.tril(jnp.ones((C, C), dtype=bool))
    stril = jnp.tril(jnp.ones((C, C), dtype=bool), -1)
    diff = gcum[..., :, None] - gcum[..., None, :]
    decay = jnp.where(tril, jnp.exp(jnp.where(tril, diff, 0.0)), 0.0)
    kbeta = kc * bc[..., None]
    lmat = jnp.where(stril, jnp.einsum('...id,...jd->...ij', kbeta, kc) * decay, 0.0)
    eye = jnp.eye(C, dtype=jnp.float32)
    tmat = lax.linalg.triangular_solve(eye + lmat, jnp.broadcast_to(eye, lmat.shape),
                                       left_side=True, lower=True, unit_diagonal=True)
    u = tmat @ (vc * bc[..., None])
    w = tmat @ (kbeta * jnp.exp(gcum)[..., None])
    attn = jnp.where(tril, jnp.einsum('...id,...jd->...ij', qc, kc) * decay, 0.0)

    def step(S, xs):
        q_i, k_i, u_i, w_i, g_i, a_i = xs
        v_new = u_i - w_i @ S
        o = (q_i * jnp.exp(g_i)[..., None]) @ S + a_i @ v_new
        g_last = g_i[..., -1]
        S = S * jnp.exp(g_last)[..., None, None] + jnp.einsum(
            'nhck,nhcv->nhkv', k_i * jnp.exp(g_last[..., None] - g_i)[..., None], v_new)
        return S, o

    S, o = lax.scan(step, S0, (qc, kc, u, w, gcum, attn))
    o = jnp.moveaxis(o, 0, 1)
    o = jnp.swapaxes(o, 2, 3).reshape(N, nc * C, H, Dv)[:, :T]
    return o, S


def _layer(x, c, pos0, k_past, v_past, S0, conv0,
           w_ada, b_ada, w_in, conv_w, a_log, dt_bias, dn_norm_w, w_out, ln_g, ln_b):
    N, T, _ = x.shape
    f32 = jnp.float32
    mod = jax.nn.silu(c) @ w_ada + b_ada
    shift, scale, gate = jnp.split(mod, 3, axis=-1)
    h = x * (1 + scale[:, None, :]) + shift[:, None, :]
    proj = h @ w_in
    sizes = [ATT_WIDTH] * 4 + [DN_QKV, DN_WIDTH, DN_HEADS, DN_HEADS]
    cuts = [int(s) for s in np.cumsum(sizes)[:-1]]
    qa, ka, va, za, qkv_d, zd, b_raw, a_raw = jnp.split(proj, cuts, axis=-1)
    positions = pos0 + jnp.arange(T, dtype=jnp.int32)

    qa = _rope(qa.reshape(N, T, ATT_HEADS, ATT_HEAD_DIM), positions)
    ka = _rope(ka.reshape(N, T, ATT_HEADS, ATT_HEAD_DIM), positions)
    va = va.reshape(N, T, ATT_HEADS, ATT_HEAD_DIM)
    k_all = jnp.concatenate([k_past.astype(ka.dtype), ka], axis=1)
    v_all = jnp.concatenate([v_past.astype(va.dtype), va], axis=1)
    oa = _moba_attention(qa, k_all, v_all, positions).reshape(N, T, ATT_WIDTH)
    ya = oa * jax.nn.silu(za)

    xpad = jnp.concatenate([conv0.astype(qkv_d.dtype), qkv_d], axis=1)
    conv = sum(xpad[:, i:i + T] * conv_w[i] for i in range(CONV_WIDTH))
    conv = jax.nn.silu(conv.astype(f32))
    new_conv = xpad[:, T:]
    qd, kd, vd = jnp.split(conv, 3, axis=-1)
    qd = _l2norm(qd.reshape(N, T, DN_HEADS, DN_HEAD_DIM)) * (DN_HEAD_DIM ** -0.5)
    kd = _l2norm(kd.reshape(N, T, DN_HEADS, DN_HEAD_DIM))
    vd = vd.reshape(N, T, DN_HEADS, DN_HEAD_DIM)
    beta = jax.nn.sigmoid(b_raw.astype(f32))
    g = -jnp.exp(a_log.astype(f32)) * jax.nn.softplus(a_raw.astype(f32) + dt_bias.astype(f32))
    od, S_new = _gated_delta_rule(qd, kd, vd, g, beta, S0.astype(f32))
    od = od * lax.rsqrt(jnp.mean(od * od, axis=-1, keepdims=True) + RMS_EPS) * dn_norm_w.astype(f32)
    yd = (od.reshape(N, T, DN_WIDTH) * jax.nn.silu(zd.astype(f32))).astype(x.dtype)

    y = jnp.concatenate([ya.astype(x.dtype), yd], axis=-1) @ w_out
    r = (DEEPNORM_ALPHA * x + (1 + gate[:, None, :]) * y).astype(f32)
    mu = jnp.mean(r, axis=-1, keepdims=True)
    var = jnp.mean(jnp.square(r - mu), axis=-1, keepdims=True)
    out = (r - mu) * lax.rsqrt(var + LN_EPS) * ln_g.astype(f32) + ln_b.astype(f32)
    return out.astype(x.dtype), ka, va, S_new, new_conv


def setup_inputs(seed: int = 0) -> dict:
    key = jax.random.key(seed)
    ks = jax.random.split(key, 24)
    f32 = jnp.float32
    n_pages = PAST_LEN // PAGE_SIZE
    n_pool = (DEC_BATCH * n_pages * 5) // 4
    nrm = jax.random.normal
    x_prompt = nrm(ks[0], (BATCH, SEQ, D_MODEL), f32)
    x_sample = nrm(ks[1], (DEC_BATCH, DEC_SEQ, D_MODEL), f32)
    c_prompt = nrm(ks[2], (BATCH, D_MODEL), f32)
    c_sample = nrm(ks[3], (DEC_BATCH, D_MODEL), f32)
    cache_k = nrm(ks[4], (DEPTH, n_pool, PAGE_SIZE, ATT_HEADS, ATT_HEAD_DIM), f32)
    cache_v = nrm(ks[5], (DEPTH, n_pool, PAGE_SIZE, ATT_HEADS, ATT_HEAD_DIM), f32)
    page_table = jax.random.permutation(ks[6], n_pool)[:DEC_BATCH * n_pages].reshape(
        DEC_BATCH, n_pages).astype(jnp.int32)
    state_ssm = 0.1 * nrm(ks[7], (DEPTH, DEC_BATCH, DN_HEADS, DN_HEAD_DIM, DN_HEAD_DIM), f32)
    state_conv = nrm(ks[8], (DEPTH, DEC_BATCH, CONV_WIDTH - 1, DN_QKV), f32)
    w_ada = nrm(ks[9], (DEPTH, D_MODEL, 3 * D_MODEL), f32) * (ADA_SCALE * D_MODEL ** -0.5)
    b_ada = 0.02 * nrm(ks[10], (DEPTH, 3 * D_MODEL), f32)
    w_in = nrm(ks[11], (DEPTH, D_MODEL, D_IN), f32) * (D_MODEL ** -0.5)
    conv_w = nrm(ks[12], (DEPTH, CONV_WIDTH, DN_QKV), f32) * (CONV_WIDTH ** -0.5)
    a_log = jnp.log(jax.random.uniform(ks[13], (DEPTH, DN_HEADS), f32, 1.0, 16.0))
    dt = jnp.exp(jax.random.uniform(ks[14], (DEPTH, DN_HEADS), f32, math.log(1e-3), math.log(1e-1)))
    dt_bias = dt + jnp.log(-jnp.expm1(-dt))
    dn_norm_w = 1.0 + 0.02 * nrm(ks[15], (DEPTH, DN_HEAD_DIM), f32)
    w_out = nrm(ks[16], (DEPTH, D_MIX, D_MODEL), f32) * (D_MIX ** -0.5 * DEEPNORM_BETA)
    ln_g = 1.0 + 0.02 * nrm(ks[17], (DEPTH, D_MODEL), f32)
    ln_b = 0.02 * nrm(ks[18], (DEPTH, D_MODEL), f32)
    return {'x_prompt': x_prompt, 'x_sample': x_sample, 'c_prompt': c_prompt, 'c_sample': c_sample,
            'cache_k': cache_k, 'cache_v': cache_v, 'page_table': page_table,
            'state_ssm': state_ssm, 'state_conv': state_conv,
            'w_ada': w_ada, 'b_ada': b_ada, 'w_in': w_in, 'conv_w': conv_w, 'a_log': a_log,
            'dt_bias': dt_bias, 'dn_norm_w': dn_norm_w, 'w_out': w_out, 'ln_g': ln_g, 'ln_b': ln_b}


def reference(x_prompt, x_sample, c_prompt, c_sample, cache_k, cache_v, page_table,
              state_ssm, state_conv, w_ada, b_ada, w_in, conv_w, a_log, dt_bias,
              dn_norm_w, w_out, ln_g, ln_b):
    n_pages = PAST_LEN // PAGE_SIZE
    bp = x_prompt.shape[0]
    bs = x_sample.shape[0]
    xp, xs = x_prompt, x_sample
    kp_l, vp_l, sp_l, cp_l = [], [], [], []
    ks_l, vs_l, ss_l, cs_l = [], [], [], []
    for l in range(DEPTH):
        lw = (w_ada[l], b_ada[l], w_in[l], conv_w[l], a_log[l], dt_bias[l],
              dn_norm_w[l], w_out[l], ln_g[l], ln_b[l])
        k0 = jnp.zeros((bp, 0, ATT_HEADS, ATT_HEAD_DIM), xp.dtype)
        s0 = jnp.zeros((bp, DN_HEADS, DN_HEAD_DIM, DN_HEAD_DIM), jnp.float32)
        c0 = jnp.zeros((bp, CONV_WIDTH - 1, DN_QKV), xp.dtype)
        xp, kp, vp, sp, cp = _layer(xp, c_prompt, 0, k0, k0, s0, c0, *lw)
        k_past = cache_k[l][page_table].reshape(bs, n_pages * PAGE_SIZE, ATT_HEADS, ATT_HEAD_DIM)
        v_past = cache_v[l][page_table].reshape(bs, n_pages * PAGE_SIZE, ATT_HEADS, ATT_HEAD_DIM)
        xs, kn, vn, sn, cn = _layer(xs, c_sample, n_pages * PAGE_SIZE, k_past, v_past,
                                    state_ssm[l], state_conv[l], *lw)
        kp_l.append(kp); vp_l.append(vp); sp_l.append(sp.astype(state_ssm.dtype)); cp_l.append(cp)
        ks_l.append(kn); vs_l.append(vn); ss_l.append(sn.astype(state_ssm.dtype)); cs_l.append(cn)
    new_k_prompt = jnp.stack(kp_l)
    new_v_prompt = jnp.stack(vp_l)
    new_ssm_prompt = jnp.stack(sp_l)
    new_conv_prompt = jnp.stack(cp_l)
    new_k_sample = jnp.stack(ks_l)
    new_v_sample = jnp.stack(vs_l)
    new_ssm_sample = jnp.stack(ss_l)
    new_conv_sample = jnp.stack(cs_l)
    return (xp, xs, new_k_prompt, new_v_prompt, new_ssm_prompt, new_conv_prompt,
            new_k_sample, new_v_sample, new_ssm_sample, new_conv_sample)
```

```python
import math
import numpy as np
import ml_dtypes
import concourse.bass as bass
import concourse.mybir as mybir
from concourse.bass_utils import run_bass_kernel_spmd

F32 = mybir.dt.float32
BF16 = mybir.dt.bfloat16
I32 = mybir.dt.int32
AF = mybir.ActivationFunctionType
ALU = mybir.AluOpType

D = 1024
NEG = -30000.0
L2_EPS = 1e-6
RMS_EPS = 1e-6
LN_EPS = 1e-5


class Cfg:
    def __init__(self, seq=8192, past=8192, depth=2, dec_batch=32, batch=2, ncores=2):
        self.SEQ = seq
        self.PAST = past
        self.DEPTH = depth
        self.DECB = dec_batch
        self.BATCH = batch
        self.NT = seq // 128
        self.NPAGES = past // 128
        self.NPOOL = (dec_batch * self.NPAGES * 5) // 4
        self.NK = max(seq, past)
        self.NKT = self.NK // 128
        self.ALPHA = (2 * depth) ** 0.25
        self.DEBUG = False
        self.NCORES = ncores
        self.NBS = dec_batch // ncores


class Buf:
    __slots__ = ("t", "name", "w", "r", "dsem", "dcnt")

    def __init__(self, t, name):
        self.t = t
        self.name = name
        self.w = None
        self.r = {}
        self.dsem = None
        self.dcnt = 0

    def __getitem__(self, k):
        return self.t[k]


class Eng:
    def __init__(self, kb, eng, name, selfsync=True):
        self.kb = kb
        self.eng = eng
        self.name = name
        self.selfsync = selfsync
        self.sem = kb.nc.alloc_semaphore(f"p_{name}_0")
        self.nsem = 1
        self.cnt = 0
        self.waited = {}
        self.n_ins = 0

    def _wait(self, tok):
        if tok is None:
            return
        sem, val = tok
        if (not self.selfsync) and sem is self.sem:
            return
        k = id(sem)
        if self.waited.get(k, 0) >= val:
            return
        self.eng.wait_ge(sem, val)
        self.waited[k] = val
        self.n_ins += 1
        self.n_wait = getattr(self, "n_wait", 0) + 1

    def _pre(self, reads, writes):
        for b in reads:
            self._wait(b.w)
        for b in writes:
            self._wait(b.w)
            for t in b.r.values():
                self._wait(t)

    def _post(self, tok, reads, writes):
        sem, val = tok
        for b in reads:
            b.r[id(sem)] = tok
        for b in writes:
            b.w = tok
            b.r = {}

    def op(self, name, reads, writes, *a, **kw):
        self._pre(reads, writes)
        ins = getattr(self.eng, name)(*a, **kw)
        if self.cnt >= 30000:
            self.sem = self.kb.nc.alloc_semaphore(f"p_{self.name}_{self.nsem}")
            self.nsem += 1
            self.cnt = 0
        self.cnt += 1
        ins.then_inc(self.sem, 1)
        self.n_ins += 1
        tok = (self.sem, self.cnt)
        self._post(tok, reads, writes)
        return tok

    def _dsem(self, sb):
        if sb.dsem is None:
            sb.dsem = self.kb.nc.alloc_semaphore(f"d_{sb.name}")
        return sb.dsem

    def dma(self, out, in_, reads, writes, sb, **kw):
        self._pre(reads, writes)
        sem = self._dsem(sb)
        ins = self.eng.dma_start(out=out, in_=in_, **kw)
        sb.dcnt += 1
        ins.then_inc(sem, 16)
        self.n_ins += 1
        tok = (sem, 16 * sb.dcnt)
        self._post(tok, reads, writes)
        return tok

    def idma(self, out, in_, idx_ap, reads, writes, sb):
        self._pre(reads, writes)
        sem = self._dsem(sb)
        ins = self.eng.indirect_dma_start(
            out=out, out_offset=None, in_=in_,
            in_offset=bass.IndirectOffsetOnAxis(ap=idx_ap, axis=0))
        sb.dcnt += 1
        ins.then_inc(sem, 16)
        self.n_ins += 1
        tok = (sem, 16 * sb.dcnt)
        self._post(tok, reads, writes)
        return tok

    def wait_all(self, bufs):
        for b in bufs:
            self._wait(b.w)
            for t in b.r.values():
                self._wait(t)


class KB:
    def __init__(self):
        self.nc = bass.Bass("TRN2", target_bir_lowering=False)
        nc = self.nc
        self.pe = Eng(self, nc.tensor, "pe", selfsync=False)
        self.act = Eng(self, nc.scalar, "act")
        self.dve = Eng(self, nc.vector, "dve")
        self.pool = Eng(self, nc.gpsimd, "pool")
        self.sp = Eng(self, nc.sync, "sp")
        self.ext_in = {}
        self.ext_out = {}

    def sb(self, name, shape, dt=F32):
        return Buf(self.nc.alloc_sbuf_tensor(name, list(shape), dt), name)

    def ps(self, name, shape, dt=F32):
        return Buf(self.nc.alloc_psum_tensor(name, list(shape), dt), name)

    def dram(self, name, shape, dt=F32, kind="Internal"):
        b = Buf(self.nc.dram_tensor(name, list(shape), dt, kind=kind).ap(), name)
        if kind == "ExternalInput":
            self.ext_in[name] = b
        elif kind == "ExternalOutput":
            self.ext_out[name] = b
        return b

    def alias(self, buf, name):
        return Buf(buf.t, name)


def build(cfg):
    kb = KB()
    pe, act, dve, pool, sp = kb.pe, kb.act, kb.dve, kb.pool, kb.sp
    SEQ, NT, NK, NKT, NPG, DEPTH = cfg.SEQ, cfg.NT, cfg.NK, cfg.NKT, cfg.NPAGES, cfg.DEPTH
    NPOOLROWS = DEPTH * cfg.NPOOL * 128 * 4

    def IN(name, shape, dt=F32):
        return kb.dram(name, shape, dt, kind="ExternalInput")

    def OUT(name, shape, dt=F32):
        return kb.dram(name, shape, dt, kind="ExternalOutput")

    xp = IN("xp", [SEQ, D]); xs = IN("xs", [128, D])
    cTp = IN("cTp", [128, 8, 128]); cTs = IN("cTs", [128, 8, 128])
    w_ada = IN("w_ada", [DEPTH, D, 3 * D]); b_ada = IN("b_ada", [DEPTH, 3 * D])
    w_in = IN("w_in", [DEPTH, 4, D, 1026])
    conv_w = IN("conv_w", [DEPTH, 4, 128, 12])
    alog = IN("alog", [DEPTH, 4, 128, 1]); dtb = IN("dtb", [DEPTH, 4, 128, 1])
    dnw = IN("dnw", [DEPTH, 128, 128])
    w_out = IN("w_out", [DEPTH, D, D])
    ln_g = IN("ln_g", [DEPTH, D]); ln_b = IN("ln_b", [DEPTH, D])
    cache_k = IN("cache_k", [NPOOLROWS, 128]); cache_v = IN("cache_v", [NPOOLROWS, 128])
    NBS = cfg.NBS
    ptab = IN("ptab", [NBS, NPG], I32)
    ssm0 = IN("ssm0", [DEPTH, NBS, 4, 128, 128]); sconv = IN("sconv", [DEPTH, NBS, 4, 128, 9])
    c_identf = IN("c_identf", [128, 128]); c_identb = IN("c_identb", [128, 128], BF16)
    c_tri = IN("c_tri", [128, 2, 128])
    c_stair = IN("c_stair", [128, 4, 512], BF16)
    c_eblk = IN("c_eblk", [32, NK + 128], BF16)
    c_cosp = IN("c_cosp", [SEQ, 128]); c_sinp = IN("c_sinp", [SEQ, 128])
    c_coss = IN("c_coss", [128, 128]); c_sins = IN("c_sins", [128, 128])
    c_iota4 = IN("c_iota4", [128, 1])
    c_esel = IN("c_esel", [4, NBS, 128], BF16)
    c_bdm = IN("c_bdm", [128, 7, 128], BF16)

    yp = OUT("yp", [SEQ, D]); ys = OUT("ys", [128, D])
    nkp = OUT("nkp", [DEPTH, SEQ, 512]); nvp = OUT("nvp", [DEPTH, SEQ, 512])
    ssmp = OUT("ssmp", [DEPTH, 4, 128, 128]); convp = OUT("convp", [DEPTH, 4, 128, 9])
    nks = OUT("nks", [DEPTH, 128, 512]); nvs = OUT("nvs", [DEPTH, 128, 512])
    ssms = OUT("ssms", [DEPTH, NBS, 4, 128, 128]); convs = OUT("convs", [DEPTH, NBS, 4, 128, 9])
    out_bufs = [yp, ys, nkp, nvp, ssmp, convp, nks, nvs, ssms, convs]

    mixp = kb.dram("mixp", [SEQ, D], BF16, kind="ExternalOutput" if cfg.DEBUG else "Internal")
    dbg_mixs = kb.dram("dbg_mixs", [DEPTH, 128, D], BF16, kind="ExternalOutput") if cfg.DEBUG else None
    x1 = kb.dram("x1", [SEQ, D], F32)

    identf = kb.sb("identf", [128, 128]); identb = kb.sb("identb", [128, 128], BF16)
    tri = kb.sb("tri", [128, 2, 128])
    stair = kb.sb("stair", [128, 4, 512], BF16)
    iota4 = kb.sb("iota4", [128, 1]); esel = kb.sb("esel", [4, NBS, 128], BF16)
    ones_f = kb.sb("ones_f", [128, 128]); ones_b = kb.sb("ones_b", [128, 1], BF16)
    coss = kb.sb("coss", [128, 128]); sins = kb.sb("sins", [128, 128])
    for dst, src in ((identf, c_identf), (identb, c_identb), (iota4, c_iota4), (coss, c_coss), (sins, c_sins)):
        sp.dma(dst[:], src[:, :], [src], [dst], dst)
    sp.dma(tri[:], c_tri[:, :, :], [c_tri], [tri], tri)
    sp.dma(stair[:], c_stair[:, :, :], [c_stair], [stair], stair)
    sp.dma(esel[:], c_esel[:, :, :], [c_esel], [esel], esel)
    bdm = kb.sb("bdm", [128, 7, 128], BF16)
    sp.dma(bdm[:], c_bdm[:, :, :], [c_bdm], [bdm], bdm)
    dve.op("memset", [], [ones_f], ones_f[:], 1.0)
    dve.op("memset", [], [ones_b], ones_b[:], 1.0)

    KT = [kb.sb(f"KT{h}", [96, NK + 128], BF16) for h in range(2)]
    VA = kb.sb("VA", [128, NKT + 1, 2, 65], BF16)
    for h in range(2):
        sp.dma(KT[h][64:96, :], c_eblk[:, :], [c_eblk], [KT[h]], KT[h])
    dve.op("memset", [], [VA], VA[:], 1.0)
    kmT = [kb.sb(f"kmT{h}", [64, 32], BF16) for h in range(2)]
    ksum = kb.sb("ksum", [64, 2])
    QT = [kb.sb(f"QT{h}", [96, 512], BF16) for h in range(2)]
    mod_p = kb.sb("mod_p", [128, 3 * D]); mod_s = kb.sb("mod_s", [128, 3 * D])
    wA = kb.sb("wA", [128, 8, 512], BF16); wZ = kb.sb("wZ", [128, 8, 130], BF16); wD = kb.sb("wD", [128, 8, 384], BF16)
    wstg = kb.sb("wstg", [128, 1026])
    wo = kb.sb("wo", [128, 8, D], BF16); wostg = wstg
    cw = kb.sb("cw", [128, 12]); aneg = kb.sb("aneg", [128, 1]); dtbt = kb.sb("dtbt", [128, 1]); dnwt = kb.sb("dnwt", [128, 128])
    lng = kb.sb("lng", [128, D]); lnb = kb.sb("lnb", [128, D])
    xt = kb.sb("xt", [128, D]); xst = kb.sb("xst", [128, D])
    hb = kb.sb("hb", [128, D], BF16); hT = kb.sb("hT", [128, 8, 128], BF16)
    pA = kb.sb("pA", [128, 512]); pZ = kb.sb("pZ", [128, 130])
    xhist = kb.sb("xhist", [128, 3, 131]); xhs = kb.sb("xhs", [128, 3, 8])
    cost = kb.sb("cost", [128, 128]); sint = kb.sb("sint", [128, 128])
    rtmp = kb.sb("rtmp", [128, 4, 128]); qkr = kb.sb("qkr", [128, 256])
    kvb = kb.sb("kvb", [128, 256], BF16)
    qa = kb.sb("qa", [128, 2, 96], BF16)
    qTt = [kb.sb(f"qTt{h}", [64, 128], BF16) for h in range(2)]
    gate_sb = kb.sb("gate_sb", [128, 2, 32]); top8 = kb.sb("top8", [128, 2, 8])
    zsil = kb.sb("zsil", [128, 4, 128]); zdsil = kb.sb("zdsil", [128, 128])
    PTb = [kb.sb(f"PTb{i}", [128, 512], BF16) for i in range(2)]
    OTs = kb.sb("OTs", [65, 512]); rden = kb.sb("rden", [128, 1])
    mix = kb.sb("mix", [128, 256], BF16); mixs = kb.sb("mixs", [128, D], BF16)
    oas = kb.sb("oas", [4, 256], BF16)
    mixl = kb.sb("mixl", [128, D], BF16); mT = kb.sb("mT", [128, 8, 128], BF16)
    rt = kb.sb("rt", [128, D]); stats = kb.sb("stats", [128, 2, 6]); mv = kb.sb("mv", [128, 2]); rstd = kb.sb("rstd", [128, 1])
    cst = kb.sb("cst", [128, 8, 128]); scb = kb.sb("scb", [128, 8, 128], BF16)
    wastg = kb.sb("wastg", [128, 8, 128]); wab = kb.sb("wab", [128, 8, 128], BF16); bab = kb.sb("bab", [128, 128])
    ptb = kb.sb("ptb", [128, NPG], I32); idxb = kb.sb("idxb", [128, NPG], I32); idx = kb.sb("idx", [128, NPG], I32)
    pgk = [kb.sb(f"pgk{i}", [128, 128]) for i in range(2)]; pgv = [kb.sb(f"pgv{i}", [128, 128]) for i in range(2)]
    pgb = kb.sb("pgb", [128, 256], BF16)
    knT = [kb.sb(f"knT{h}", [64, 128], BF16) for h in range(2)]
    G = {}
    for nm in ["grep", "Dsb", "E1", "E1i", "E1s", "EG", "Xf", "Ksb", "Qsb", "Vsb", "u", "osb", "od", "S", "qkraw"]:
        G[nm] = kb.sb("g_" + nm, [128, 256] if nm in ("Xf", "qkraw") else [128, 128])
    for nm in ["NT", "attnT", "Zc", "Yn", "Zn", "Xb", "wb", "wT", "QTn", "KTn", "QeT", "Kd", "vnb", "Sb", "Kb", "Qb"]:
        G[nm] = kb.sb("g_" + nm, [128, 256] if nm == "Xb" else [128, 128], BF16)
    G["Y"] = [kb.sb(f"g_Y{i}", [128, 128], BF16) for i in range(2)]
    G["Z"] = [kb.sb(f"g_Z{i}", [128, 128], BF16) for i in range(2)]
    for nm in ["gcol", "egcol", "beta", "gv", "ssq", "rn", "kds", "sp_", "bq", "bk"]:
        G[nm] = kb.sb("g_" + nm, [128, 1])

    ps_t = kb.ps("ps_t", [128, 8, 128], BF16)
    ps_a = kb.ps("ps_a", [128, 512])
    ps_b = kb.ps("ps_b", [128, 512])
    ps_d = kb.ps("ps_d", [128, 512])
    ps_s = kb.ps("ps_s", [128, 512])
    ps_o = kb.ps("ps_o", [128, 512])
    ps_g = kb.ps("ps_g", [128, 512])
    ps_h = kb.ps("ps_h", [128, 512])

    def mm(out_buf, out_ap, lhsT, rhs, reads, start=True, stop=True):
        pe.op("matmul", reads, [out_buf], out_ap, lhsT=lhsT, rhs=rhs, start=start, stop=stop)

    def tr(out_buf, out_ap, in_ap, reads, ident_ap):
        pe.op("transpose", reads, [out_buf], out_ap, in_ap, ident_ap)

    def compute_mod(l, cT_src, mod):
        sp.dma(cst[:], cT_src[:, :, :], [cT_src], [cst], cst)
        act.op("activation", [cst], [scb], out=scb[:], in_=cst[:], func=AF.Silu)
        for cc in range(24):
            c0 = cc * 128
            sp.dma(wastg[:], w_ada[l, :, c0:c0 + 128].rearrange("(k p) c -> p k c", p=128), [w_ada], [wastg], wastg)
            sp.dma(bab[:], b_ada.t[l:l + 1, c0:c0 + 128].to_broadcast([128, 128]), [b_ada], [bab], bab)
            act.op("activation", [wastg], [wab], out=wab[:], in_=wastg[:], func=AF.Copy)
            for k in range(8):
                mm(ps_a, ps_a[:, 0:128], scb[:, k, :], wab[:, k, :], [scb, wab], start=(k == 0), stop=(k == 7))
            dve.op("tensor_tensor", [ps_a, bab], [mod], out=mod[:, c0:c0 + 128], in0=ps_a[:, 0:128], in1=bab[:], op=ALU.add)
        dve.op("tensor_scalar", [mod], [mod], out=mod[:, D:3 * D], in0=mod[:, D:3 * D], scalar1=1.0, scalar2=None, op0=ALU.add)

    def front(l, hg, xtile, mod, cos_t, sin_t):
        dve.op("tensor_tensor", [xtile, mod], [rt], out=rt[:], in0=xtile[:], in1=mod[:, D:2 * D], op=ALU.mult)
        dve.op("tensor_tensor", [rt, mod], [hb], out=hb[:], in0=rt[:], in1=mod[:, 0:D], op=ALU.add)
        for k in range(8):
            tr(ps_t, ps_t[:, k, :], hb[:, k * 128:(k + 1) * 128], [hb, identb], identb[:])
        act.op("activation", [ps_t], [hT], out=hT[:], in_=ps_t[:], func=AF.Copy)
        for k in range(8):
            mm(ps_a, ps_a[:, :], hT[:, k, :], wA[:, k, :], [hT, wA], start=(k == 0), stop=(k == 7))
        act.op("activation", [ps_a], [pA], out=pA[:], in_=ps_a[:], func=AF.Copy)
        for k in range(8):
            mm(ps_b, ps_b[:, 0:130], hT[:, k, :], wZ[:, k, :], [hT, wZ], start=(k == 0), stop=(k == 7))
        dve.op("tensor_copy", [ps_b], [pZ], out=pZ[:], in_=ps_b[:, 0:130])
        for c in range(3):
            for k in range(8):
                mm(ps_d, ps_d[:, c * 128:(c + 1) * 128], wD[:, k, c * 128:(c + 1) * 128], hT[:, k, :], [hT, wD],
                   start=(k == 0), stop=(k == 7))
        v4 = pA[:, 0:256].rearrange("p (g two f) -> p g two f", g=4, two=2)
        A_ = v4[:, :, 0, :]
        B_ = v4[:, :, 1, :]
        c4 = cos_t[:].rearrange("p (g f) -> p g f", g=4)
        s4 = sin_t[:].rearrange("p (g f) -> p g f", g=4)
        o4 = qkr[:].rearrange("p (g two f) -> p g two f", g=4, two=2)
        r4 = rtmp[:]
        dve.op("tensor_tensor", [pA, cos_t], [rtmp], out=r4[:, :, 0:32], in0=A_, in1=c4, op=ALU.mult)
        pool.op("tensor_tensor", [pA, sin_t], [rtmp], out=r4[:, :, 32:64], in0=B_, in1=s4, op=ALU.mult)
        dve.op("tensor_tensor", [rtmp], [qkr], out=o4[:, :, 0, :], in0=r4[:, :, 0:32], in1=r4[:, :, 32:64], op=ALU.subtract)
        pool.op("tensor_tensor", [pA, sin_t], [rtmp], out=r4[:, :, 64:96], in0=A_, in1=s4, op=ALU.mult)
        dve.op("tensor_tensor", [pA, cos_t], [rtmp], out=r4[:, :, 96:128], in0=B_, in1=c4, op=ALU.mult)
        dve.op("tensor_tensor", [rtmp], [qkr], out=o4[:, :, 1, :], in0=r4[:, :, 64:96], in1=r4[:, :, 96:128], op=ALU.add)
        act.op("activation", [qkr], [kvb], out=kvb[:, 0:128], in_=qkr[:, 128:256], func=AF.Copy)
        act.op("activation", [pA], [kvb], out=kvb[:, 128:256], in_=pA[:, 256:384], func=AF.Copy)
        act.op("activation", [qkr], [qa], out=qa[:, :, 0:64], in_=qkr[:, 0:128].rearrange("p (h f) -> p h f", h=2),
               func=AF.Copy, scale=0.125)

    def qT_and_gate(nblk_valid, topk):
        for h in range(2):
            tr(ps_t, ps_t[0:64, h, :], qa[:, h, 0:64], [qa, identb], identb[:])
        for h in range(2):
            act.op("activation", [ps_t], [qTt[h]], out=qTt[h][:], in_=ps_t[0:64, h, :], func=AF.Copy)
        dve.op("memset", [], [qa], qa[:, :, 64:96], 0.0)
        if topk and nblk_valid > 3:
            nb = nblk_valid
            for h in range(2):
                mm(ps_b, ps_b[:, 256 + h * 32:256 + h * 32 + nb], qTt[h][:], kmT[h][:, 0:nb], [qTt[h], kmT[h]])
            dve.op("tensor_copy", [ps_b], [gate_sb], out=gate_sb[:, :, 0:nb],
                   in_=ps_b[:, 256:320].rearrange("p (h f) -> p h f", h=2)[:, :, 0:nb])
            for h in range(2):
                dve.op("max", [gate_sb], [top8], out=top8[:, h, :], in_=gate_sb[:, h, :])
                dve.op("tensor_scalar", [gate_sb, top8], [qa], out=qa[:, h, 64:64 + nb], in0=gate_sb[:, h, 0:nb],
                       scalar1=top8[:, h, 2:3], scalar2=NEG, op0=ALU.is_lt, op1=ALU.mult)

    def qaug_T(h, dst_buf, dst_ap):
        tr(ps_t, ps_t[0:96, 2 + h, :], qa[:, h, :], [qa, identb], identb[:])
        act.op("activation", [ps_t], [dst_buf], out=dst_ap, in_=ps_t[0:96, 2 + h, :], func=AF.Copy)

    def store_kv_tile(kt, rows=128):
        for h in range(2):
            tr(ps_t, ps_t[0:64, 4 + h, :], kvb[:, h * 64:(h + 1) * 64], [kvb, identb], identb[:])
        for h in range(2):
            act.op("activation", [ps_t], [KT[h]], out=KT[h][0:64, kt * 128:(kt + 1) * 128], in_=ps_t[0:64, 4 + h, :], func=AF.Copy)
        pool.op("tensor_copy", [kvb], [VA], out=VA[:, kt, :, 0:64], in_=kvb[:, 128:256].rearrange("p (h f) -> p h f", h=2))

    def kmean_update(kt, src_kb, src_ap_fn):
        for h in range(2):
            mm(ps_b, ps_b[0:64, 384 + h:385 + h], src_ap_fn(h), ones_b[:, 0:1], [src_kb, ones_b])
        if kt % 2 == 0:
            dve.op("tensor_copy", [ps_b], [ksum], out=ksum[:], in_=ps_b[0:64, 384:386])
        else:
            j = kt // 2
            for h in range(2):
                dve.op("tensor_scalar", [ps_b, ksum], [kmT[h]], out=kmT[h][:, j:j + 1], in0=ps_b[0:64, 384 + h:385 + h],
                       scalar1=ksum[:, h:h + 1], scalar2=1.0 / 256.0, op0=ALU.add, op1=ALU.mult)

    def attention(h, q_buf, q_ap, NQ, ktiles, o_ap):
        per = max(1, 512 // NQ)
        n = len(ktiles)
        gi = 0
        first = True
        i = 0
        while i < n:
            grp = []
            while i < n and len(grp) < per and (not grp or (ktiles[i][1] == 128 and grp[0][1] == 128)):
                grp.append(ktiles[i]); i += 1
            nk = grp[0][1]
            pb = PTb[gi % 2]; gi += 1
            for j, (kap, nk_, vap, mask) in enumerate(grp):
                mm(ps_s, ps_s[0:nk, j * NQ:(j + 1) * NQ], kap, q_ap, [KT[h], q_buf], start=True, stop=(mask is None))
                if mask is not None:
                    mm(ps_s, ps_s[0:nk, j * NQ:(j + 1) * NQ], identb[0:nk, 0:nk], mask, [identb, stair], start=False, stop=True)
            act.op("activation", [ps_s], [pb], out=pb[0:nk, 0:len(grp) * NQ], in_=ps_s[0:nk, 0:len(grp) * NQ], func=AF.Exp)
            for j, (kap, nk_, vap, mask) in enumerate(grp):
                last = (i == n and j == len(grp) - 1)
                mm(ps_o, ps_o[0:65, 0:NQ], vap, pb[0:nk, j * NQ:(j + 1) * NQ], [VA, pb], start=first, stop=last)
                first = False

    def gdn_chunk(P, c0, conv_in_buf, conv_in_ap, pz_rows_direct, S_ready):
        g = G
        Pn = P
        conv = g["Xf"]
        cq = g["qkraw"]
        cv = g["u"]
        dsts = [cq[:, 0:P], cq[:, 128:128 + P], cv[:, 0:P]]
        dbuf = [cq, cq, cv]
        for c in range(3):
            e = dve
            e.op("tensor_scalar", [conv_in_buf, cw], [dbuf[c]], out=dsts[c], in0=conv_in_ap[:, c, 0:P],
                 scalar1=cw[:, c * 4:c * 4 + 1], scalar2=None, op0=ALU.mult)
            for tap in range(1, 4):
                e.op("scalar_tensor_tensor", [conv_in_buf, cw, dbuf[c]], [dbuf[c]], out=dsts[c], in0=conv_in_ap[:, c, tap:tap + P],
                     scalar=cw[:, c * 4 + tap:c * 4 + tap + 1], in1=dsts[c], op0=ALU.mult, op1=ALU.add)
        act.op("activation", [cq], [cq], out=cq[:, 0:P], in_=cq[:, 0:P], func=AF.Silu)
        act.op("activation", [cq], [cq], out=cq[:, 128:128 + P], in_=cq[:, 128:128 + P], func=AF.Silu)
        act.op("activation", [cv], [cv], out=cv[:, 0:P], in_=cv[:, 0:P], func=AF.Silu)
        tr(ps_g, ps_g[0:P, 0:128], cq[:, 0:P], [cq, identf], identf[:])
        tr(ps_g, ps_g[0:P, 128:256], cq[:, 128:128 + P], [cq, identf], identf[:])
        tr(ps_g, ps_g[0:P, 256:384], cv[:, 0:P], [cv, identf], identf[:])
        act.op("activation", [ps_g], [g["Qsb"], g["bq"]], out=g["Qsb"][0:P, :], in_=ps_g[0:P, 0:128], func=AF.Square, accum_out=g["bq"][0:P, :])
        act.op("activation", [ps_g], [g["Ksb"], g["bk"]], out=g["Ksb"][0:P, :], in_=ps_g[0:P, 128:256], func=AF.Square, accum_out=g["bk"][0:P, :])
        for nm in ("bq", "bk"):
            dve.op("tensor_scalar", [g[nm]], [g[nm]], out=g[nm][0:P, :], in0=g[nm][0:P, :], scalar1=L2_EPS, scalar2=None, op0=ALU.add)
            act.op("activation", [g[nm]], [g[nm]], out=g[nm][0:P, :], in_=g[nm][0:P, :], func=AF.Sqrt)
            dve.op("reciprocal", [g[nm]], [g[nm]], out=g[nm][0:P, :], in_=g[nm][0:P, :])
        dve.op("tensor_scalar", [ps_g, g["bq"]], [g["Qb"]], out=g["Qb"][0:P, :], in0=ps_g[0:P, 0:128], scalar1=g["bq"][0:P, 0:1],
               scalar2=128.0 ** -0.5, op0=ALU.mult, op1=ALU.mult)
        dve.op("tensor_scalar", [ps_g, g["bk"]], [g["Ksb"]], out=g["Ksb"][0:P, :], in0=ps_g[0:P, 128:256], scalar1=g["bk"][0:P, 0:1],
               scalar2=None, op0=ALU.mult)
        act.op("activation", [g["Ksb"]], [g["Kb"]], out=g["Kb"][0:P, :], in_=g["Ksb"][0:P, :], func=AF.Copy)
        act.op("activation", [ps_g], [g["Vsb"]], out=g["Vsb"][0:P, :], in_=ps_g[0:P, 256:384], func=AF.Copy)
        tr(ps_t, ps_t[:, 6, 0:P], g["Qb"][0:P, :], [g["Qb"], identb], identb[0:P, 0:P])
        tr(ps_t, ps_t[:, 7, 0:P], g["Kb"][0:P, :], [g["Kb"], identb], identb[0:P, 0:P])
        act.op("activation", [ps_t], [g["QTn"]], out=g["QTn"][:, 0:P], in_=ps_t[:, 6, 0:P], func=AF.Copy)
        act.op("activation", [ps_t], [g["KTn"]], out=g["KTn"][:, 0:P], in_=ps_t[:, 7, 0:P], func=AF.Copy)
        if pz_rows_direct:
            bz_buf, bz = pZ, pZ[0:P, 128:130]
        else:
            mm(ps_h, ps_h[0:P, 384:386], identf[:, c0:c0 + P], pZ[:, 128:130], [identf, pZ])
            bz_buf, bz = ps_h, ps_h[0:P, 384:386]
        act.op("activation", [bz_buf], [g["beta"]], out=g["beta"][0:P, :], in_=bz[:, 0:1], func=AF.Sigmoid)
        act.op("activation", [bz_buf, dtbt], [g["sp_"]], out=g["sp_"][0:P, :], in_=bz[:, 1:2], func=AF.Exp, bias=dtbt[0:P, 0:1])
        act.op("activation", [g["sp_"]], [g["sp_"]], out=g["sp_"][0:P, :], in_=g["sp_"][0:P, :], func=AF.Ln, bias=1.0)
        dve.op("tensor_tensor", [g["sp_"], aneg], [g["gv"]], out=g["gv"][0:P, :], in0=g["sp_"][0:P, :], in1=aneg[0:P, :], op=ALU.mult)
        act.op("activation", [ones_f, g["gv"]], [g["grep"]], out=g["grep"][0:P, :], in_=ones_f[0:P, :], func=AF.Copy, scale=g["gv"][0:P, 0:1])
        mm(ps_h, ps_h[:, 0:P], g["grep"][0:P, :], tri[0:P, 0, 0:P], [g["grep"], tri])
        mm(ps_h, ps_h[0:P, 386:387], tri[0:P, 0, 0:P], g["gv"][0:P, 0:1], [tri, g["gv"]])
        dve.op("tensor_copy", [ps_h], [g["gcol"]], out=g["gcol"][0:P, :], in_=ps_h[0:P, 386:387])
        dve.op("tensor_scalar", [ps_h, g["gcol"]], [g["Dsb"]], out=g["Dsb"][0:P, 0:P], in0=ps_h[0:P, 0:P], scalar1=g["gcol"][0:P, 0:1],
               scalar2=0.0, op0=ALU.subtract, op1=ALU.min)
        act.op("activation", [g["Dsb"]], [g["E1"]], out=g["E1"][0:P, 0:P], in_=g["Dsb"][0:P, 0:P], func=AF.Exp)
        act.op("activation", [ps_h], [g["EG"]], out=g["EG"][:, 0:P], in_=ps_h[:, 0:P], func=AF.Exp)
        act.op("activation", [g["gcol"]], [g["egcol"]], out=g["egcol"][0:P, :], in_=g["gcol"][0:P, :], func=AF.Exp)
        pool.op("tensor_tensor", [g["E1"], tri], [g["E1i"]], out=g["E1i"][0:P, 0:P], in0=g["E1"][0:P, 0:P], in1=tri[0:P, 0, 0:P], op=ALU.mult)
        pool.op("tensor_tensor", [g["E1"], tri], [g["E1s"]], out=g["E1s"][0:P, 0:P], in0=g["E1"][0:P, 0:P], in1=tri[0:P, 1, 0:P], op=ALU.mult)
        mm(ps_g, ps_g[0:P, 384:384 + P], g["KTn"][:, 0:P], g["KTn"][:, 0:P], [g["KTn"]])
        dve.op("scalar_tensor_tensor", [ps_g, g["beta"], g["E1s"]], [g["NT"]], out=g["NT"][0:P, 0:P], in0=ps_g[0:P, 384:384 + P],
               scalar=g["beta"][0:P, 0:1], in1=g["E1s"][0:P, 0:P], op0=ALU.mult, op1=ALU.mult)
        mm(ps_g, ps_g[0:P, 0:P], g["KTn"][:, 0:P], g["QTn"][:, 0:P], [g["KTn"], g["QTn"]])
        dve.op("tensor_tensor", [ps_g, g["E1i"]], [g["attnT"]], out=g["attnT"][0:P, 0:P], in0=ps_g[0:P, 0:P], in1=g["E1i"][0:P, 0:P], op=ALU.mult)
        act.op("activation", [g["Vsb"]], [g["Xb"]], out=g["Xb"][0:P, 0:128], in_=g["Vsb"][0:P, :], func=AF.Copy)
        dve.op("tensor_scalar", [g["Ksb"], g["egcol"]], [g["Xb"]], out=g["Xb"][0:P, 128:256], in0=g["Ksb"][0:P, :],
               scalar1=g["egcol"][0:P, 0:1], scalar2=None, op0=ALU.mult)
        tr(ps_t, ps_t[0:P, 6, 0:P], g["NT"][0:P, 0:P], [g["NT"], identb], identb[0:P, 0:P])
        act.op("activation", [ps_t], [g["Zc"]], out=g["Zc"][0:P, 0:P], in_=ps_t[0:P, 6, 0:P], func=AF.Copy)
        W, WT = g["Y"][0], g["Y"][1]
        pool.op("tensor_tensor", [g["Zc"], bdm], [g["Zn"]], out=g["Zn"][0:P, 0:P], in0=g["Zc"][0:P, 0:P], in1=bdm[0:P, 0, 0:P], op=ALU.mult)
        pool.op("tensor_tensor", [g["NT"], bdm], [g["Yn"]], out=g["Yn"][0:P, 0:P], in0=g["NT"][0:P, 0:P], in1=bdm[0:P, 0, 0:P], op=ALU.mult)
        dve.op("tensor_tensor", [identb, g["Zn"]], [W], out=W[0:P, 0:P], in0=identb[0:P, 0:P], in1=g["Zn"][0:P, 0:P], op=ALU.subtract)
        dve.op("tensor_tensor", [identb, g["Yn"]], [WT], out=WT[0:P, 0:P], in0=identb[0:P, 0:P], in1=g["Yn"][0:P, 0:P], op=ALU.subtract)
        s_ = 2
        li = 1
        while s_ < P:
            C, CT, T1, T1p = g["Zn"], g["Yn"], g["Z"][0], g["Z"][1]
            pool.op("tensor_tensor", [g["Zc"], bdm], [C], out=C[0:P, 0:P], in0=g["Zc"][0:P, 0:P], in1=bdm[0:P, li, 0:P], op=ALU.mult)
            pool.op("tensor_tensor", [g["NT"], bdm], [CT], out=CT[0:P, 0:P], in0=g["NT"][0:P, 0:P], in1=bdm[0:P, li, 0:P], op=ALU.mult)
            mm(ps_g, ps_g[0:P, 128:128 + P], CT[0:P, 0:P], W[0:P, 0:P], [CT, W])
            mm(ps_g, ps_g[0:P, 256:256 + P], C[0:P, 0:P], WT[0:P, 0:P], [C, WT])
            act.op("activation", [ps_g], [T1], out=T1[0:P, 0:P], in_=ps_g[0:P, 128:128 + P], func=AF.Copy)
            act.op("activation", [ps_g], [T1p], out=T1p[0:P, 0:P], in_=ps_g[0:P, 256:256 + P], func=AF.Copy)
            mm(ps_h, ps_h[0:P, 0:P], WT[0:P, 0:P], T1[0:P, 0:P], [WT, T1])
            mm(ps_h, ps_h[0:P, 128:128 + P], W[0:P, 0:P], T1p[0:P, 0:P], [W, T1p])
            dve.op("tensor_tensor", [W, ps_h], [W], out=W[0:P, 0:P], in0=W[0:P, 0:P], in1=ps_h[0:P, 0:P], op=ALU.subtract)
            dve.op("tensor_tensor", [WT, ps_h], [WT], out=WT[0:P, 0:P], in0=WT[0:P, 0:P], in1=ps_h[0:P, 128:128 + P], op=ALU.subtract)
            s_ *= 2
            li += 1
        mm(ps_h, ps_h[0:P, 0:256], WT[0:P, 0:P], g["Xb"][0:P, :], [WT, g["Xb"]])
        dve.op("tensor_copy", [ps_h], [g["Xf"]], out=g["Xf"][0:P, :], in_=ps_h[0:P, 0:256])
        dve.op("tensor_scalar", [g["Xf"], g["beta"]], [g["u"]], out=g["u"][0:P, :], in0=g["Xf"][0:P, 0:128], scalar1=g["beta"][0:P, 0:1],
               scalar2=None, op0=ALU.mult)
        dve.op("tensor_scalar", [g["Xf"], g["beta"]], [g["wb"]], out=g["wb"][0:P, :], in0=g["Xf"][0:P, 128:256], scalar1=g["beta"][0:P, 0:1],
               scalar2=None, op0=ALU.mult)
        tr(ps_t, ps_t[:, 7, 0:P], g["wb"][0:P, :], [g["wb"], identb], identb[0:P, 0:P])
        act.op("activation", [ps_t], [g["wT"]], out=g["wT"][:, 0:P], in_=ps_t[:, 7, 0:P], func=AF.Copy)
        pool.op("tensor_tensor", [g["QTn"], g["EG"]], [g["QeT"]], out=g["QeT"][:, 0:P], in0=g["QTn"][:, 0:P], in1=g["EG"][:, 0:P], op=ALU.mult)
        dve.op("tensor_scalar", [g["Ksb"], g["E1"]], [g["Kd"]], out=g["Kd"][0:P, :], in0=g["Ksb"][0:P, :], scalar1=g["E1"][0:P, P - 1:P],
               scalar2=None, op0=ALU.mult)
        mm(ps_g, ps_g[0:P, 0:128], g["wT"][:, 0:P], g["Sb"][:, :], [g["wT"], g["Sb"]])
        dve.op("tensor_tensor", [g["u"], ps_g], [g["vnb"]], out=g["vnb"][0:P, :], in0=g["u"][0:P, :], in1=ps_g[0:P, 0:128], op=ALU.subtract)
        mm(ps_g, ps_g[0:P, 128:256], g["QeT"][:, 0:P], g["Sb"][:, :], [g["QeT"], g["Sb"]], start=True, stop=False)
        mm(ps_g, ps_g[0:P, 128:256], g["attnT"][0:P, 0:P], g["vnb"][0:P, :], [g["attnT"], g["vnb"]], start=False, stop=True)
        mm(ps_h, ps_h[:, 256:384], g["Kd"][0:P, :], g["vnb"][0:P, :], [g["Kd"], g["vnb"]])
        dve.op("scalar_tensor_tensor", [g["S"], g["EG"], ps_h], [g["S"]], out=g["S"][:, :], in0=g["S"][:, :], scalar=g["EG"][:, P - 1:P],
               in1=ps_h[:, 256:384], op0=ALU.mult, op1=ALU.add)
        act.op("activation", [g["S"]], [g["Sb"]], out=g["Sb"][:, :], in_=g["S"][:, :], func=AF.Copy)
        act.op("activation", [ps_g], [g["osb"], g["ssq"]], out=g["osb"][0:P, :], in_=ps_g[0:P, 128:256], func=AF.Square, accum_out=g["ssq"][0:P, :])
        dve.op("tensor_scalar", [g["ssq"]], [g["ssq"]], out=g["ssq"][0:P, :], in0=g["ssq"][0:P, :], scalar1=1.0 / 128.0, scalar2=RMS_EPS,
               op0=ALU.mult, op1=ALU.add)
        act.op("activation", [g["ssq"]], [g["ssq"]], out=g["ssq"][0:P, :], in_=g["ssq"][0:P, :], func=AF.Sqrt)
        dve.op("reciprocal", [g["ssq"]], [g["rn"]], out=g["rn"][0:P, :], in_=g["ssq"][0:P, :])
        dve.op("scalar_tensor_tensor", [ps_g, g["rn"], dnwt], [g["od"]], out=g["od"][0:P, :], in0=ps_g[0:P, 128:256], scalar=g["rn"][0:P, 0:1],
               in1=dnwt[0:P, :], op0=ALU.mult, op1=ALU.mult)

    def load_pass_weights(l, hg):
        for k in range(8):
            sp.dma(wstg[:], w_in[l, hg, k * 128:(k + 1) * 128, :], [w_in], [wstg], wstg)
            act.op("activation", [wstg], [wA], out=wA[:, k, :], in_=wstg[:, 0:512], func=AF.Copy)
            dve.op("tensor_copy", [wstg], [wZ], out=wZ[:, k, :], in_=wstg[:, 512:642])
            pool.op("tensor_copy", [wstg], [wD], out=wD[:, k, :], in_=wstg[:, 642:1026])
        sp.dma(cw[:], conv_w[l, hg, :, :], [conv_w], [cw], cw)
        sp.dma(aneg[:], alog[l, hg, :, :], [alog], [aneg], aneg)
        sp.dma(dtbt[:], dtb[l, hg, :, :], [dtb], [dtbt], dtbt)
        act.op("activation", [aneg], [aneg], out=aneg[:], in_=aneg[:], func=AF.Exp)
        dve.op("tensor_scalar", [aneg], [aneg], out=aneg[:], in0=aneg[:], scalar1=-1.0, scalar2=None, op0=ALU.mult)

    sp.dma(xst[:], xs[:, :], [xs], [xst], xst)

    for l in range(DEPTH):
        compute_mod(l, cTp, mod_p)
        compute_mod(l, cTs, mod_s)
        sp.dma(dnwt[:], dnw[l, :, :], [dnw], [dnwt], dnwt)
        sp.dma(lng[:], ln_g.t[l:l + 1, :].to_broadcast([128, D]), [ln_g], [lng], lng)
        sp.dma(lnb[:], ln_b.t[l:l + 1, :].to_broadcast([128, D]), [ln_b], [lnb], lnb)
        for k in range(8):
            sp.dma(wostg[:, 0:D], w_out[l, k * 128:(k + 1) * 128, :], [w_out], [wostg], wostg)
            act.op("activation", [wostg], [wo], out=wo[:, k, :], in_=wostg[:, 0:D], func=AF.Copy)
        xsrc = xp if l == 0 else x1

        for hg in range(4):
            load_pass_weights(l, hg)
            front(l, hg, xst, mod_s, coss, sins)
            sp.dma(nks[l, :, hg * 128:(hg + 1) * 128], qkr[:, 128:256], [qkr], [nks], qkr)
            sp.dma(nvs[l, :, hg * 128:(hg + 1) * 128], pA[:, 256:384], [pA], [nvs], pA)
            qT_and_gate(0, False)
            act.op("activation", [pA], [zsil], out=zsil[:, 0, :], in_=pA[:, 384:512], func=AF.Silu)
            act.op("activation", [pZ], [zdsil], out=zdsil[:], in_=pZ[:, 0:128], func=AF.Silu)
            dve.op("tensor_copy", [ps_d], [xhist], out=xhist[:, :, 3:131], in_=ps_d[:, 0:384].rearrange("p (c t) -> p c t", c=3))
            for h in range(2):
                tr(ps_t, ps_t[0:64, 4 + h, :], kvb[:, h * 64:(h + 1) * 64], [kvb, identb], identb[:])
            for h in range(2):
                act.op("activation", [ps_t], [knT[h]], out=knT[h][:], in_=ps_t[0:64, 4 + h, :], func=AF.Copy)
            knew = knT
            for j in range(NBS):
                r0 = 4 * j
                sp.dma(ptb[:], ptab.t[j:j + 1, :].to_broadcast([128, NPG]), [ptab], [ptb], ptb)
                dve.op("tensor_scalar", [ptb, iota4], [idxb], out=idxb[:], in0=ptb[:], scalar1=512.0, scalar2=iota4[:, 0:1], op0=ALU.mult, op1=ALU.add)
                dve.op("tensor_scalar", [idxb], [idx], out=idx[:], in0=idxb[:], scalar1=float(l * cfg.NPOOL * 512 + hg), scalar2=None, op0=ALU.add)
                for pg in range(NPG):
                    bk_, bv_ = pgk[pg % 2], pgv[pg % 2]
                    pool.idma(bk_[:], cache_k[:, :], idx[:, pg:pg + 1], [cache_k, idx], [bk_], bk_)
                    pool.idma(bv_[:], cache_v[:, :], idx[:, pg:pg + 1], [cache_v, idx], [bv_], bv_)
                    act.op("activation", [bk_], [pgb], out=pgb[:, 0:128], in_=bk_[:], func=AF.Copy)
                    dve.op("tensor_copy", [bv_], [pgb], out=pgb[:, 128:256], in_=bv_[:])
                    for h in range(2):
                        tr(ps_t, ps_t[0:64, 4 + h, :], pgb[:, h * 64:(h + 1) * 64], [pgb, identb], identb[:])
                    for h in range(2):
                        act.op("activation", [ps_t], [KT[h]], out=KT[h][0:64, pg * 128:(pg + 1) * 128], in_=ps_t[0:64, 4 + h, :], func=AF.Copy)
                    pool.op("tensor_copy", [pgb], [VA], out=VA[:, pg, :, 0:64], in_=pgb[:, 128:256].rearrange("p (h f) -> p h f", h=2))
                    kmean_update(pg, pgb, lambda h: pgb[:, h * 64:(h + 1) * 64])
                for h in range(2):
                    dve.op("tensor_copy", [knew[h]], [KT[h]], out=KT[h][0:64, NK:NK + 4], in_=knew[h][0:64, r0:r0 + 4])
                mm(ps_b, ps_b[0:4, 0:128], identb[:, r0:r0 + 4], kvb[:, 128:256], [identb, kvb])
                dve.op("tensor_copy", [ps_b], [VA], out=VA[0:4, NKT, :, 0:64], in_=ps_b[0:4, 0:128].rearrange("p (h f) -> p h f", h=2))
                nb = NPG // 2
                for h in range(2):
                    mm(ps_b, ps_b[:, 256 + h * 32:256 + h * 32 + nb], qTt[h][:], kmT[h][:, 0:nb], [qTt[h], kmT[h]])
                dve.op("memset", [], [qa], qa[:, :, 64:96], 0.0)
                if nb > 3:
                    dve.op("memset", [], [gate_sb], gate_sb[:], -1e30)
                    dve.op("tensor_copy", [ps_b], [gate_sb], out=gate_sb[:, :, 0:nb],
                           in_=ps_b[:, 256:320].rearrange("p (h f) -> p h f", h=2)[:, :, 0:nb])
                    for h in range(2):
                        dve.op("max", [gate_sb], [top8], out=top8[:, h, :], in_=gate_sb[:, h, :])
                        dve.op("tensor_scalar", [gate_sb, top8], [qa], out=qa[:, h, 64:64 + nb], in0=gate_sb[:, h, 0:nb],
                               scalar1=top8[:, h, 2:3], scalar2=NEG, op0=ALU.is_lt, op1=ALU.mult)
                for h in range(2):
                    qaug_T(h, QT[h], QT[h][:, 0:128])
                    ktl = [(KT[h][:, kt * 128:(kt + 1) * 128], 128, VA[:, kt, h, :], None) for kt in range(NPG)]
                    ktl.append((KT[h][:, NK:NK + 4], 4, VA[0:4, NKT, h, :], stair[0:4, 0, 0:4]))
                    attention(h, QT[h], QT[h][:, r0:r0 + 4], 4, ktl, None)
                    dve.op("tensor_copy", [ps_o], [OTs], out=OTs[:, 0:4], in_=ps_o[0:65, 0:4])
                    tr(ps_b, ps_b[0:4, 128:193], OTs[0:65, 0:4], [OTs, identf], identf[0:65, 0:65])
                    dve.op("reciprocal", [ps_b], [rden], out=rden[0:4, :], in_=ps_b[0:4, 192:193])
                    dve.op("tensor_scalar", [ps_b, rden], [oas], out=oas[0:4, h * 64:(h + 1) * 64], in0=ps_b[0:4, 128:192], scalar1=rden[0:4, 0:1],
                           scalar2=None, op0=ALU.mult)
                sp.dma(xhs[:, :, 0:3], sconv[l, j, hg, :, :].rearrange("p (c t) -> p c t", c=3), [sconv], [xhs], xhs)
                dve.op("tensor_copy", [xhist], [xhs], out=xhs[:, :, 3:7], in_=xhist[:, :, 3 + r0:3 + r0 + 4])
                sp.dma(G["S"][:], ssm0[l, j, hg, :, :], [ssm0], [G["S"]], G["S"])
                act.op("activation", [G["S"]], [G["Sb"]], out=G["Sb"][:], in_=G["S"][:], func=AF.Copy)
                gdn_chunk(4, r0, xhs, xhs[:, :, 0:7], False, None)
                sp.dma(ssms[l, j, hg, :, :], G["S"][:], [G["S"]], [ssms], G["S"])
                sp.dma(convs[l, j, hg, :, :].rearrange("p (c t) -> p c t", c=3), xhs[:, :, 4:7], [xhs], [convs], xhs)
                act.op("activation", [G["od"]], [oas], out=oas[0:4, 128:256], in_=G["od"][0:4, :], func=AF.Copy)
                mm(ps_a, ps_a[:, 0:256], esel[0:4, j, :], oas[0:4, :], [esel, oas], start=(j == 0), stop=(j == NBS - 1))
            dve.op("tensor_tensor", [ps_a, zsil], [mixs], out=mixs[:, hg * 256:hg * 256 + 128], in0=ps_a[:, 0:128], in1=zsil[:, 0, :], op=ALU.mult)
            dve.op("tensor_tensor", [ps_a, zdsil], [mixs], out=mixs[:, hg * 256 + 128:hg * 256 + 256], in0=ps_a[:, 128:256], in1=zdsil[:], op=ALU.mult)

            dve.op("memset", [], [G["S"]], G["S"][:], 0.0)
            dve.op("memset", [], [G["Sb"]], G["Sb"][:], 0.0)
            dve.op("memset", [], [xhist], xhist[:, :, 0:3], 0.0)
            dve.op("memset", [], [gate_sb], gate_sb[:], -1e30)
            for i in range(NT):
                c = i % 4
                sp.dma(xt[:], xsrc[i * 128:(i + 1) * 128, :], [xsrc], [xt], xt)
                sp.dma(cost[:], c_cosp[i * 128:(i + 1) * 128, :], [c_cosp], [cost], cost)
                sp.dma(sint[:], c_sinp[i * 128:(i + 1) * 128, :], [c_sinp], [sint], sint)
                front(l, hg, xt, mod_p, cost, sint)
                sp.dma(nkp[l, i * 128:(i + 1) * 128, hg * 128:(hg + 1) * 128], qkr[:, 128:256], [qkr], [nkp], qkr)
                sp.dma(nvp[l, i * 128:(i + 1) * 128, hg * 128:(hg + 1) * 128], pA[:, 256:384], [pA], [nvp], pA)
                act.op("activation", [pA], [zsil], out=zsil[:, c, :], in_=pA[:, 384:512], func=AF.Silu)
                act.op("activation", [pZ], [zdsil], out=zdsil[:], in_=pZ[:, 0:128], func=AF.Silu)
                dve.op("tensor_copy", [ps_d], [xhist], out=xhist[:, :, 3:131], in_=ps_d[:, 0:384].rearrange("p (c t) -> p c t", c=3))
                store_kv_tile(i)
                qT_and_gate(i // 2, True)
                kmean_update(i, kvb, lambda h: kvb[:, h * 64:(h + 1) * 64])
                for h in range(2):
                    qaug_T(h, QT[h], QT[h][:, c * 128:(c + 1) * 128])
                gdn_chunk(128, 0, xhist, xhist[:, :, :], True, None)
                dve.op("tensor_copy", [xhist], [xhs], out=xhs[:, :, 0:3], in_=xhist[:, :, 128:131])
                dve.op("tensor_copy", [xhs], [xhist], out=xhist[:, :, 0:3], in_=xhs[:, :, 0:3])
                dve.op("tensor_tensor", [G["od"], zdsil], [mix], out=mix[:, 128:256], in0=G["od"][:, :], in1=zdsil[:], op=ALU.mult)
                sp.dma(mixp[i * 128:(i + 1) * 128, hg * 256 + 128:hg * 256 + 256], mix[:, 128:256], [mix], [mixp], mix)
                if c == 3 or i == NT - 1:
                    g0 = (i // 4) * 4
                    ng = i - g0 + 1
                    NQ = ng * 128
                    for h in range(2):
                        ktl = []
                        for kt in range(0, i + 1):
                            mask = stair[:, kt - g0, 0:NQ] if kt >= g0 else None
                            ktl.append((KT[h][:, kt * 128:(kt + 1) * 128], 128, VA[:, kt, h, :], mask))
                        attention(h, QT[h], QT[h][:, 0:NQ], NQ, ktl, None)
                        dve.op("tensor_copy", [ps_o], [OTs], out=OTs[:, 0:NQ], in_=ps_o[0:65, 0:NQ])
                        for cc in range(ng):
                            tr(ps_b, ps_b[:, 0:65], OTs[0:65, cc * 128:(cc + 1) * 128], [OTs, identf], identf[0:65, 0:65])
                            dve.op("reciprocal", [ps_b], [rden], out=rden[:], in_=ps_b[:, 64:65])
                            dve.op("scalar_tensor_tensor", [ps_b, rden, zsil], [mix], out=mix[:, h * 64:(h + 1) * 64], in0=ps_b[:, 0:64],
                                   scalar=rden[:, 0:1], in1=zsil[:, cc, h * 64:(h + 1) * 64], op0=ALU.mult, op1=ALU.mult)
                            ti = g0 + cc
                            sp.dma(mixp[ti * 128:(ti + 1) * 128, hg * 256 + h * 64:hg * 256 + (h + 1) * 64], mix[:, h * 64:(h + 1) * 64],
                                   [mix], [mixp], mix)
            sp.dma(ssmp[l, hg, :, :], G["S"][:], [G["S"]], [ssmp], G["S"])
            sp.dma(convp[l, hg, :, :].rearrange("p (c t) -> p c t", c=3), xhist[:, :, 0:3], [xhist], [convp], xhist)

        if cfg.DEBUG:
            sp.dma(dbg_mixs[l, :, :], mixs[:], [mixs], [dbg_mixs], mixs)
            out_bufs.append(dbg_mixs)

        def phase2(mix_buf, mix_ap, xtile, mod, out_buf, out_ap, keep=None):
            for k in range(8):
                tr(ps_t, ps_t[:, k, :], mix_ap[:, k * 128:(k + 1) * 128], [mix_buf, identb], identb[:])
            act.op("activation", [ps_t], [mT], out=mT[:], in_=ps_t[:], func=AF.Copy)
            for half, pb_ in ((0, ps_a), (1, ps_b)):
                for k in range(8):
                    mm(pb_, pb_[:, :], mT[:, k, :], wo[:, k, half * 512:(half + 1) * 512], [mT, wo], start=(k == 0), stop=(k == 7))
                dve.op("tensor_tensor", [pb_, mod], [rt], out=rt[:, half * 512:(half + 1) * 512], in0=pb_[:, :],
                       in1=mod[:, 2 * D + half * 512:2 * D + (half + 1) * 512], op=ALU.mult)
            dve.op("scalar_tensor_tensor", [xtile, rt], [rt], out=rt[:], in0=xtile[:], scalar=cfg.ALPHA, in1=rt[:], op0=ALU.mult, op1=ALU.add)
            for half in range(2):
                dve.op("bn_stats", [rt], [stats], out=stats[:, half, :], in_=rt[:, half * 512:(half + 1) * 512])
            dve.op("bn_aggr", [stats], [mv], out=mv[:], in_=stats[:].rearrange("p a b -> p (a b)"))
            dve.op("tensor_scalar", [mv], [rstd], out=rstd[:], in0=mv[:, 1:2], scalar1=LN_EPS, scalar2=None, op0=ALU.add)
            act.op("activation", [rstd], [rstd], out=rstd[:], in_=rstd[:], func=AF.Sqrt)
            dve.op("reciprocal", [rstd], [rstd], out=rstd[:], in_=rstd[:])
            dve.op("tensor_scalar", [rt, mv, rstd], [rt], out=rt[:], in0=rt[:], scalar1=mv[:, 0:1], scalar2=rstd[:, 0:1], op0=ALU.subtract, op1=ALU.mult)
            pool.op("tensor_tensor", [rt, lng], [rt], out=rt[:], in0=rt[:], in1=lng[:], op=ALU.mult)
            tgt = keep if keep is not None else rt
            dve.op("tensor_tensor", [rt, lnb], [tgt], out=tgt[:], in0=rt[:], in1=lnb[:], op=ALU.add)
            if out_buf is not None:
                sp.dma(out_ap, tgt[:], [tgt], [out_buf], tgt)

        for i in range(NT):
            sp.dma(xt[:], xsrc[i * 128:(i + 1) * 128, :], [xsrc], [xt], xt)
            sp.dma(mixl[:], mixp[i * 128:(i + 1) * 128, :], [mixp], [mixl], mixl)
            if l == DEPTH - 1:
                phase2(mixl, mixl, xt, mod_p, yp, yp[i * 128:(i + 1) * 128, :])
            else:
                phase2(mixl, mixl, xt, mod_p, x1, x1[i * 128:(i + 1) * 128, :])
        if l == DEPTH - 1:
            phase2(mixs, mixs, xst, mod_s, ys, ys[:, :])
        else:
            phase2(mixs, mixs, xst, mod_s, None, None, keep=xst)

    for e in (sp, pool, act, dve, pe):
        e.wait_all(out_bufs)
    return kb


_CACHE = {}


def _consts(cfg):
    SEQ, NK = cfg.SEQ, cfg.NK
    bf = ml_dtypes.bfloat16
    c = {}
    c["c_identf"] = np.eye(128, dtype=np.float32)
    c["c_identb"] = np.eye(128, dtype=np.float32).astype(bf)
    t = np.arange(128)
    tri = np.zeros((128, 2, 128), np.float32)
    tri[:, 0, :] = (t[:, None] <= t[None, :])
    tri[:, 1, :] = (t[:, None] < t[None, :])
    c["c_tri"] = tri
    st = np.zeros((128, 4, 512), np.float32)
    for r in range(4):
        for cc in range(4):
            blk = st[:, r, cc * 128:(cc + 1) * 128]
            if cc < r:
                blk[:] = NEG
            elif cc == r:
                blk[:] = np.where(t[:, None] <= t[None, :], 0.0, NEG)
    c["c_stair"] = st.astype(bf)
    e = np.zeros((32, NK + 128), np.float32)
    pos = np.arange(NK)
    for j in range(32):
        e[j, :NK] = (pos // 256 == j)
    c["c_eblk"] = e.astype(bf)
    half = 32
    inv = (10000.0 ** (-np.arange(half, dtype=np.float32) / half)).astype(np.float32)

    def tab(pos):
        ang = pos.astype(np.float32)[:, None] * inv[None, :]
        return np.tile(np.cos(ang).astype(np.float32), (1, 4)), np.tile(np.sin(ang).astype(np.float32), (1, 4))
    c["c_cosp"], c["c_sinp"] = tab(np.arange(SEQ))
    ps = cfg.PAST + (np.arange(128) % 4)
    cs, sn = tab(ps)
    c["c_coss"], c["c_sins"] = cs, sn
    c["c_iota4"] = (4 * np.arange(128, dtype=np.float32)).reshape(128, 1)
    es = np.zeros((4, cfg.NBS, 128), np.float32)
    for j in range(cfg.NBS):
        for tt in range(4):
            es[tt, j, 4 * j + tt] = 1.0
    c["c_esel"] = es.astype(bf)
    bd = np.zeros((128, 7, 128), np.float32)
    prev = None
    for li, sz in enumerate([2, 4, 8, 16, 32, 64, 128]):
        m_ = (t[:, None] // sz == t[None, :] // sz).astype(np.float32)
        bd[:, li, :] = m_ if prev is None else m_ - prev
        prev = m_
    c["c_bdm"] = bd.astype(bf)
    return c


def _run(cfg, x_prompt, x_sample, c_prompt, c_sample, cache_k, cache_v, page_table,
         state_ssm, state_conv, w_ada, b_ada, w_in, conv_w, a_log, dt_bias,
         dn_norm_w, w_out, ln_g, ln_b):
    f = np.float32
    key = (cfg.SEQ, cfg.PAST, cfg.DEPTH, cfg.DECB)
    if key not in _CACHE:
        _CACHE[key] = build(cfg)
    kb = _CACHE[key]
    DEPTH = cfg.DEPTH
    A, Dq = 512, 1536
    w_in = np.asarray(w_in, f)
    win_r = np.zeros((DEPTH, 4, D, 1026), f)
    for hg in range(4):
        a0 = hg * 128
        cols = []
        for g in range(4):
            cols += list(range(g * 512 + a0, g * 512 + a0 + 128))
        cols += list(range(2048 + 1536 + a0, 2048 + 1536 + a0 + 128))
        cols += [2048 + 1536 + 512 + hg, 2048 + 1536 + 512 + 4 + hg]
        for g in range(3):
            cols += list(range(2048 + g * 512 + a0, 2048 + g * 512 + a0 + 128))
        win_r[:, hg] = w_in[:, :, cols]
    conv_w = np.asarray(conv_w, f)
    cw_r = np.zeros((DEPTH, 4, 128, 12), f)
    for hg in range(4):
        for g in range(3):
            blk = conv_w[:, :, g * 512 + hg * 128: g * 512 + (hg + 1) * 128]
            cw_r[:, hg, :, g * 4:(g + 1) * 4] = np.transpose(blk, (0, 2, 1))
    alog_r = np.broadcast_to(np.asarray(a_log, f)[:, :, None, None], (DEPTH, 4, 128, 1)).copy()
    dtb_r = np.broadcast_to(np.asarray(dt_bias, f)[:, :, None, None], (DEPTH, 4, 128, 1)).copy()
    dnw_r = np.broadcast_to(np.asarray(dn_norm_w, f)[:, None, :], (DEPTH, 128, 128)).copy()
    rows = []
    for hg in range(4):
        rows += list(range(hg * 128, (hg + 1) * 128)) + list(range(512 + hg * 128, 512 + (hg + 1) * 128))
    wout_r = np.ascontiguousarray(np.asarray(w_out, f)[:, rows, :])
    ck = np.ascontiguousarray(np.asarray(cache_k, f)).reshape(-1, 128)
    cv = np.ascontiguousarray(np.asarray(cache_v, f)).reshape(-1, 128)
    consts = _consts(cfg)
    state_conv = np.asarray(state_conv, f)
    state_ssm = np.asarray(state_ssm, f)
    in_maps = []
    NC_, NBS = cfg.NCORES, cfg.NBS
    cpb = NC_ // cfg.BATCH
    for c in range(NC_):
        b = c // cpb
        m = dict(consts)
        m["xp"] = np.ascontiguousarray(np.asarray(x_prompt, f)[b])
        xs_ = np.zeros((128, D), f)
        cTs = np.zeros((128, 8, 128), f)
        sc = np.zeros((DEPTH, NBS, 4, 128, 9), f)
        for j in range(NBS):
            sb_ = NBS * c + j
            xs_[4 * j:4 * j + 4] = np.asarray(x_sample, f)[sb_]
            cTs[:, :, 4 * j:4 * j + 4] = np.asarray(c_sample, f)[sb_].reshape(8, 128).T[:, :, None]
            for hg in range(4):
                for g in range(3):
                    blk = state_conv[:, sb_, :, g * 512 + hg * 128:g * 512 + (hg + 1) * 128]
                    sc[:, j, hg, :, g * 3:(g + 1) * 3] = np.transpose(blk, (0, 2, 1))
        m["xs"] = xs_
        m["cTs"] = cTs
        m["cTp"] = np.broadcast_to(np.asarray(c_prompt, f)[b].reshape(8, 128).T[:, :, None], (128, 8, 128)).copy()
        m["w_ada"] = np.asarray(w_ada, f); m["b_ada"] = np.asarray(b_ada, f)
        m["w_in"] = win_r; m["conv_w"] = cw_r; m["alog"] = alog_r; m["dtb"] = dtb_r; m["dnw"] = dnw_r
        m["w_out"] = wout_r; m["ln_g"] = np.asarray(ln_g, f); m["ln_b"] = np.asarray(ln_b, f)
        m["cache_k"] = ck; m["cache_v"] = cv
        m["ptab"] = np.ascontiguousarray(np.asarray(page_table, np.int32)[NBS * c:NBS * c + NBS])
        m["ssm0"] = np.ascontiguousarray(state_ssm[:, NBS * c:NBS * c + NBS])
        m["sconv"] = sc
        in_maps.append(m)
    res = run_bass_kernel_spmd(kb.nc, in_maps, core_ids=list(range(NC_)))
    R = res.results
    B, SEQ, DECB = cfg.BATCH, cfg.SEQ, cfg.DECB
    y_prompt = np.stack([R[cpb * b]["yp"] for b in range(B)]).astype(f)
    y_sample = np.zeros((DECB, 4, D), f)
    nk_p = np.zeros((DEPTH, B, SEQ, 8, 64), f); nv_p = np.zeros_like(nk_p)
    ssm_p = np.zeros((DEPTH, B, 4, 128, 128), f); conv_p = np.zeros((DEPTH, B, 3, 1536), f)
    nk_s = np.zeros((DEPTH, DECB, 4, 8, 64), f); nv_s = np.zeros_like(nk_s)
    ssm_s = np.zeros((DEPTH, DECB, 4, 128, 128), f); conv_s = np.zeros((DEPTH, DECB, 3, 1536), f)
    for b in range(B):
        r = R[cpb * b]
        nk_p[:, b] = r["nkp"].reshape(DEPTH, SEQ, 8, 64)
        nv_p[:, b] = r["nvp"].reshape(DEPTH, SEQ, 8, 64)
        ssm_p[:, b] = r["ssmp"]
        cp = r["convp"].reshape(DEPTH, 4, 128, 3, 3)
        for hg in range(4):
            for g in range(3):
                conv_p[:, b, :, g * 512 + hg * 128:g * 512 + (hg + 1) * 128] = np.transpose(cp[:, hg, :, g, :], (0, 2, 1))
    for c in range(NC_):
        r = R[c]
        for j in range(NBS):
            sb_ = NBS * c + j
            y_sample[sb_] = r["ys"][4 * j:4 * j + 4]
            nk_s[:, sb_] = r["nks"][:, 4 * j:4 * j + 4].reshape(DEPTH, 4, 8, 64)
            nv_s[:, sb_] = r["nvs"][:, 4 * j:4 * j + 4].reshape(DEPTH, 4, 8, 64)
            ssm_s[:, sb_] = r["ssms"][:, j]
            cs = r["convs"][:, j].reshape(DEPTH, 4, 128, 3, 3)
            for hg in range(4):
                for g in range(3):
                    conv_s[:, sb_, :, g * 512 + hg * 128:g * 512 + (hg + 1) * 128] = np.transpose(cs[:, hg, :, g, :], (0, 2, 1))
    return (y_prompt, y_sample, nk_p, nv_p, ssm_p, conv_p, nk_s, nv_s, ssm_s, conv_s)


def kernel(**inputs):
    cfg = Cfg()
    return _run(cfg, **inputs)
```

```python
import math
import numpy as np
import ml_dtypes
import concourse.bass as bass
import concourse.mybir as mybir
from concourse.bass_utils import run_bass_kernel_spmd

F32 = mybir.dt.float32
BF16 = mybir.dt.bfloat16
I32 = mybir.dt.int32
AF = mybir.ActivationFunctionType
ALU = mybir.AluOpType

D = 1024
NEG = -30000.0
L2_EPS = 1e-6
RMS_EPS = 1e-6
LN_EPS = 1e-5


class Cfg:
    def __init__(self, seq=8192, past=8192, depth=2, dec_batch=32, batch=2, ncores=2):
        self.SEQ = seq
        self.PAST = past
        self.DEPTH = depth
        self.DECB = dec_batch
        self.BATCH = batch
        self.NT = seq // 128
        self.NPAGES = past // 128
        self.NPOOL = (dec_batch * self.NPAGES * 5) // 4
        self.NK = max(seq, past)
        self.NKT = self.NK // 128
        self.ALPHA = (2 * depth) ** 0.25
        self.DEBUG = False
        self.NCORES = ncores
        self.NBS = dec_batch // ncores


class Buf:
    __slots__ = ("t", "name", "w", "r", "dsem", "dcnt")

    def __init__(self, t, name):
        self.t = t
        self.name = name
        self.w = None
        self.r = {}
        self.dsem = None
        self.dcnt = 0

    def __getitem__(self, k):
        return self.t[k]


class Eng:
    def __init__(self, kb, eng, name, selfsync=True):
        self.kb = kb
        self.eng = eng
        self.name = name
        self.selfsync = selfsync
        self.sem = kb.nc.alloc_semaphore(f"p_{name}_0")
        self.nsem = 1
        self.cnt = 0
        self.waited = {}
        self.n_ins = 0

    def _wait(self, tok):
        if tok is None:
            return
        sem, val = tok
        if (not self.selfsync) and sem is self.sem:
            return
        k = id(sem)
        if self.waited.get(k, 0) >= val:
            return
        self.eng.wait_ge(sem, val)
        self.waited[k] = val
        self.n_ins += 1
        self.n_wait = getattr(self, "n_wait", 0) + 1

    def _pre(self, reads, writes):
        for b in reads:
            self._wait(b.w)
        for b in writes:
            self._wait(b.w)
            for t in b.r.values():
                self._wait(t)

    def _post(self, tok, reads, writes):
        sem, val = tok
        for b in reads:
            b.r[id(sem)] = tok
        for b in writes:
            b.w = tok
            b.r = {}

    def op(self, name, reads, writes, *a, **kw):
        if self.kb.rec is not None:
            self.kb.rec.append((self, "op", name, reads, writes, a, kw))
            return None
        self._pre(reads, writes)
        ins = getattr(self.eng, name)(*a, **kw)
        if self.cnt >= 30000:
            self.kb.all_sems.append(self.sem)
            self.sem = self.kb.nc.alloc_semaphore(f"p_{self.name}_{self.nsem}")
            self.nsem += 1
            self.cnt = 0
        self.cnt += 1
        ins.then_inc(self.sem, 1)
        self.n_ins += 1
        tok = (self.sem, self.cnt)
        self._post(tok, reads, writes)
        return tok

    def _dsem(self, sb):
        if sb.dsem is None:
            sb.dsem = self.kb.nc.alloc_semaphore(f"d_{sb.name}")
        return sb.dsem

    def dma(self, out, in_, reads, writes, sb, **kw):
        if self.kb.rec is not None:
            self.kb.rec.append((self, "dma", None, reads, writes, (out, in_, sb), kw))
            return None
        self._pre(reads, writes)
        sem = self._dsem(sb)
        ins = self.eng.dma_start(out=out, in_=in_, **kw)
        sb.dcnt += 1
        ins.then_inc(sem, 16)
        self.n_ins += 1
        tok = (sem, 16 * sb.dcnt)
        self._post(tok, reads, writes)
        return tok

    def idma(self, out, in_, idx_ap, reads, writes, sb):
        self._pre(reads, writes)
        sem = self._dsem(sb)
        ins = self.eng.indirect_dma_start(
            out=out, out_offset=None, in_=in_,
            in_offset=bass.IndirectOffsetOnAxis(ap=idx_ap, axis=0))
        sb.dcnt += 1
        ins.then_inc(sem, 16)
        self.n_ins += 1
        tok = (sem, 16 * sb.dcnt)
        self._post(tok, reads, writes)
        return tok

    def wait_all(self, bufs):
        for b in bufs:
            self._wait(b.w)
            for t in b.r.values():
                self._wait(t)


class KB:
    def __init__(self):
        self.nc = bass.Bass("TRN2", target_bir_lowering=False)
        nc = self.nc
        self.pe = Eng(self, nc.tensor, "pe", selfsync=False)
        self.act = Eng(self, nc.scalar, "act")
        self.dve = Eng(self, nc.vector, "dve")
        self.pool = Eng(self, nc.gpsimd, "pool")
        self.sp = Eng(self, nc.sync, "sp")
        self.ext_in = {}
        self.ext_out = {}
        self.rec = None
        self.all_sems = []

    def record(self, fn):
        assert self.rec is None
        self.rec = []
        fn()
        r, self.rec = self.rec, None
        return r

    def emit(self, item):
        eng, kind, name, reads, writes, a, kw = item
        if kind == "op":
            eng.op(name, reads, writes, *a, **kw)
        else:
            out, in_, sb = a
            eng.dma(out, in_, reads, writes, sb, **kw)

    def merge_emit(self, A, B):
        import os
        if not os.environ.get("K_MERGE"):
            for it in A:
                self.emit(it)
            for it in B:
                self.emit(it)
            return
        na, nb = len(A), len(B)
        ia = ib = 0
        while ia < na or ib < nb:
            if ib >= nb or (ia < na and ia * nb <= ib * na):
                self.emit(A[ia]); ia += 1
            else:
                self.emit(B[ib]); ib += 1

    def sb(self, name, shape, dt=F32):
        return Buf(self.nc.alloc_sbuf_tensor(name, list(shape), dt), name)

    def ps(self, name, shape, dt=F32):
        return Buf(self.nc.alloc_psum_tensor(name, list(shape), dt), name)

    def dram(self, name, shape, dt=F32, kind="Internal"):
        b = Buf(self.nc.dram_tensor(name, list(shape), dt, kind=kind).ap(), name)
        if kind == "ExternalInput":
            self.ext_in[name] = b
        elif kind == "ExternalOutput":
            self.ext_out[name] = b
        return b

    def alias(self, buf, name):
        return Buf(buf.t, name)


def build(cfg):
    kb = KB()
    pe, act, dve, pool, sp = kb.pe, kb.act, kb.dve, kb.pool, kb.sp
    SEQ, NT, NK, NKT, NPG, DEPTH = cfg.SEQ, cfg.NT, cfg.NK, cfg.NKT, cfg.NPAGES, cfg.DEPTH
    NPOOLROWS = DEPTH * cfg.NPOOL * 128 * 4

    def IN(name, shape, dt=F32):
        return kb.dram(name, shape, dt, kind="ExternalInput")

    def OUT(name, shape, dt=F32):
        return kb.dram(name, shape, dt, kind="ExternalOutput")

    xp = IN("xp", [SEQ, D]); xs = IN("xs", [128, D])
    cTp = IN("cTp", [128, 8, 128]); cTs = IN("cTs", [128, 8, 128])
    w_ada = IN("w_ada", [DEPTH, D, 3 * D]); b_ada = IN("b_ada", [DEPTH, 3 * D])
    w_in = IN("w_in", [DEPTH, 4, D, 1026])
    conv_w = IN("conv_w", [DEPTH, 4, 128, 12])
    alog = IN("alog", [DEPTH, 4, 128, 1]); dtb = IN("dtb", [DEPTH, 4, 128, 1])
    dnw = IN("dnw", [DEPTH, 128, 128])
    w_out = IN("w_out", [DEPTH, D, D])
    ln_g = IN("ln_g", [DEPTH, D]); ln_b = IN("ln_b", [DEPTH, D])
    cache_k = IN("cache_k", [NPOOLROWS, 128]); cache_v = IN("cache_v", [NPOOLROWS, 128])
    NBS = cfg.NBS
    ptab = IN("ptab", [NBS, NPG], I32)
    ssm0 = IN("ssm0", [DEPTH, NBS, 4, 128, 128]); sconv = IN("sconv", [DEPTH, NBS, 4, 128, 9])
    c_identf = IN("c_identf", [128, 128]); c_identb = IN("c_identb", [128, 128], BF16)
    c_tri = IN("c_tri", [128, 2, 128])
    c_stair = IN("c_stair", [128, 4, 512], BF16)
    c_eblk = IN("c_eblk", [32, NK + 128], BF16)
    c_cosp = IN("c_cosp", [SEQ, 128]); c_sinp = IN("c_sinp", [SEQ, 128])
    c_coss = IN("c_coss", [128, 128]); c_sins = IN("c_sins", [128, 128])
    c_iota4 = IN("c_iota4", [128, 1])
    c_esel = IN("c_esel", [4, NBS, 128], BF16)
    c_bdm = IN("c_bdm", [128, 7, 128], BF16)

    yp = OUT("yp", [SEQ, D]); ys = OUT("ys", [128, D])
    nkp = OUT("nkp", [DEPTH, SEQ, 512]); nvp = OUT("nvp", [DEPTH, SEQ, 512])
    ssmp = OUT("ssmp", [DEPTH, 4, 128, 128]); convp = OUT("convp", [DEPTH, 4, 128, 9])
    nks = OUT("nks", [DEPTH, 128, 512]); nvs = OUT("nvs", [DEPTH, 128, 512])
    ssms = OUT("ssms", [DEPTH, NBS, 4, 128, 128]); convs = OUT("convs", [DEPTH, NBS, 4, 128, 9])
    out_bufs = [yp, ys, nkp, nvp, ssmp, convp, nks, nvs, ssms, convs]

    mixp = kb.dram("mixp", [SEQ, D], BF16, kind="ExternalOutput" if cfg.DEBUG else "Internal")
    dbg_mixs = kb.dram("dbg_mixs", [DEPTH, 128, D], BF16, kind="ExternalOutput") if cfg.DEBUG else None
    x1 = kb.dram("x1", [SEQ, D], F32)

    identf = kb.sb("identf", [128, 128]); identb = kb.sb("identb", [128, 128], BF16)
    tri = kb.sb("tri", [128, 2, 128])
    stair = kb.sb("stair", [128, 4, 512], BF16)
    iota4 = kb.sb("iota4", [128, 1]); esel = kb.sb("esel", [4, NBS, 128], BF16)
    ones_f = kb.sb("ones_f", [128, 128]); ones_b = kb.sb("ones_b", [128, 1], BF16)
    coss = kb.sb("coss", [128, 128]); sins = kb.sb("sins", [128, 128])
    for dst, src in ((identf, c_identf), (identb, c_identb), (iota4, c_iota4), (coss, c_coss), (sins, c_sins)):
        sp.dma(dst[:], src[:, :], [src], [dst], dst)
    sp.dma(tri[:], c_tri[:, :, :], [c_tri], [tri], tri)
    sp.dma(stair[:], c_stair[:, :, :], [c_stair], [stair], stair)
    sp.dma(esel[:], c_esel[:, :, :], [c_esel], [esel], esel)
    bdm = kb.sb("bdm", [128, 7, 128], BF16)
    sp.dma(bdm[:], c_bdm[:, :, :], [c_bdm], [bdm], bdm)
    dve.op("memset", [], [ones_f], ones_f[:], 1.0)
    dve.op("memset", [], [ones_b], ones_b[:], 1.0)

    KT = [kb.sb(f"KT{h}", [96, NK + 128], BF16) for h in range(2)]
    VA = kb.sb("VA", [128, NKT + 1, 2, 65], BF16)
    for h in range(2):
        sp.dma(KT[h][64:96, :], c_eblk[:, :], [c_eblk], [KT[h]], KT[h])
    dve.op("memset", [], [VA], VA[:], 1.0)
    kmT = [kb.sb(f"kmT{h}", [64, 32], BF16) for h in range(2)]
    ksum = kb.sb("ksum", [64, 2])
    QT = [kb.sb(f"QT{h}", [96, 512], BF16) for h in range(2)]
    mod_p = kb.sb("mod_p", [128, 3 * D]); mod_s = kb.sb("mod_s", [128, 3 * D])
    wA = kb.sb("wA", [128, 8, 512], BF16); wZ = kb.sb("wZ", [128, 8, 130], BF16); wD = kb.sb("wD", [128, 8, 384], BF16)
    wstg = kb.sb("wstg", [128, 1026])
    wo = kb.sb("wo", [128, 8, D], BF16); wostg = wstg
    cw = kb.sb("cw", [128, 12]); aneg = kb.sb("aneg", [128, 1]); dtbt = kb.sb("dtbt", [128, 1]); dnwt = kb.sb("dnwt", [128, 128])
    lng = kb.sb("lng", [128, D]); lnb = kb.sb("lnb", [128, D])
    xt = kb.sb("xt", [128, D]); xst = kb.sb("xst", [128, D])
    hb = kb.sb("hb", [128, D], BF16); hT = kb.sb("hT", [128, 8, 128], BF16)
    pA = kb.sb("pA", [128, 512]); pZs = [kb.sb(f"pZ{i}", [128, 130]) for i in range(2)]
    xhists = [kb.sb(f"xhist{i}", [128, 3, 131]) for i in range(2)]; xhs = kb.sb("xhs", [128, 3, 8])
    cost = kb.sb("cost", [128, 128]); sint = kb.sb("sint", [128, 128])
    rtmp = kb.sb("rtmp", [128, 4, 128]); qkr = kb.sb("qkr", [128, 256])
    kvb = kb.sb("kvb", [128, 256], BF16)
    qa = kb.sb("qa", [128, 2, 96], BF16)
    qTt = [kb.sb(f"qTt{h}", [64, 128], BF16) for h in range(2)]
    gate_sb = kb.sb("gate_sb", [128, 2, 32]); top8 = kb.sb("top8", [128, 2, 8])
    zsil = kb.sb("zsil", [128, 4, 128]); zdsils = [kb.sb(f"zdsil{i}", [128, 128]) for i in range(2)]
    PTb = [kb.sb(f"PTb{i}", [128, 512], BF16) for i in range(2)]
    OTs = kb.sb("OTs", [65, 512]); rden = kb.sb("rden", [128, 1])
    mix = kb.sb("mix", [128, 128], BF16); mixd = kb.sb("mixd", [128, 128], BF16); mixs = kb.sb("mixs", [128, D], BF16)
    oas = kb.sb("oas", [4, 256], BF16)
    mixl = kb.sb("mixl", [128, D], BF16); mT = kb.sb("mT", [128, 8, 128], BF16)
    rt = kb.sb("rt", [128, D]); stats = kb.sb("stats", [128, 2, 6]); mv = kb.sb("mv", [128, 2]); rstd = kb.sb("rstd", [128, 1])
    cst = kb.sb("cst", [128, 8, 128]); scb = kb.sb("scb", [128, 8, 128], BF16)
    wastg = kb.sb("wastg", [128, 8, 128]); wab = kb.sb("wab", [128, 8, 128], BF16); bab = kb.sb("bab", [128, 128])
    ptb = kb.sb("ptb", [128, NPG], I32); idxb = kb.sb("idxb", [128, NPG], I32); idx = kb.sb("idx", [128, NPG], I32)
    pgk = [kb.sb(f"pgk{i}", [128, 128]) for i in range(2)]; pgv = [kb.sb(f"pgv{i}", [128, 128]) for i in range(2)]
    pgbs = [kb.sb(f"pgb{i}", [128, 256], BF16) for i in range(2)]
    knT = [kb.sb(f"knT{h}", [64, 128], BF16) for h in range(2)]
    G = {}
    for nm in ["grep", "Dsb", "E1", "E1i", "E1s", "EG", "Xf", "Ksb", "Qsb", "Vsb", "u", "osb", "od", "S", "qkraw"]:
        G[nm] = kb.sb("g_" + nm, [128, 256] if nm in ("Xf", "qkraw") else [128, 128])
    for nm in ["NT", "attnT", "Zc", "Yn", "Zn", "Xb", "wb", "wT", "QTn", "KTn", "QeT", "Kd", "vnb", "Sb", "Kb", "Qb"]:
        G[nm] = kb.sb("g_" + nm, [128, 256] if nm == "Xb" else [128, 128], BF16)
    G["Y"] = [kb.sb(f"g_Y{i}", [128, 128], BF16) for i in range(2)]
    G["Z"] = [kb.sb(f"g_Z{i}", [128, 128], BF16) for i in range(2)]
    for nm in ["gcol", "egcol", "beta", "gv", "ssq", "rn", "kds", "sp_", "bq", "bk"]:
        G[nm] = kb.sb("g_" + nm, [128, 1])

    ps_t = kb.ps("ps_t", [128, 8, 128], BF16)
    ps_tg = kb.alias(ps_t, "ps_tg")
    ps_a = kb.ps("ps_a", [128, 512])
    ps_b = kb.ps("ps_b", [128, 512])
    ps_ss = [kb.ps(f"ps_s{i}", [128, 512]) for i in range(2)]
    ps_o = kb.ps("ps_o", [128, 512])
    ps_g = kb.ps("ps_g", [128, 512])
    ps_h = kb.ps("ps_h", [128, 512])

    def mm(out_buf, out_ap, lhsT, rhs, reads, start=True, stop=True):
        pe.op("matmul", reads, [out_buf], out_ap, lhsT=lhsT, rhs=rhs, start=start, stop=stop)

    def tr(out_buf, out_ap, in_ap, reads, ident_ap):
        pe.op("transpose", reads, [out_buf], out_ap, in_ap, ident_ap)

    def compute_mod(l, cT_src, mod):
        sp.dma(cst[:], cT_src[:, :, :], [cT_src], [cst], cst)
        act.op("activation", [cst], [scb], out=scb[:], in_=cst[:], func=AF.Silu)
        for cc in range(24):
            c0 = cc * 128
            sp.dma(wastg[:], w_ada[l, :, c0:c0 + 128].rearrange("(k p) c -> p k c", p=128), [w_ada], [wastg], wastg)
            sp.dma(bab[:], b_ada.t[l:l + 1, c0:c0 + 128].to_broadcast([128, 128]), [b_ada], [bab], bab)
            act.op("activation", [wastg], [wab], out=wab[:], in_=wastg[:], func=AF.Copy)
            for k in range(8):
                mm(ps_a, ps_a[:, 0:128], scb[:, k, :], wab[:, k, :], [scb, wab], start=(k == 0), stop=(k == 7))
            dve.op("tensor_tensor", [ps_a, bab], [mod], out=mod[:, c0:c0 + 128], in0=ps_a[:, 0:128], in1=bab[:], op=ALU.add)
        dve.op("tensor_scalar", [mod], [mod], out=mod[:, D:3 * D], in0=mod[:, D:3 * D], scalar1=1.0, scalar2=None, op0=ALU.add)

    def front(l, hg, xtile, mod, cos_t, sin_t, pZ):
        dve.op("tensor_tensor", [xtile, mod], [rt], out=rt[:], in0=xtile[:], in1=mod[:, D:2 * D], op=ALU.mult)
        dve.op("tensor_tensor", [rt, mod], [hb], out=hb[:], in0=rt[:], in1=mod[:, 0:D], op=ALU.add)
        for rnd in range(2):
            for k in range(4):
                tr(ps_t, ps_t[:, k, :], hb[:, (rnd * 4 + k) * 128:(rnd * 4 + k + 1) * 128], [hb, identb], identb[:])
            act.op("activation", [ps_t], [hT], out=hT[:, rnd * 4:rnd * 4 + 4, :], in_=ps_t[:, 0:4, :], func=AF.Copy)
        for k in range(8):
            mm(ps_a, ps_a[:, :], hT[:, k, :], wA[:, k, :], [hT, wA], start=(k == 0), stop=(k == 7))
        act.op("activation", [ps_a], [pA], out=pA[:], in_=ps_a[:], func=AF.Copy)
        for k in range(8):
            mm(ps_b, ps_b[:, 0:130], hT[:, k, :], wZ[:, k, :], [hT, wZ], start=(k == 0), stop=(k == 7))
        dve.op("tensor_copy", [ps_b], [pZ], out=pZ[:], in_=ps_b[:, 0:130])
        for c in range(3):
            for k in range(8):
                mm(ps_a, ps_a[:, c * 128:(c + 1) * 128], wD[:, k, c * 128:(c + 1) * 128], hT[:, k, :], [hT, wD],
                   start=(k == 0), stop=(k == 7))
        v4 = pA[:, 0:256].rearrange("p (g two f) -> p g two f", g=4, two=2)
        A_ = v4[:, :, 0, :]
        B_ = v4[:, :, 1, :]
        c4 = cos_t[:].rearrange("p (g f) -> p g f", g=4)
        s4 = sin_t[:].rearrange("p (g f) -> p g f", g=4)
        o4 = qkr[:].rearrange("p (g two f) -> p g two f", g=4, two=2)
        r4 = rtmp[:]
        dve.op("tensor_tensor", [pA, cos_t], [rtmp], out=r4[:, :, 0:32], in0=A_, in1=c4, op=ALU.mult)
        pool.op("tensor_tensor", [pA, sin_t], [rtmp], out=r4[:, :, 32:64], in0=B_, in1=s4, op=ALU.mult)
        dve.op("tensor_tensor", [rtmp], [qkr], out=o4[:, :, 0, :], in0=r4[:, :, 0:32], in1=r4[:, :, 32:64], op=ALU.subtract)
        pool.op("tensor_tensor", [pA, sin_t], [rtmp], out=r4[:, :, 64:96], in0=A_, in1=s4, op=ALU.mult)
        dve.op("tensor_tensor", [pA, cos_t], [rtmp], out=r4[:, :, 96:128], in0=B_, in1=c4, op=ALU.mult)
        dve.op("tensor_tensor", [rtmp], [qkr], out=o4[:, :, 1, :], in0=r4[:, :, 64:96], in1=r4[:, :, 96:128], op=ALU.add)
        act.op("activation", [qkr], [kvb], out=kvb[:, 0:128], in_=qkr[:, 128:256], func=AF.Copy)
        act.op("activation", [pA], [kvb], out=kvb[:, 128:256], in_=pA[:, 256:384], func=AF.Copy)
        act.op("activation", [qkr], [qa], out=qa[:, :, 0:64], in_=qkr[:, 0:128].rearrange("p (h f) -> p h f", h=2),
               func=AF.Copy, scale=0.125)

    def qT_and_gate(nblk_valid, topk):
        for h in range(2):
            tr(ps_t, ps_t[0:64, h, :], qa[:, h, 0:64], [qa, identb], identb[:])
        for h in range(2):
            act.op("activation", [ps_t], [qTt[h]], out=qTt[h][:], in_=ps_t[0:64, h, :], func=AF.Copy)
        dve.op("memset", [], [qa], qa[:, :, 64:96], 0.0)
        if topk and nblk_valid > 3:
            nb = nblk_valid
            for h in range(2):
                mm(ps_b, ps_b[:, 256 + h * 32:256 + h * 32 + nb], qTt[h][:], kmT[h][:, 0:nb], [qTt[h], kmT[h]])
            dve.op("tensor_copy", [ps_b], [gate_sb], out=gate_sb[:, :, 0:nb],
                   in_=ps_b[:, 256:320].rearrange("p (h f) -> p h f", h=2)[:, :, 0:nb])
            for h in range(2):
                dve.op("max", [gate_sb], [top8], out=top8[:, h, :], in_=gate_sb[:, h, :])
                dve.op("tensor_scalar", [gate_sb, top8], [qa], out=qa[:, h, 64:64 + nb], in0=gate_sb[:, h, 0:nb],
                       scalar1=top8[:, h, 2:3], scalar2=NEG, op0=ALU.is_lt, op1=ALU.mult)

    def qaug_T(h, dst_buf, dst_ap):
        tr(ps_t, ps_t[0:96, 2 + h, :], qa[:, h, :], [qa, identb], identb[:])
        act.op("activation", [ps_t], [dst_buf], out=dst_ap, in_=ps_t[0:96, 2 + h, :], func=AF.Copy)

    def store_kv_tile(kt, rows=128):
        for h in range(2):
            tr(ps_t, ps_t[0:64, h, :], kvb[:, h * 64:(h + 1) * 64], [kvb, identb], identb[:])
        for h in range(2):
            act.op("activation", [ps_t], [KT[h]], out=KT[h][0:64, kt * 128:(kt + 1) * 128], in_=ps_t[0:64, h, :], func=AF.Copy)
        dve.op("tensor_copy", [kvb], [VA], out=VA[:, kt, :, 0:64], in_=kvb[:, 128:256].rearrange("p (h f) -> p h f", h=2))

    def kmean_update(kt, src_kb, src_ap_fn):
        for h in range(2):
            mm(ps_b, ps_b[0:64, 384 + h:385 + h], src_ap_fn(h), ones_b[:, 0:1], [src_kb, ones_b])
        if kt % 2 == 0:
            dve.op("tensor_copy", [ps_b], [ksum], out=ksum[:], in_=ps_b[0:64, 384:386])
        else:
            j = kt // 2
            for h in range(2):
                dve.op("tensor_scalar", [ps_b, ksum], [kmT[h]], out=kmT[h][:, j:j + 1], in0=ps_b[0:64, 384 + h:385 + h],
                       scalar1=ksum[:, h:h + 1], scalar2=1.0 / 256.0, op0=ALU.add, op1=ALU.mult)

    def attention(h, q_buf, q_ap, NQ, ktiles, o_ap):
        per = max(1, 512 // NQ)
        n = len(ktiles)
        gi = 0
        first = True
        i = 0
        while i < n:
            grp = []
            while i < n and len(grp) < per and (not grp or (ktiles[i][1] == 128 and grp[0][1] == 128)):
                grp.append(ktiles[i]); i += 1
            nk = grp[0][1]
            pb = PTb[gi % 2]; ps_s = ps_ss[gi % 2]; gi += 1
            for j, (kap, nk_, vap, mask) in enumerate(grp):
                mm(ps_s, ps_s[0:nk, j * NQ:(j + 1) * NQ], kap, q_ap, [KT[h], q_buf], start=True, stop=(mask is None))
                if mask is not None:
                    mm(ps_s, ps_s[0:nk, j * NQ:(j + 1) * NQ], identb[0:nk, 0:nk], mask, [identb, stair], start=False, stop=True)
            act.op("activation", [ps_s], [pb], out=pb[0:nk, 0:len(grp) * NQ], in_=ps_s[0:nk, 0:len(grp) * NQ], func=AF.Exp)
            for j, (kap, nk_, vap, mask) in enumerate(grp):
                last = (i == n and j == len(grp) - 1)
                mm(ps_o, ps_o[0:65, 0:NQ], vap, pb[0:nk, j * NQ:(j + 1) * NQ], [VA, pb], start=first, stop=last)
                first = False

    def gdn_chunk(P, c0, conv_in_buf, conv_in_ap, pz_rows_direct, pZ):
        g = G
        Pn = P
        conv = g["Xf"]
        cq = g["qkraw"]
        cv = g["u"]
        dsts = [cq[:, 0:P], cq[:, 128:128 + P], cv[:, 0:P]]
        dbuf = [cq, cq, cv]
        for c in range(3):
            e = dve
            e.op("tensor_scalar", [conv_in_buf, cw], [dbuf[c]], out=dsts[c], in0=conv_in_ap[:, c, 0:P],
                 scalar1=cw[:, c * 4:c * 4 + 1], scalar2=None, op0=ALU.mult)
            for tap in range(1, 4):
                e.op("scalar_tensor_tensor", [conv_in_buf, cw, dbuf[c]], [dbuf[c]], out=dsts[c], in0=conv_in_ap[:, c, tap:tap + P],
                     scalar=cw[:, c * 4 + tap:c * 4 + tap + 1], in1=dsts[c], op0=ALU.mult, op1=ALU.add)
        act.op("activation", [cq], [cq], out=cq[:, 0:P], in_=cq[:, 0:P], func=AF.Silu)
        act.op("activation", [cq], [cq], out=cq[:, 128:128 + P], in_=cq[:, 128:128 + P], func=AF.Silu)
        act.op("activation", [cv], [cv], out=cv[:, 0:P], in_=cv[:, 0:P], func=AF.Silu)
        tr(ps_g, ps_g[0:P, 0:128], cq[:, 0:P], [cq, identf], identf[:])
        tr(ps_g, ps_g[0:P, 128:256], cq[:, 128:128 + P], [cq, identf], identf[:])
        tr(ps_g, ps_g[0:P, 256:384], cv[:, 0:P], [cv, identf], identf[:])
        act.op("activation", [ps_g], [g["Qsb"], g["bq"]], out=g["Qsb"][0:P, :], in_=ps_g[0:P, 0:128], func=AF.Square, accum_out=g["bq"][0:P, :])
        act.op("activation", [ps_g], [g["Ksb"], g["bk"]], out=g["Ksb"][0:P, :], in_=ps_g[0:P, 128:256], func=AF.Square, accum_out=g["bk"][0:P, :])
        for nm in ("bq", "bk"):
            dve.op("tensor_scalar", [g[nm]], [g[nm]], out=g[nm][0:P, :], in0=g[nm][0:P, :], scalar1=L2_EPS, scalar2=None, op0=ALU.add)
            act.op("activation", [g[nm]], [g[nm]], out=g[nm][0:P, :], in_=g[nm][0:P, :], func=AF.Sqrt)
            dve.op("reciprocal", [g[nm]], [g[nm]], out=g[nm][0:P, :], in_=g[nm][0:P, :])
        dve.op("tensor_scalar", [ps_g, g["bq"]], [g["Qb"]], out=g["Qb"][0:P, :], in0=ps_g[0:P, 0:128], scalar1=g["bq"][0:P, 0:1],
               scalar2=128.0 ** -0.5, op0=ALU.mult, op1=ALU.mult)
        dve.op("tensor_scalar", [ps_g, g["bk"]], [g["Ksb"]], out=g["Ksb"][0:P, :], in0=ps_g[0:P, 128:256], scalar1=g["bk"][0:P, 0:1],
               scalar2=None, op0=ALU.mult)
        act.op("activation", [g["Ksb"]], [g["Kb"]], out=g["Kb"][0:P, :], in_=g["Ksb"][0:P, :], func=AF.Copy)
        act.op("activation", [ps_g], [g["Vsb"]], out=g["Vsb"][0:P, :], in_=ps_g[0:P, 256:384], func=AF.Copy)
        tr(ps_tg, ps_t[:, 6, 0:P], g["Qb"][0:P, :], [g["Qb"], identb], identb[0:P, 0:P])
        tr(ps_tg, ps_t[:, 7, 0:P], g["Kb"][0:P, :], [g["Kb"], identb], identb[0:P, 0:P])
        act.op("activation", [ps_tg], [g["QTn"]], out=g["QTn"][:, 0:P], in_=ps_t[:, 6, 0:P], func=AF.Copy)
        act.op("activation", [ps_tg], [g["KTn"]], out=g["KTn"][:, 0:P], in_=ps_t[:, 7, 0:P], func=AF.Copy)
        if pz_rows_direct:
            bz_buf, bz = pZ, pZ[0:P, 128:130]
        else:
            mm(ps_h, ps_h[0:P, 384:386], identf[:, c0:c0 + P], pZ[:, 128:130], [identf, pZ])
            bz_buf, bz = ps_h, ps_h[0:P, 384:386]
        act.op("activation", [bz_buf], [g["beta"]], out=g["beta"][0:P, :], in_=bz[:, 0:1], func=AF.Sigmoid)
        act.op("activation", [bz_buf, dtbt], [g["sp_"]], out=g["sp_"][0:P, :], in_=bz[:, 1:2], func=AF.Exp, bias=dtbt[0:P, 0:1])
        act.op("activation", [g["sp_"]], [g["sp_"]], out=g["sp_"][0:P, :], in_=g["sp_"][0:P, :], func=AF.Ln, bias=1.0)
        dve.op("tensor_tensor", [g["sp_"], aneg], [g["gv"]], out=g["gv"][0:P, :], in0=g["sp_"][0:P, :], in1=aneg[0:P, :], op=ALU.mult)
        act.op("activation", [ones_f, g["gv"]], [g["grep"]], out=g["grep"][0:P, :], in_=ones_f[0:P, :], func=AF.Copy, scale=g["gv"][0:P, 0:1])
        mm(ps_h, ps_h[:, 0:P], g["grep"][0:P, :], tri[0:P, 0, 0:P], [g["grep"], tri])
        mm(ps_h, ps_h[0:P, 386:387], tri[0:P, 0, 0:P], g["gv"][0:P, 0:1], [tri, g["gv"]])
        dve.op("tensor_copy", [ps_h], [g["gcol"]], out=g["gcol"][0:P, :], in_=ps_h[0:P, 386:387])
        dve.op("tensor_scalar", [ps_h, g["gcol"]], [g["Dsb"]], out=g["Dsb"][0:P, 0:P], in0=ps_h[0:P, 0:P], scalar1=g["gcol"][0:P, 0:1],
               scalar2=0.0, op0=ALU.subtract, op1=ALU.min)
        act.op("activation", [g["Dsb"]], [g["E1"]], out=g["E1"][0:P, 0:P], in_=g["Dsb"][0:P, 0:P], func=AF.Exp)
        act.op("activation", [ps_h], [g["EG"]], out=g["EG"][:, 0:P], in_=ps_h[:, 0:P], func=AF.Exp)
        act.op("activation", [g["gcol"]], [g["egcol"]], out=g["egcol"][0:P, :], in_=g["gcol"][0:P, :], func=AF.Exp)
        pool.op("tensor_tensor", [g["E1"], tri], [g["E1i"]], out=g["E1i"][0:P, 0:P], in0=g["E1"][0:P, 0:P], in1=tri[0:P, 0, 0:P], op=ALU.mult)
        pool.op("tensor_tensor", [g["E1"], tri], [g["E1s"]], out=g["E1s"][0:P, 0:P], in0=g["E1"][0:P, 0:P], in1=tri[0:P, 1, 0:P], op=ALU.mult)
        mm(ps_g, ps_g[0:P, 384:384 + P], g["KTn"][:, 0:P], g["KTn"][:, 0:P], [g["KTn"]])
        dve.op("scalar_tensor_tensor", [ps_g, g["beta"], g["E1s"]], [g["NT"]], out=g["NT"][0:P, 0:P], in0=ps_g[0:P, 384:384 + P],
               scalar=g["beta"][0:P, 0:1], in1=g["E1s"][0:P, 0:P], op0=ALU.mult, op1=ALU.mult)
        mm(ps_g, ps_g[0:P, 0:P], g["KTn"][:, 0:P], g["QTn"][:, 0:P], [g["KTn"], g["QTn"]])
        dve.op("tensor_tensor", [ps_g, g["E1i"]], [g["attnT"]], out=g["attnT"][0:P, 0:P], in0=ps_g[0:P, 0:P], in1=g["E1i"][0:P, 0:P], op=ALU.mult)
        act.op("activation", [g["Vsb"]], [g["Xb"]], out=g["Xb"][0:P, 0:128], in_=g["Vsb"][0:P, :], func=AF.Copy)
        dve.op("tensor_scalar", [g["Ksb"], g["egcol"]], [g["Xb"]], out=g["Xb"][0:P, 128:256], in0=g["Ksb"][0:P, :],
               scalar1=g["egcol"][0:P, 0:1], scalar2=None, op0=ALU.mult)
        tr(ps_tg, ps_t[0:P, 6, 0:P], g["NT"][0:P, 0:P], [g["NT"], identb], identb[0:P, 0:P])
        act.op("activation", [ps_tg], [g["Zc"]], out=g["Zc"][0:P, 0:P], in_=ps_t[0:P, 6, 0:P], func=AF.Copy)
        W, WT = g["Y"][0], g["Y"][1]
        pool.op("tensor_tensor", [g["Zc"], bdm], [g["Zn"]], out=g["Zn"][0:P, 0:P], in0=g["Zc"][0:P, 0:P], in1=bdm[0:P, 0, 0:P], op=ALU.mult)
        pool.op("tensor_tensor", [g["NT"], bdm], [g["Yn"]], out=g["Yn"][0:P, 0:P], in0=g["NT"][0:P, 0:P], in1=bdm[0:P, 0, 0:P], op=ALU.mult)
        dve.op("tensor_tensor", [identb, g["Zn"]], [W], out=W[0:P, 0:P], in0=identb[0:P, 0:P], in1=g["Zn"][0:P, 0:P], op=ALU.subtract)
        dve.op("tensor_tensor", [identb, g["Yn"]], [WT], out=WT[0:P, 0:P], in0=identb[0:P, 0:P], in1=g["Yn"][0:P, 0:P], op=ALU.subtract)
        s_ = 2
        li = 1
        while s_ < P:
            C, CT, T1, T1p = g["Zn"], g["Yn"], g["Z"][0], g["Z"][1]
            pool.op("tensor_tensor", [g["Zc"], bdm], [C], out=C[0:P, 0:P], in0=g["Zc"][0:P, 0:P], in1=bdm[0:P, li, 0:P], op=ALU.mult)
            pool.op("tensor_tensor", [g["NT"], bdm], [CT], out=CT[0:P, 0:P], in0=g["NT"][0:P, 0:P], in1=bdm[0:P, li, 0:P], op=ALU.mult)
            mm(ps_g, ps_g[0:P, 128:128 + P], CT[0:P, 0:P], W[0:P, 0:P], [CT, W])
            mm(ps_g, ps_g[0:P, 256:256 + P], C[0:P, 0:P], WT[0:P, 0:P], [C, WT])
            act.op("activation", [ps_g], [T1], out=T1[0:P, 0:P], in_=ps_g[0:P, 128:128 + P], func=AF.Copy)
            act.op("activation", [ps_g], [T1p], out=T1p[0:P, 0:P], in_=ps_g[0:P, 256:256 + P], func=AF.Copy)
            mm(ps_h, ps_h[0:P, 0:P], WT[0:P, 0:P], T1[0:P, 0:P], [WT, T1])
            mm(ps_h, ps_h[0:P, 128:128 + P], W[0:P, 0:P], T1p[0:P, 0:P], [W, T1p])
            dve.op("tensor_tensor", [W, ps_h], [W], out=W[0:P, 0:P], in0=W[0:P, 0:P], in1=ps_h[0:P, 0:P], op=ALU.subtract)
            dve.op("tensor_tensor", [WT, ps_h], [WT], out=WT[0:P, 0:P], in0=WT[0:P, 0:P], in1=ps_h[0:P, 128:128 + P], op=ALU.subtract)
            s_ *= 2
            li += 1
        mm(ps_h, ps_h[0:P, 0:256], WT[0:P, 0:P], g["Xb"][0:P, :], [WT, g["Xb"]])
        dve.op("tensor_copy", [ps_h], [g["Xf"]], out=g["Xf"][0:P, :], in_=ps_h[0:P, 0:256])
        dve.op("tensor_scalar", [g["Xf"], g["beta"]], [g["u"]], out=g["u"][0:P, :], in0=g["Xf"][0:P, 0:128], scalar1=g["beta"][0:P, 0:1],
               scalar2=None, op0=ALU.mult)
        dve.op("tensor_scalar", [g["Xf"], g["beta"]], [g["wb"]], out=g["wb"][0:P, :], in0=g["Xf"][0:P, 128:256], scalar1=g["beta"][0:P, 0:1],
               scalar2=None, op0=ALU.mult)
        tr(ps_tg, ps_t[:, 7, 0:P], g["wb"][0:P, :], [g["wb"], identb], identb[0:P, 0:P])
        act.op("activation", [ps_tg], [g["wT"]], out=g["wT"][:, 0:P], in_=ps_t[:, 7, 0:P], func=AF.Copy)
        pool.op("tensor_tensor", [g["QTn"], g["EG"]], [g["QeT"]], out=g["QeT"][:, 0:P], in0=g["QTn"][:, 0:P], in1=g["EG"][:, 0:P], op=ALU.mult)
        dve.op("tensor_scalar", [g["Ksb"], g["E1"]], [g["Kd"]], out=g["Kd"][0:P, :], in0=g["Ksb"][0:P, :], scalar1=g["E1"][0:P, P - 1:P],
               scalar2=None, op0=ALU.mult)
        mm(ps_g, ps_g[0:P, 0:128], g["wT"][:, 0:P], g["Sb"][:, :], [g["wT"], g["Sb"]])
        dve.op("tensor_tensor", [g["u"], ps_g], [g["vnb"]], out=g["vnb"][0:P, :], in0=g["u"][0:P, :], in1=ps_g[0:P, 0:128], op=ALU.subtract)
        mm(ps_g, ps_g[0:P, 128:256], g["QeT"][:, 0:P], g["Sb"][:, :], [g["QeT"], g["Sb"]], start=True, stop=False)
        mm(ps_g, ps_g[0:P, 128:256], g["attnT"][0:P, 0:P], g["vnb"][0:P, :], [g["attnT"], g["vnb"]], start=False, stop=True)
        mm(ps_h, ps_h[:, 256:384], g["Kd"][0:P, :], g["vnb"][0:P, :], [g["Kd"], g["vnb"]])
        dve.op("scalar_tensor_tensor", [g["S"], g["EG"], ps_h], [g["S"]], out=g["S"][:, :], in0=g["S"][:, :], scalar=g["EG"][:, P - 1:P],
               in1=ps_h[:, 256:384], op0=ALU.mult, op1=ALU.add)
        act.op("activation", [g["S"]], [g["Sb"]], out=g["Sb"][:, :], in_=g["S"][:, :], func=AF.Copy)
        act.op("activation", [ps_g], [g["osb"], g["ssq"]], out=g["osb"][0:P, :], in_=ps_g[0:P, 128:256], func=AF.Square, accum_out=g["ssq"][0:P, :])
        dve.op("tensor_scalar", [g["ssq"]], [g["ssq"]], out=g["ssq"][0:P, :], in0=g["ssq"][0:P, :], scalar1=1.0 / 128.0, scalar2=RMS_EPS,
               op0=ALU.mult, op1=ALU.add)
        act.op("activation", [g["ssq"]], [g["ssq"]], out=g["ssq"][0:P, :], in_=g["ssq"][0:P, :], func=AF.Sqrt)
        dve.op("reciprocal", [g["ssq"]], [g["rn"]], out=g["rn"][0:P, :], in_=g["ssq"][0:P, :])
        dve.op("scalar_tensor_tensor", [ps_g, g["rn"], dnwt], [g["od"]], out=g["od"][0:P, :], in0=ps_g[0:P, 128:256], scalar=g["rn"][0:P, 0:1],
               in1=dnwt[0:P, :], op0=ALU.mult, op1=ALU.mult)

    def load_pass_weights(l, hg):
        for k in range(8):
            sp.dma(wstg[:], w_in[l, hg, k * 128:(k + 1) * 128, :], [w_in], [wstg], wstg)
            act.op("activation", [wstg], [wA], out=wA[:, k, :], in_=wstg[:, 0:512], func=AF.Copy)
            dve.op("tensor_copy", [wstg], [wZ], out=wZ[:, k, :], in_=wstg[:, 512:642])
            pool.op("tensor_copy", [wstg], [wD], out=wD[:, k, :], in_=wstg[:, 642:1026])
        sp.dma(cw[:], conv_w[l, hg, :, :], [conv_w], [cw], cw)
        sp.dma(aneg[:], alog[l, hg, :, :], [alog], [aneg], aneg)
        sp.dma(dtbt[:], dtb[l, hg, :, :], [dtb], [dtbt], dtbt)
        act.op("activation", [aneg], [aneg], out=aneg[:], in_=aneg[:], func=AF.Exp)
        dve.op("tensor_scalar", [aneg], [aneg], out=aneg[:], in0=aneg[:], scalar1=-1.0, scalar2=None, op0=ALU.mult)

    sp.dma(xst[:], xs[:, :], [xs], [xst], xst)

    for l in range(DEPTH):
        compute_mod(l, cTp, mod_p)
        compute_mod(l, cTs, mod_s)
        sp.dma(dnwt[:], dnw[l, :, :], [dnw], [dnwt], dnwt)
        sp.dma(lng[:], ln_g.t[l:l + 1, :].to_broadcast([128, D]), [ln_g], [lng], lng)
        sp.dma(lnb[:], ln_b.t[l:l + 1, :].to_broadcast([128, D]), [ln_b], [lnb], lnb)
        for k in range(8):
            sp.dma(wostg[:, 0:D], w_out[l, k * 128:(k + 1) * 128, :], [w_out], [wostg], wostg)
            act.op("activation", [wostg], [wo], out=wo[:, k, :], in_=wostg[:, 0:D], func=AF.Copy)
        xsrc = xp if l == 0 else x1

        for hg in range(4):
            load_pass_weights(l, hg)
            pZ = pZs[0]; xhist = xhists[0]; zdsil = zdsils[0]
            front(l, hg, xst, mod_s, coss, sins, pZ)
            sp.dma(nks[l, :, hg * 128:(hg + 1) * 128], qkr[:, 128:256], [qkr], [nks], qkr)
            sp.dma(nvs[l, :, hg * 128:(hg + 1) * 128], pA[:, 256:384], [pA], [nvs], pA)
            qT_and_gate(0, False)
            act.op("activation", [pA], [zsil], out=zsil[:, 0, :], in_=pA[:, 384:512], func=AF.Silu)
            act.op("activation", [pZ], [zdsil], out=zdsil[:], in_=pZ[:, 0:128], func=AF.Silu)
            dve.op("tensor_copy", [ps_a], [xhist], out=xhist[:, :, 3:131], in_=ps_a[:, 0:384].rearrange("p (c t) -> p c t", c=3))
            for h in range(2):
                tr(ps_t, ps_t[0:64, h, :], kvb[:, h * 64:(h + 1) * 64], [kvb, identb], identb[:])
            for h in range(2):
                act.op("activation", [ps_t], [knT[h]], out=knT[h][:], in_=ps_t[0:64, h, :], func=AF.Copy)
            knew = knT
            for j in range(NBS):
                r0 = 4 * j
                sp.dma(ptb[:], ptab.t[j:j + 1, :].to_broadcast([128, NPG]), [ptab], [ptb], ptb)
                dve.op("tensor_scalar", [ptb, iota4], [idxb], out=idxb[:], in0=ptb[:], scalar1=512.0, scalar2=iota4[:, 0:1], op0=ALU.mult, op1=ALU.add)
                dve.op("tensor_scalar", [idxb], [idx], out=idx[:], in0=idxb[:], scalar1=float(l * cfg.NPOOL * 512 + hg), scalar2=None, op0=ALU.add)
                for pg in range(NPG):
                    bk_, bv_ = pgk[pg % 2], pgv[pg % 2]
                    pgb = pgbs[pg % 2]
                    ptb_, so = (ps_t, 2) if pg % 2 == 0 else (ps_tg, 4)
                    pool.idma(bk_[:], cache_k[:, :], idx[:, pg:pg + 1], [cache_k, idx], [bk_], bk_)
                    pool.idma(bv_[:], cache_v[:, :], idx[:, pg:pg + 1], [cache_v, idx], [bv_], bv_)
                    act.op("activation", [bk_], [pgb], out=pgb[:, 0:128], in_=bk_[:], func=AF.Copy)
                    for h in range(2):
                        tr(ptb_, ps_t[0:64, so + h, :], pgb[:, h * 64:(h + 1) * 64], [pgb, identb], identb[:])
                    for h in range(2):
                        act.op("activation", [ptb_], [KT[h]], out=KT[h][0:64, pg * 128:(pg + 1) * 128], in_=ps_t[0:64, so + h, :], func=AF.Copy)
                    dve.op("tensor_copy", [bv_], [VA], out=VA[:, pg, :, 0:64], in_=bv_[:].rearrange("p (h f) -> p h f", h=2))
                    kmean_update(pg, pgb, lambda h: pgb[:, h * 64:(h + 1) * 64])
                for h in range(2):
                    dve.op("tensor_copy", [knew[h]], [KT[h]], out=KT[h][0:64, NK:NK + 4], in_=knew[h][0:64, r0:r0 + 4])
                mm(ps_b, ps_b[0:4, 0:128], identb[:, r0:r0 + 4], kvb[:, 128:256], [identb, kvb])
                dve.op("tensor_copy", [ps_b], [VA], out=VA[0:4, NKT, :, 0:64], in_=ps_b[0:4, 0:128].rearrange("p (h f) -> p h f", h=2))
                nb = NPG // 2
                for h in range(2):
                    mm(ps_b, ps_b[:, 256 + h * 32:256 + h * 32 + nb], qTt[h][:], kmT[h][:, 0:nb], [qTt[h], kmT[h]])
                dve.op("memset", [], [qa], qa[:, :, 64:96], 0.0)
                if nb > 3:
                    dve.op("memset", [], [gate_sb], gate_sb[:], -1e30)
                    dve.op("tensor_copy", [ps_b], [gate_sb], out=gate_sb[:, :, 0:nb],
                           in_=ps_b[:, 256:320].rearrange("p (h f) -> p h f", h=2)[:, :, 0:nb])
                    for h in range(2):
                        dve.op("max", [gate_sb], [top8], out=top8[:, h, :], in_=gate_sb[:, h, :])
                        dve.op("tensor_scalar", [gate_sb, top8], [qa], out=qa[:, h, 64:64 + nb], in0=gate_sb[:, h, 0:nb],
                               scalar1=top8[:, h, 2:3], scalar2=NEG, op0=ALU.is_lt, op1=ALU.mult)
                for h in range(2):
                    qaug_T(h, QT[h], QT[h][:, 0:128])
                    ktl = [(KT[h][:, kt * 128:(kt + 1) * 128], 128, VA[:, kt, h, :], None) for kt in range(NPG)]
                    ktl.append((KT[h][:, NK:NK + 4], 4, VA[0:4, NKT, h, :], stair[0:4, 0, 0:4]))
                    attention(h, QT[h], QT[h][:, r0:r0 + 4], 4, ktl, None)
                    dve.op("tensor_copy", [ps_o], [OTs], out=OTs[:, 0:4], in_=ps_o[0:65, 0:4])
                    tr(ps_b, ps_b[0:4, 128:193], OTs[0:65, 0:4], [OTs, identf], identf[0:65, 0:65])
                    dve.op("reciprocal", [ps_b], [rden], out=rden[0:4, :], in_=ps_b[0:4, 192:193])
                    dve.op("tensor_scalar", [ps_b, rden], [oas], out=oas[0:4, h * 64:(h + 1) * 64], in0=ps_b[0:4, 128:192], scalar1=rden[0:4, 0:1],
                           scalar2=None, op0=ALU.mult)
                sp.dma(xhs[:, :, 0:3], sconv[l, j, hg, :, :].rearrange("p (c t) -> p c t", c=3), [sconv], [xhs], xhs)
                dve.op("tensor_copy", [xhist], [xhs], out=xhs[:, :, 3:7], in_=xhist[:, :, 3 + r0:3 + r0 + 4])
                sp.dma(G["S"][:], ssm0[l, j, hg, :, :], [ssm0], [G["S"]], G["S"])
                act.op("activation", [G["S"]], [G["Sb"]], out=G["Sb"][:], in_=G["S"][:], func=AF.Copy)
                gdn_chunk(4, r0, xhs, xhs[:, :, 0:7], False, pZ)
                sp.dma(ssms[l, j, hg, :, :], G["S"][:], [G["S"]], [ssms], G["S"])
                sp.dma(convs[l, j, hg, :, :].rearrange("p (c t) -> p c t", c=3), xhs[:, :, 4:7], [xhs], [convs], xhs)
                act.op("activation", [G["od"]], [oas], out=oas[0:4, 128:256], in_=G["od"][0:4, :], func=AF.Copy)
                mm(ps_a, ps_a[:, 0:256], esel[0:4, j, :], oas[0:4, :], [esel, oas], start=(j == 0), stop=(j == NBS - 1))
            dve.op("tensor_tensor", [ps_a, zsil], [mixs], out=mixs[:, hg * 256:hg * 256 + 128], in0=ps_a[:, 0:128], in1=zsil[:, 0, :], op=ALU.mult)
            dve.op("tensor_tensor", [ps_a, zdsil], [mixs], out=mixs[:, hg * 256 + 128:hg * 256 + 256], in0=ps_a[:, 128:256], in1=zdsil[:], op=ALU.mult)

            dve.op("memset", [], [G["S"]], G["S"][:], 0.0)
            dve.op("memset", [], [G["Sb"]], G["Sb"][:], 0.0)
            dve.op("memset", [], [xhists[1]], xhists[1][:, :, 128:131], 0.0)
            dve.op("memset", [], [gate_sb], gate_sb[:], -1e30)

            def stream_b(i):
                p = i % 2
                c = i % 4
                pZ, xh, zd = pZs[p], xhists[p], zdsils[p]
                sp.dma(xt[:], xsrc[i * 128:(i + 1) * 128, :], [xsrc], [xt], xt)
                sp.dma(cost[:], c_cosp[i * 128:(i + 1) * 128, :], [c_cosp], [cost], cost)
                sp.dma(sint[:], c_sinp[i * 128:(i + 1) * 128, :], [c_sinp], [sint], sint)
                front(l, hg, xt, mod_p, cost, sint, pZ)
                sp.dma(nkp[l, i * 128:(i + 1) * 128, hg * 128:(hg + 1) * 128], qkr[:, 128:256], [qkr], [nkp], qkr)
                sp.dma(nvp[l, i * 128:(i + 1) * 128, hg * 128:(hg + 1) * 128], pA[:, 256:384], [pA], [nvp], pA)
                act.op("activation", [pA], [zsil], out=zsil[:, c, :], in_=pA[:, 384:512], func=AF.Silu)
                act.op("activation", [pZ], [zd], out=zd[:], in_=pZ[:, 0:128], func=AF.Silu)
                dve.op("tensor_copy", [ps_a], [xh], out=xh[:, :, 3:131], in_=ps_a[:, 0:384].rearrange("p (c t) -> p c t", c=3))
                dve.op("tensor_copy", [xhists[1 - p]], [xh], out=xh[:, :, 0:3], in_=xhists[1 - p][:, :, 128:131])
                store_kv_tile(i)
                qT_and_gate(i // 2, True)
                kmean_update(i, kvb, lambda h: kvb[:, h * 64:(h + 1) * 64])
                for h in range(2):
                    qaug_T(h, QT[h], QT[h][:, c * 128:(c + 1) * 128])
                if c == 3 or i == NT - 1:
                    g0 = (i // 4) * 4
                    ng = i - g0 + 1
                    NQ = ng * 128
                    for h in range(2):
                        ktl = []
                        for kt in range(0, i + 1):
                            mask = stair[:, kt - g0, 0:NQ] if kt >= g0 else None
                            ktl.append((KT[h][:, kt * 128:(kt + 1) * 128], 128, VA[:, kt, h, :], mask))
                        attention(h, QT[h], QT[h][:, 0:NQ], NQ, ktl, None)
                        dve.op("tensor_copy", [ps_o], [OTs], out=OTs[:, 0:NQ], in_=ps_o[0:65, 0:NQ])
                        for cc in range(ng):
                            tr(ps_b, ps_b[:, 0:65], OTs[0:65, cc * 128:(cc + 1) * 128], [OTs, identf], identf[0:65, 0:65])
                            dve.op("reciprocal", [ps_b], [rden], out=rden[:], in_=ps_b[:, 64:65])
                            dve.op("scalar_tensor_tensor", [ps_b, rden, zsil], [mix], out=mix[:, h * 64:(h + 1) * 64], in0=ps_b[:, 0:64],
                                   scalar=rden[:, 0:1], in1=zsil[:, cc, h * 64:(h + 1) * 64], op0=ALU.mult, op1=ALU.mult)
                            ti = g0 + cc
                            sp.dma(mixp[ti * 128:(ti + 1) * 128, hg * 256 + h * 64:hg * 256 + (h + 1) * 64], mix[:, h * 64:(h + 1) * 64],
                                   [mix], [mixp], mix)

            def stream_a(i):
                p = i % 2
                pZ, xh, zd = pZs[p], xhists[p], zdsils[p]
                gdn_chunk(128, 0, xh, xh[:, :, :], True, pZ)
                dve.op("tensor_tensor", [G["od"], zd], [mixd], out=mixd[:, :], in0=G["od"][:, :], in1=zd[:], op=ALU.mult)
                sp.dma(mixp[i * 128:(i + 1) * 128, hg * 256 + 128:hg * 256 + 256], mixd[:, :], [mixd], [mixp], mixd)

            stream_b(0)
            for i in range(NT):
                A = kb.record(lambda: stream_a(i))
                B = kb.record(lambda: stream_b(i + 1)) if i + 1 < NT else []
                kb.merge_emit(A, B)
            xhist = xhists[(NT - 1) % 2]
            dve.op("tensor_copy", [xhist], [xhs], out=xhs[:, :, 0:3], in_=xhist[:, :, 128:131])
            sp.dma(ssmp[l, hg, :, :], G["S"][:], [G["S"]], [ssmp], G["S"])
            sp.dma(convp[l, hg, :, :].rearrange("p (c t) -> p c t", c=3), xhs[:, :, 0:3], [xhs], [convp], xhs)

        if cfg.DEBUG:
            sp.dma(dbg_mixs[l, :, :], mixs[:], [mixs], [dbg_mixs], mixs)
            out_bufs.append(dbg_mixs)

        def phase2(mix_buf, mix_ap, xtile, mod, out_buf, out_ap, keep=None):
            for rnd in range(2):
                for k in range(4):
                    tr(ps_t, ps_t[:, k, :], mix_ap[:, (rnd * 4 + k) * 128:(rnd * 4 + k + 1) * 128], [mix_buf, identb], identb[:])
                act.op("activation", [ps_t], [mT], out=mT[:, rnd * 4:rnd * 4 + 4, :], in_=ps_t[:, 0:4, :], func=AF.Copy)
            for half, pb_ in ((0, ps_a), (1, ps_b)):
                for k in range(8):
                    mm(pb_, pb_[:, :], mT[:, k, :], wo[:, k, half * 512:(half + 1) * 512], [mT, wo], start=(k == 0), stop=(k == 7))
                dve.op("tensor_tensor", [pb_, mod], [rt], out=rt[:, half * 512:(half + 1) * 512], in0=pb_[:, :],
                       in1=mod[:, 2 * D + half * 512:2 * D + (half + 1) * 512], op=ALU.mult)
            dve.op("scalar_tensor_tensor", [xtile, rt], [rt], out=rt[:], in0=xtile[:], scalar=cfg.ALPHA, in1=rt[:], op0=ALU.mult, op1=ALU.add)
            for half in range(2):
                dve.op("bn_stats", [rt], [stats], out=stats[:, half, :], in_=rt[:, half * 512:(half + 1) * 512])
            dve.op("bn_aggr", [stats], [mv], out=mv[:], in_=stats[:].rearrange("p a b -> p (a b)"))
            dve.op("tensor_scalar", [mv], [rstd], out=rstd[:], in0=mv[:, 1:2], scalar1=LN_EPS, scalar2=None, op0=ALU.add)
            act.op("activation", [rstd], [rstd], out=rstd[:], in_=rstd[:], func=AF.Sqrt)
            dve.op("reciprocal", [rstd], [rstd], out=rstd[:], in_=rstd[:])
            dve.op("tensor_scalar", [rt, mv, rstd], [rt], out=rt[:], in0=rt[:], scalar1=mv[:, 0:1], scalar2=rstd[:, 0:1], op0=ALU.subtract, op1=ALU.mult)
            pool.op("tensor_tensor", [rt, lng], [rt], out=rt[:], in0=rt[:], in1=lng[:], op=ALU.mult)
            tgt = keep if keep is not None else rt
            dve.op("tensor_tensor", [rt, lnb], [tgt], out=tgt[:], in0=rt[:], in1=lnb[:], op=ALU.add)
            if out_buf is not None:
                sp.dma(out_ap, tgt[:], [tgt], [out_buf], tgt)

        for i in range(NT):
            sp.dma(xt[:], xsrc[i * 128:(i + 1) * 128, :], [xsrc], [xt], xt)
            sp.dma(mixl[:], mixp[i * 128:(i + 1) * 128, :], [mixp], [mixl], mixl)
            if l == DEPTH - 1:
                phase2(mixl, mixl, xt, mod_p, yp, yp[i * 128:(i + 1) * 128, :])
            else:
                phase2(mixl, mixl, xt, mod_p, x1, x1[i * 128:(i + 1) * 128, :])
        if l == DEPTH - 1:
            phase2(mixs, mixs, xst, mod_s, ys, ys[:, :])
        else:
            phase2(mixs, mixs, xst, mod_s, None, None, keep=xst)

    for e in (sp, pool, act, dve, pe):
        e.wait_all(out_bufs)
    return kb


_CACHE = {}


def _consts(cfg):
    SEQ, NK = cfg.SEQ, cfg.NK
    bf = ml_dtypes.bfloat16
    c = {}
    c["c_identf"] = np.eye(128, dtype=np.float32)
    c["c_identb"] = np.eye(128, dtype=np.float32).astype(bf)
    t = np.arange(128)
    tri = np.zeros((128, 2, 128), np.float32)
    tri[:, 0, :] = (t[:, None] <= t[None, :])
    tri[:, 1, :] = (t[:, None] < t[None, :])
    c["c_tri"] = tri
    st = np.zeros((128, 4, 512), np.float32)
    for r in range(4):
        for cc in range(4):
            blk = st[:, r, cc * 128:(cc + 1) * 128]
            if cc < r:
                blk[:] = NEG
            elif cc == r:
                blk[:] = np.where(t[:, None] <= t[None, :], 0.0, NEG)
    c["c_stair"] = st.astype(bf)
    e = np.zeros((32, NK + 128), np.float32)
    pos = np.arange(NK)
    for j in range(32):
        e[j, :NK] = (pos // 256 == j)
    c["c_eblk"] = e.astype(bf)
    half = 32
    inv = (10000.0 ** (-np.arange(half, dtype=np.float32) / half)).astype(np.float32)

    def tab(pos):
        ang = pos.astype(np.float32)[:, None] * inv[None, :]
        return np.tile(np.cos(ang).astype(np.float32), (1, 4)), np.tile(np.sin(ang).astype(np.float32), (1, 4))
    c["c_cosp"], c["c_sinp"] = tab(np.arange(SEQ))
    ps = cfg.PAST + (np.arange(128) % 4)
    cs, sn = tab(ps)
    c["c_coss"], c["c_sins"] = cs, sn
    c["c_iota4"] = (4 * np.arange(128, dtype=np.float32)).reshape(128, 1)
    es = np.zeros((4, cfg.NBS, 128), np.float32)
    for j in range(cfg.NBS):
        for tt in range(4):
            es[tt, j, 4 * j + tt] = 1.0
    c["c_esel"] = es.astype(bf)
    bd = np.zeros((128, 7, 128), np.float32)
    prev = None
    for li, sz in enumerate([2, 4, 8, 16, 32, 64, 128]):
        m_ = (t[:, None] // sz == t[None, :] // sz).astype(np.float32)
        bd[:, li, :] = m_ if prev is None else m_ - prev
        prev = m_
    c["c_bdm"] = bd.astype(bf)
    return c


def _run(cfg, x_prompt, x_sample, c_prompt, c_sample, cache_k, cache_v, page_table,
         state_ssm, state_conv, w_ada, b_ada, w_in, conv_w, a_log, dt_bias,
         dn_norm_w, w_out, ln_g, ln_b):
    f = np.float32
    key = (cfg.SEQ, cfg.PAST, cfg.DEPTH, cfg.DECB)
    if key not in _CACHE:
        _CACHE[key] = build(cfg)
    kb = _CACHE[key]
    DEPTH = cfg.DEPTH
    A, Dq = 512, 1536
    w_in = np.asarray(w_in, f)
    win_r = np.zeros((DEPTH, 4, D, 1026), f)
    for hg in range(4):
        a0 = hg * 128
        cols = []
        for g in range(4):
            cols += list(range(g * 512 + a0, g * 512 + a0 + 128))
        cols += list(range(2048 + 1536 + a0, 2048 + 1536 + a0 + 128))
        cols += [2048 + 1536 + 512 + hg, 2048 + 1536 + 512 + 4 + hg]
        for g in range(3):
            cols += list(range(2048 + g * 512 + a0, 2048 + g * 512 + a0 + 128))
        win_r[:, hg] = w_in[:, :, cols]
    conv_w = np.asarray(conv_w, f)
    cw_r = np.zeros((DEPTH, 4, 128, 12), f)
    for hg in range(4):
        for g in range(3):
            blk = conv_w[:, :, g * 512 + hg * 128: g * 512 + (hg + 1) * 128]
            cw_r[:, hg, :, g * 4:(g + 1) * 4] = np.transpose(blk, (0, 2, 1))
    alog_r = np.broadcast_to(np.asarray(a_log, f)[:, :, None, None], (DEPTH, 4, 128, 1)).copy()
    dtb_r = np.broadcast_to(np.asarray(dt_bias, f)[:, :, None, None], (DEPTH, 4, 128, 1)).copy()
    dnw_r = np.broadcast_to(np.asarray(dn_norm_w, f)[:, None, :], (DEPTH, 128, 128)).copy()
    rows = []
    for hg in range(4):
        rows += list(range(hg * 128, (hg + 1) * 128)) + list(range(512 + hg * 128, 512 + (hg + 1) * 128))
    wout_r = np.ascontiguousarray(np.asarray(w_out, f)[:, rows, :])
    ck = np.ascontiguousarray(np.asarray(cache_k, f)).reshape(-1, 128)
    cv = np.ascontiguousarray(np.asarray(cache_v, f)).reshape(-1, 128)
    consts = _consts(cfg)
    state_conv = np.asarray(state_conv, f)
    state_ssm = np.asarray(state_ssm, f)
    in_maps = []
    NC_, NBS = cfg.NCORES, cfg.NBS
    cpb = NC_ // cfg.BATCH
    for c in range(NC_):
        b = c // cpb
        m = dict(consts)
        m["xp"] = np.ascontiguousarray(np.asarray(x_prompt, f)[b])
        xs_ = np.zeros((128, D), f)
        cTs = np.zeros((128, 8, 128), f)
        sc = np.zeros((DEPTH, NBS, 4, 128, 9), f)
        for j in range(NBS):
            sb_ = NBS * c + j
            xs_[4 * j:4 * j + 4] = np.asarray(x_sample, f)[sb_]
            cTs[:, :, 4 * j:4 * j + 4] = np.asarray(c_sample, f)[sb_].reshape(8, 128).T[:, :, None]
            for hg in range(4):
                for g in range(3):
                    blk = state_conv[:, sb_, :, g * 512 + hg * 128:g * 512 + (hg + 1) * 128]
                    sc[:, j, hg, :, g * 3:(g + 1) * 3] = np.transpose(blk, (0, 2, 1))
        m["xs"] = xs_
        m["cTs"] = cTs
        m["cTp"] = np.broadcast_to(np.asarray(c_prompt, f)[b].reshape(8, 128).T[:, :, None], (128, 8, 128)).copy()
        m["w_ada"] = np.asarray(w_ada, f); m["b_ada"] = np.asarray(b_ada, f)
        m["w_in"] = win_r; m["conv_w"] = cw_r; m["alog"] = alog_r; m["dtb"] = dtb_r; m["dnw"] = dnw_r
        m["w_out"] = wout_r; m["ln_g"] = np.asarray(ln_g, f); m["ln_b"] = np.asarray(ln_b, f)
        m["cache_k"] = ck; m["cache_v"] = cv
        m["ptab"] = np.ascontiguousarray(np.asarray(page_table, np.int32)[NBS * c:NBS * c + NBS])
        m["ssm0"] = np.ascontiguousarray(state_ssm[:, NBS * c:NBS * c + NBS])
        m["sconv"] = sc
        in_maps.append(m)
    res = run_bass_kernel_spmd(kb.nc, in_maps, core_ids=list(range(NC_)))
    R = res.results
    B, SEQ, DECB = cfg.BATCH, cfg.SEQ, cfg.DECB
    y_prompt = np.stack([R[cpb * b]["yp"] for b in range(B)]).astype(f)
    y_sample = np.zeros((DECB, 4, D), f)
    nk_p = np.zeros((DEPTH, B, SEQ, 8, 64), f); nv_p = np.zeros_like(nk_p)
    ssm_p = np.zeros((DEPTH, B, 4, 128, 128), f); conv_p = np.zeros((DEPTH, B, 3, 1536), f)
    nk_s = np.zeros((DEPTH, DECB, 4, 8, 64), f); nv_s = np.zeros_like(nk_s)
    ssm_s = np.zeros((DEPTH, DECB, 4, 128, 128), f); conv_s = np.zeros((DEPTH, DECB, 3, 1536), f)
    for b in range(B):
        r = R[cpb * b]
        nk_p[:, b] = r["nkp"].reshape(DEPTH, SEQ, 8, 64)
        nv_p[:, b] = r["nvp"].reshape(DEPTH, SEQ, 8, 64)
        ssm_p[:, b] = r["ssmp"]
        cp = r["convp"].reshape(DEPTH, 4, 128, 3, 3)
        for hg in range(4):
            for g in range(3):
                conv_p[:, b, :, g * 512 + hg * 128:g * 512 + (hg + 1) * 128] = np.transpose(cp[:, hg, :, g, :], (0, 2, 1))
    for c in range(NC_):
        r = R[c]
        for j in range(NBS):
            sb_ = NBS * c + j
            y_sample[sb_] = r["ys"][4 * j:4 * j + 4]
            nk_s[:, sb_] = r["nks"][:, 4 * j:4 * j + 4].reshape(DEPTH, 4, 8, 64)
            nv_s[:, sb_] = r["nvs"][:, 4 * j:4 * j + 4].reshape(DEPTH, 4, 8, 64)
            ssm_s[:, sb_] = r["ssms"][:, j]
            cs = r["convs"][:, j].reshape(DEPTH, 4, 128, 3, 3)
            for hg in range(4):
                for g in range(3):
                    conv_s[:, sb_, :, g * 512 + hg * 128:g * 512 + (hg + 1) * 128] = np.transpose(cs[:, hg, :, g, :], (0, 2, 1))
    return (y_prompt, y_sample, nk_p, nv_p, ssm_p, conv_p, nk_s, nv_s, ssm_s, conv_s)


def kernel(**inputs):
    cfg = Cfg()
    return _run(cfg, **inputs)
```

```python
import math
import numpy as np
import ml_dtypes
import concourse.bass as bass
import concourse.mybir as mybir
from concourse.bass_utils import run_bass_kernel_spmd

F32 = mybir.dt.float32
BF16 = mybir.dt.bfloat16
I32 = mybir.dt.int32
AF = mybir.ActivationFunctionType
ALU = mybir.AluOpType

import os
ROLL = int(os.environ.get('K_ROLL', '30000'))
D = 1024
NEG = -30000.0
L2_EPS = 1e-6
RMS_EPS = 1e-6
LN_EPS = 1e-5


class Cfg:
    def __init__(self, seq=8192, past=8192, depth=2, dec_batch=32, batch=2, ncores=2):
        self.SEQ = seq
        self.PAST = past
        self.DEPTH = depth
        self.DECB = dec_batch
        self.BATCH = batch
        self.NT = seq // 128
        self.NPAGES = past // 128
        self.NPOOL = (dec_batch * self.NPAGES * 5) // 4
        self.NK = max(seq, past)
        self.NKT = self.NK // 128
        self.ALPHA = (2 * depth) ** 0.25
        self.DEBUG = False
        self.NCORES = ncores
        self.NBS = dec_batch // ncores


class Buf:
    __slots__ = ("t", "name", "w", "r", "dsem", "dcnt")

    def __init__(self, t, name):
        self.t = t
        self.name = name
        self.w = None
        self.r = {}
        self.dsem = None
        self.dcnt = 0

    def __getitem__(self, k):
        return self.t[k]


class Eng:
    def __init__(self, kb, eng, name, selfsync=True):
        self.kb = kb
        self.eng = eng
        self.name = name
        self.selfsync = selfsync
        self.sem = kb.nc.alloc_semaphore(f"p_{name}_0")
        self.nsem = 1
        self.cnt = 0
        self.waited = {}
        self.n_ins = 0

    def _wait(self, tok):
        if tok is None:
            return
        sem, val = tok
        if (not self.selfsync) and sem is self.sem:
            return
        k = id(sem)
        if self.waited.get(k, 0) >= val:
            return
        self.eng.wait_ge(sem, val)
        self.waited[k] = val
        self.n_ins += 1
        self.n_wait = getattr(self, "n_wait", 0) + 1

    def _pre(self, reads, writes):
        for b in reads:
            self._wait(b.w)
        for b in writes:
            self._wait(b.w)
            for t in b.r.values():
                self._wait(t)

    def _post(self, tok, reads, writes):
        sem, val = tok
        for b in reads:
            b.r[id(sem)] = tok
        for b in writes:
            b.w = tok
            b.r = {}

    def op(self, name, reads, writes, *a, **kw):
        if self.kb.rec is not None:
            self.kb.rec.append((self, "op", name, reads, writes, a, kw))
            return None
        self._pre(reads, writes)
        ins = getattr(self.eng, name)(*a, **kw)
        if self.cnt >= ROLL:
            self.kb.all_sems.append(self.sem)
            self.sem = self.kb.nc.alloc_semaphore(f"p_{self.name}_{self.nsem}")
            self.nsem += 1
            self.cnt = 0
        self.cnt += 1
        ins.then_inc(self.sem, 1)
        self.n_ins += 1
        tok = (self.sem, self.cnt)
        self._post(tok, reads, writes)
        return tok

    def _dsem(self, sb):
        if sb.dsem is None:
            sb.dsem = self.kb.nc.alloc_semaphore(f"d_{sb.name}")
        return sb.dsem

    def dma(self, out, in_, reads, writes, sb, **kw):
        if self.kb.rec is not None:
            self.kb.rec.append((self, "dma", None, reads, writes, (out, in_, sb), kw))
            return None
        self._pre(reads, writes)
        sem = self._dsem(sb)
        ins = self.eng.dma_start(out=out, in_=in_, **kw)
        sb.dcnt += 1
        ins.then_inc(sem, 16)
        self.n_ins += 1
        tok = (sem, 16 * sb.dcnt)
        self._post(tok, reads, writes)
        return tok

    def idma(self, out, in_, idx_ap, reads, writes, sb):
        self._pre(reads, writes)
        sem = self._dsem(sb)
        ins = self.eng.indirect_dma_start(
            out=out, out_offset=None, in_=in_,
            in_offset=bass.IndirectOffsetOnAxis(ap=idx_ap, axis=0))
        sb.dcnt += 1
        ins.then_inc(sem, 16)
        self.n_ins += 1
        tok = (sem, 16 * sb.dcnt)
        self._post(tok, reads, writes)
        return tok

    def wait_all(self, bufs):
        for b in bufs:
            self._wait(b.w)
            for t in b.r.values():
                self._wait(t)


class KB:
    def __init__(self):
        self.nc = bass.Bass("TRN2", target_bir_lowering=False)
        nc = self.nc
        self.pe = Eng(self, nc.tensor, "pe", selfsync=False)
        self.act = Eng(self, nc.scalar, "act")
        self.dve = Eng(self, nc.vector, "dve")
        self.pool = Eng(self, nc.gpsimd, "pool")
        self.sp = Eng(self, nc.sync, "sp")
        self.ext_in = {}
        self.ext_out = {}
        self.rec = None
        self.all_sems = []

    def record(self, fn):
        assert self.rec is None
        self.rec = []
        fn()
        r, self.rec = self.rec, None
        return r

    def emit(self, item):
        eng, kind, name, reads, writes, a, kw = item
        if kind == "op":
            eng.op(name, reads, writes, *a, **kw)
        else:
            out, in_, sb = a
            eng.dma(out, in_, reads, writes, sb, **kw)

    def merge_emit(self, A, B):
        import os
        if os.environ.get("K_NOMERGE"):
            for it in A:
                self.emit(it)
            for it in B:
                self.emit(it)
            return
        na, nb = len(A), len(B)
        ia = ib = 0
        while ia < na or ib < nb:
            if ib >= nb or (ia < na and ia * nb <= ib * na):
                self.emit(A[ia]); ia += 1
            else:
                self.emit(B[ib]); ib += 1

    def sb(self, name, shape, dt=F32):
        return Buf(self.nc.alloc_sbuf_tensor(name, list(shape), dt), name)

    def ps(self, name, shape, dt=F32):
        return Buf(self.nc.alloc_psum_tensor(name, list(shape), dt), name)

    def dram(self, name, shape, dt=F32, kind="Internal"):
        b = Buf(self.nc.dram_tensor(name, list(shape), dt, kind=kind).ap(), name)
        if kind == "ExternalInput":
            self.ext_in[name] = b
        elif kind == "ExternalOutput":
            self.ext_out[name] = b
        return b

    def alias(self, buf, name):
        return Buf(buf.t, name)


def build(cfg):
    kb = KB()
    pe, act, dve, pool, sp = kb.pe, kb.act, kb.dve, kb.pool, kb.sp
    SEQ, NT, NK, NKT, NPG, DEPTH = cfg.SEQ, cfg.NT, cfg.NK, cfg.NKT, cfg.NPAGES, cfg.DEPTH
    NPOOLROWS = DEPTH * cfg.NPOOL * 128 * 4

    def IN(name, shape, dt=F32):
        return kb.dram(name, shape, dt, kind="ExternalInput")

    def OUT(name, shape, dt=F32):
        return kb.dram(name, shape, dt, kind="ExternalOutput")

    xp = IN("xp", [SEQ, D]); xs = IN("xs", [128, D])
    cTp = IN("cTp", [128, 8, 128]); cTs = IN("cTs", [128, 8, 128])
    w_ada = IN("w_ada", [DEPTH, D, 3 * D]); b_ada = IN("b_ada", [DEPTH, 3 * D])
    w_in = IN("w_in", [DEPTH, 4, D, 1026])
    conv_w = IN("conv_w", [DEPTH, 4, 128, 12])
    alog = IN("alog", [DEPTH, 4, 128, 1]); dtb = IN("dtb", [DEPTH, 4, 128, 1])
    dnw = IN("dnw", [DEPTH, 128, 128])
    w_out = IN("w_out", [DEPTH, D, D])
    ln_g = IN("ln_g", [DEPTH, D]); ln_b = IN("ln_b", [DEPTH, D])
    cache_kv = IN("cache_kv", [NPOOLROWS, 256])
    NBS = cfg.NBS
    ptab = IN("ptab", [NBS, NPG], I32)
    ssm0 = IN("ssm0", [DEPTH, NBS, 4, 128, 128]); sconv = IN("sconv", [DEPTH, NBS, 4, 128, 9])
    c_identf = IN("c_identf", [128, 128]); c_identb = IN("c_identb", [128, 128], BF16)
    c_tri = IN("c_tri", [128, 2, 128])
    c_stair = IN("c_stair", [128, 4, 512], BF16)
    c_eblk = IN("c_eblk", [32, NK + 128], BF16)
    c_cosp = IN("c_cosp", [SEQ, 128]); c_sinp = IN("c_sinp", [SEQ, 128])
    c_coss = IN("c_coss", [128, 128]); c_sins = IN("c_sins", [128, 128])
    c_iota4 = IN("c_iota4", [128, 1])
    c_esel = IN("c_esel", [4, NBS, 128], BF16)
    c_bdm = IN("c_bdm", [128, 7, 128], BF16)

    yp = OUT("yp", [SEQ, D]); ys = OUT("ys", [128, D])
    nkp = OUT("nkp", [DEPTH, SEQ, 512]); nvp = OUT("nvp", [DEPTH, SEQ, 512])
    ssmp = OUT("ssmp", [DEPTH, 4, 128, 128]); convp = OUT("convp", [DEPTH, 4, 128, 9])
    nks = OUT("nks", [DEPTH, 128, 512]); nvs = OUT("nvs", [DEPTH, 128, 512])
    ssms = OUT("ssms", [DEPTH, NBS, 4, 128, 128]); convs = OUT("convs", [DEPTH, NBS, 4, 128, 9])
    out_bufs = [yp, ys, nkp, nvp, ssmp, convp, nks, nvs, ssms, convs]

    mixp = kb.dram("mixp", [SEQ, D], BF16, kind="ExternalOutput" if cfg.DEBUG else "Internal")
    dbg_mixs = kb.dram("dbg_mixs", [DEPTH, 128, D], BF16, kind="ExternalOutput") if cfg.DEBUG else None
    x1 = kb.dram("x1", [SEQ, D], F32)

    identf = kb.sb("identf", [128, 128]); identb = kb.sb("identb", [128, 128], BF16)
    tri = kb.sb("tri", [128, 2, 128])
    stair = kb.sb("stair", [128, 4, 512], BF16)
    iota4 = kb.sb("iota4", [128, 1]); esel = kb.sb("esel", [4, NBS, 128], BF16)
    ones_f = kb.sb("ones_f", [128, 128]); ones_b = kb.sb("ones_b", [128, 1], BF16)
    coss = kb.sb("coss", [128, 128]); sins = kb.sb("sins", [128, 128])
    for dst, src in ((identf, c_identf), (identb, c_identb), (iota4, c_iota4), (coss, c_coss), (sins, c_sins)):
        sp.dma(dst[:], src[:, :], [src], [dst], dst)
    sp.dma(tri[:], c_tri[:, :, :], [c_tri], [tri], tri)
    sp.dma(stair[:], c_stair[:, :, :], [c_stair], [stair], stair)
    sp.dma(esel[:], c_esel[:, :, :], [c_esel], [esel], esel)
    bdm = kb.sb("bdm", [128, 7, 128], BF16)
    sp.dma(bdm[:], c_bdm[:, :, :], [c_bdm], [bdm], bdm)
    dve.op("memset", [], [ones_f], ones_f[:], 1.0)
    dve.op("memset", [], [ones_b], ones_b[:], 1.0)

    KT = [kb.sb(f"KT{h}", [96, NK + 128], BF16) for h in range(2)]
    VA = kb.sb("VA", [128, NKT + 1, 2, 65], BF16)
    for h in range(2):
        sp.dma(KT[h][64:96, :], c_eblk[:, :], [c_eblk], [KT[h]], KT[h])
    dve.op("memset", [], [VA], VA[:], 1.0)
    kmT = [kb.sb(f"kmT{h}", [64, 32], BF16) for h in range(2)]
    ksum = kb.sb("ksum", [64, 2])
    QT = [kb.sb(f"QT{h}", [96, 512], BF16) for h in range(2)]
    mod_p = kb.sb("mod_p", [128, 3 * D]); mod_s = kb.sb("mod_s", [128, 3 * D])
    wA = kb.sb("wA", [128, 8, 512], BF16); wZ = kb.sb("wZ", [128, 8, 130], BF16); wD = kb.sb("wD", [128, 8, 384], BF16)
    wstg = kb.sb("wstg", [128, 1026])
    wo = kb.sb("wo", [128, 8, D], BF16); wostg = wstg
    cw = kb.sb("cw", [128, 12]); aneg = kb.sb("aneg", [128, 1]); dtbt = kb.sb("dtbt", [128, 1]); dnwt = kb.sb("dnwt", [128, 128])
    lng = kb.sb("lng", [128, D]); lnb = kb.sb("lnb", [128, D])
    xt = kb.sb("xt", [128, D]); xst = kb.sb("xst", [128, D])
    hb = kb.sb("hb", [128, D], BF16); hT = kb.sb("hT", [128, 8, 128], BF16)
    pA = kb.sb("pA", [128, 512]); pZs = [kb.sb(f"pZ{i}", [128, 130]) for i in range(2)]
    xhists = [kb.sb(f"xhist{i}", [128, 3, 131]) for i in range(2)]; xhs = kb.sb("xhs", [128, 3, 8])
    cost = kb.sb("cost", [128, 128]); sint = kb.sb("sint", [128, 128])
    rtmp = kb.sb("rtmp", [128, 4, 128]); qkr = kb.sb("qkr", [128, 256])
    kvb = kb.sb("kvb", [128, 256], BF16)
    qa = kb.sb("qa", [128, 2, 96], BF16)
    qTt = [kb.sb(f"qTt{h}", [64, 128], BF16) for h in range(2)]
    gate_sb = kb.sb("gate_sb", [128, 2, 32]); top8 = kb.sb("top8", [128, 2, 8])
    zsil = kb.sb("zsil", [128, 4, 128]); zdsils = [kb.sb(f"zdsil{i}", [128, 128]) for i in range(2)]
    PTb = [kb.sb(f"PTb{i}", [128, 512], BF16) for i in range(2)]
    OTs = kb.sb("OTs", [65, 512]); rden = kb.sb("rden", [128, 1])
    mix = kb.sb("mix", [128, 128], BF16); mixd = kb.sb("mixd", [128, 128], BF16); mixs = kb.sb("mixs", [128, D], BF16)
    oas = kb.sb("oas", [4, 256], BF16)
    mixl = kb.sb("mixl", [128, D], BF16); mT = kb.sb("mT", [128, 8, 128], BF16)
    rt = kb.sb("rt", [128, D]); stats = kb.sb("stats", [128, 2, 6]); mv = kb.sb("mv", [128, 2]); rstd = kb.sb("rstd", [128, 1])
    cst = kb.sb("cst", [128, 8, 128]); scb = kb.sb("scb", [128, 8, 128], BF16)
    wastg = kb.sb("wastg", [128, 8, 128]); wab = kb.sb("wab", [128, 8, 128], BF16); bab = kb.sb("bab", [128, 128])
    ptb = kb.sb("ptb", [128, NPG], I32); idxb = kb.sb("idxb", [128, NPG], I32); idx = kb.sb("idx", [128, NPG], I32)
    pgk = [kb.sb(f"pgk{i}", [128, 256]) for i in range(2)]
    pgbs = [kb.sb(f"pgb{i}", [128, 256], BF16) for i in range(2)]
    knT = [kb.sb(f"knT{h}", [64, 128], BF16) for h in range(2)]
    G = {}
    for nm in ["grep", "Dsb", "E1", "E1i", "E1s", "EG", "Xf", "Ksb", "Qsb", "Vsb", "u", "osb", "od", "S", "qkraw"]:
        G[nm] = kb.sb("g_" + nm, [128, 256] if nm in ("Xf", "qkraw") else [128, 128])
    for nm in ["NT", "attnT", "Zc", "Yn", "Zn", "Xb", "wb", "wT", "QTn", "KTn", "QeT", "Kd", "vnb", "Sb", "Kb", "Qb"]:
        G[nm] = kb.sb("g_" + nm, [128, 256] if nm == "Xb" else [128, 128], BF16)
    G["Y"] = [kb.sb(f"g_Y{i}", [128, 128], BF16) for i in range(2)]
    G["Z"] = [kb.sb(f"g_Z{i}", [128, 128], BF16) for i in range(2)]
    for nm in ["gcol", "egcol", "beta", "gv", "ssq", "rn", "kds", "sp_", "bq", "bk"]:
        G[nm] = kb.sb("g_" + nm, [128, 1])

    ps_t = kb.ps("ps_t", [128, 8, 128], BF16)
    ps_tg = kb.alias(ps_t, "ps_tg")
    ps_a = kb.ps("ps_a", [128, 512])
    ps_b = kb.ps("ps_b", [128, 512])
    ps_ss = [kb.ps(f"ps_s{i}", [128, 512]) for i in range(2)]
    ps_o = kb.ps("ps_o", [128, 512])
    ps_g = kb.ps("ps_g", [128, 512])
    ps_h = kb.ps("ps_h", [128, 512])

    def mm(out_buf, out_ap, lhsT, rhs, reads, start=True, stop=True):
        pe.op("matmul", reads, [out_buf], out_ap, lhsT=lhsT, rhs=rhs, start=start, stop=stop)

    def tr(out_buf, out_ap, in_ap, reads, ident_ap):
        pe.op("transpose", reads, [out_buf], out_ap, in_ap, ident_ap)

    def compute_mod(l, cT_src, mod):
        sp.dma(cst[:], cT_src[:, :, :], [cT_src], [cst], cst)
        act.op("activation", [cst], [scb], out=scb[:], in_=cst[:], func=AF.Silu)
        for cc in range(24):
            c0 = cc * 128
            sp.dma(wastg[:], w_ada[l, :, c0:c0 + 128].rearrange("(k p) c -> p k c", p=128), [w_ada], [wastg], wastg)
            sp.dma(bab[:], b_ada.t[l:l + 1, c0:c0 + 128].to_broadcast([128, 128]), [b_ada], [bab], bab)
            act.op("activation", [wastg], [wab], out=wab[:], in_=wastg[:], func=AF.Copy)
            for k in range(8):
                mm(ps_a, ps_a[:, 0:128], scb[:, k, :], wab[:, k, :], [scb, wab], start=(k == 0), stop=(k == 7))
            dve.op("tensor_tensor", [ps_a, bab], [mod], out=mod[:, c0:c0 + 128], in0=ps_a[:, 0:128], in1=bab[:], op=ALU.add)
        dve.op("tensor_scalar", [mod], [mod], out=mod[:, D:3 * D], in0=mod[:, D:3 * D], scalar1=1.0, scalar2=None, op0=ALU.add)

    def front(l, hg, xtile, mod, cos_t, sin_t, pZ):
        dve.op("tensor_tensor", [xtile, mod], [rt], out=rt[:], in0=xtile[:], in1=mod[:, D:2 * D], op=ALU.mult)
        dve.op("tensor_tensor", [rt, mod], [hb], out=hb[:], in0=rt[:], in1=mod[:, 0:D], op=ALU.add)
        for rnd in range(2):
            for k in range(4):
                tr(ps_t, ps_t[:, k, :], hb[:, (rnd * 4 + k) * 128:(rnd * 4 + k + 1) * 128], [hb, identb], identb[:])
            act.op("activation", [ps_t], [hT], out=hT[:, rnd * 4:rnd * 4 + 4, :], in_=ps_t[:, 0:4, :], func=AF.Copy)
        for k in range(8):
            mm(ps_a, ps_a[:, :], hT[:, k, :], wA[:, k, :], [hT, wA], start=(k == 0), stop=(k == 7))
        act.op("activation", [ps_a], [pA], out=pA[:], in_=ps_a[:], func=AF.Copy)
        for k in range(8):
            mm(ps_b, ps_b[:, 0:130], hT[:, k, :], wZ[:, k, :], [hT, wZ], start=(k == 0), stop=(k == 7))
        dve.op("tensor_copy", [ps_b], [pZ], out=pZ[:], in_=ps_b[:, 0:130])
        for c in range(3):
            for k in range(8):
                mm(ps_a, ps_a[:, c * 128:(c + 1) * 128], wD[:, k, c * 128:(c + 1) * 128], hT[:, k, :], [hT, wD],
                   start=(k == 0), stop=(k == 7))
        v4 = pA[:, 0:256].rearrange("p (g two f) -> p g two f", g=4, two=2)
        A_ = v4[:, :, 0, :]
        B_ = v4[:, :, 1, :]
        c4 = cos_t[:].rearrange("p (g f) -> p g f", g=4)
        s4 = sin_t[:].rearrange("p (g f) -> p g f", g=4)
        o4 = qkr[:].rearrange("p (g two f) -> p g two f", g=4, two=2)
        r4 = rtmp[:]
        dve.op("tensor_tensor", [pA, cos_t], [rtmp], out=r4[:, :, 0:32], in0=A_, in1=c4, op=ALU.mult)
        pool.op("tensor_tensor", [pA, sin_t], [rtmp], out=r4[:, :, 32:64], in0=B_, in1=s4, op=ALU.mult)
        dve.op("tensor_tensor", [rtmp], [qkr], out=o4[:, :, 0, :], in0=r4[:, :, 0:32], in1=r4[:, :, 32:64], op=ALU.subtract)
        pool.op("tensor_tensor", [pA, sin_t], [rtmp], out=r4[:, :, 64:96], in0=A_, in1=s4, op=ALU.mult)
        dve.op("tensor_tensor", [pA, cos_t], [rtmp], out=r4[:, :, 96:128], in0=B_, in1=c4, op=ALU.mult)
        dve.op("tensor_tensor", [rtmp], [qkr], out=o4[:, :, 1, :], in0=r4[:, :, 64:96], in1=r4[:, :, 96:128], op=ALU.add)
        act.op("activation", [qkr], [kvb], out=kvb[:, 0:128], in_=qkr[:, 128:256], func=AF.Copy)
        act.op("activation", [pA], [kvb], out=kvb[:, 128:256], in_=pA[:, 256:384], func=AF.Copy)
        act.op("activation", [qkr], [qa], out=qa[:, :, 0:64], in_=qkr[:, 0:128].rearrange("p (h f) -> p h f", h=2),
               func=AF.Copy, scale=0.125)

    def qT_and_gate(nblk_valid, topk):
        for h in range(2):
            tr(ps_t, ps_t[0:64, h, :], qa[:, h, 0:64], [qa, identb], identb[:])
        for h in range(2):
            act.op("activation", [ps_t], [qTt[h]], out=qTt[h][:], in_=ps_t[0:64, h, :], func=AF.Copy)
        dve.op("memset", [], [qa], qa[:, :, 64:96], 0.0)
        if topk and nblk_valid > 3:
            nb = nblk_valid
            for h in range(2):
                mm(ps_b, ps_b[:, 256 + h * 32:256 + h * 32 + nb], qTt[h][:], kmT[h][:, 0:nb], [qTt[h], kmT[h]])
            dve.op("tensor_copy", [ps_b], [gate_sb], out=gate_sb[:, :, 0:nb],
                   in_=ps_b[:, 256:320].rearrange("p (h f) -> p h f", h=2)[:, :, 0:nb])
            for h in range(2):
                dve.op("max", [gate_sb], [top8], out=top8[:, h, :], in_=gate_sb[:, h, :])
                dve.op("tensor_scalar", [gate_sb, top8], [qa], out=qa[:, h, 64:64 + nb], in0=gate_sb[:, h, 0:nb],
                       scalar1=top8[:, h, 2:3], scalar2=NEG, op0=ALU.is_lt, op1=ALU.mult)

    def qaug_T(h, dst_buf, dst_ap):
        tr(ps_t, ps_t[0:96, 2 + h, :], qa[:, h, :], [qa, identb], identb[:])
        act.op("activation", [ps_t], [dst_buf], out=dst_ap, in_=ps_t[0:96, 2 + h, :], func=AF.Copy)

    def store_kv_tile(kt, rows=128):
        for h in range(2):
            tr(ps_t, ps_t[0:64, h, :], kvb[:, h * 64:(h + 1) * 64], [kvb, identb], identb[:])
        for h in range(2):
            act.op("activation", [ps_t], [KT[h]], out=KT[h][0:64, kt * 128:(kt + 1) * 128], in_=ps_t[0:64, h, :], func=AF.Copy)
        dve.op("tensor_copy", [kvb], [VA], out=VA[:, kt, :, 0:64], in_=kvb[:, 128:256].rearrange("p (h f) -> p h f", h=2))

    def kmean_update(kt, src_kb, src_ap_fn):
        for h in range(2):
            mm(ps_b, ps_b[0:64, 384 + h:385 + h], src_ap_fn(h), ones_b[:, 0:1], [src_kb, ones_b])
        if kt % 2 == 0:
            dve.op("tensor_copy", [ps_b], [ksum], out=ksum[:], in_=ps_b[0:64, 384:386])
        else:
            j = kt // 2
            for h in range(2):
                dve.op("tensor_scalar", [ps_b, ksum], [kmT[h]], out=kmT[h][:, j:j + 1], in0=ps_b[0:64, 384 + h:385 + h],
                       scalar1=ksum[:, h:h + 1], scalar2=1.0 / 256.0, op0=ALU.add, op1=ALU.mult)

    def attention(h, q_buf, q_ap, NQ, ktiles, o_ap):
        per = max(1, 512 // NQ)
        n = len(ktiles)
        gi = 0
        first = True
        i = 0
        while i < n:
            grp = []
            while i < n and len(grp) < per and (not grp or (ktiles[i][1] == 128 and grp[0][1] == 128)):
                grp.append(ktiles[i]); i += 1
            nk = grp[0][1]
            pb = PTb[gi % 2]; ps_s = ps_ss[gi % 2]; gi += 1
            for j, (kap, nk_, vap, mask) in enumerate(grp):
                mm(ps_s, ps_s[0:nk, j * NQ:(j + 1) * NQ], kap, q_ap, [KT[h], q_buf], start=True, stop=(mask is None))
                if mask is not None:
                    mm(ps_s, ps_s[0:nk, j * NQ:(j + 1) * NQ], identb[0:nk, 0:nk], mask, [identb, stair], start=False, stop=True)
            act.op("activation", [ps_s], [pb], out=pb[0:nk, 0:len(grp) * NQ], in_=ps_s[0:nk, 0:len(grp) * NQ], func=AF.Exp)
            for j, (kap, nk_, vap, mask) in enumerate(grp):
                last = (i == n and j == len(grp) - 1)
                mm(ps_o, ps_o[0:65, 0:NQ], vap, pb[0:nk, j * NQ:(j + 1) * NQ], [VA, pb], start=first, stop=last)
                first = False

    def gdn_chunk(P, c0, conv_in_buf, conv_in_ap, pz_rows_direct, pZ):
        g = G
        Pn = P
        conv = g["Xf"]
        cq = g["qkraw"]
        cv = g["u"]
        dsts = [cq[:, 0:P], cq[:, 128:128 + P], cv[:, 0:P]]
        dbuf = [cq, cq, cv]
        for c in range(3):
            e = dve
            e.op("tensor_scalar", [conv_in_buf, cw], [dbuf[c]], out=dsts[c], in0=conv_in_ap[:, c, 0:P],
                 scalar1=cw[:, c * 4:c * 4 + 1], scalar2=None, op0=ALU.mult)
            for tap in range(1, 4):
                e.op("scalar_tensor_tensor", [conv_in_buf, cw, dbuf[c]], [dbuf[c]], out=dsts[c], in0=conv_in_ap[:, c, tap:tap + P],
                     scalar=cw[:, c * 4 + tap:c * 4 + tap + 1], in1=dsts[c], op0=ALU.mult, op1=ALU.add)
        act.op("activation", [cq], [cq], out=cq[:, 0:P], in_=cq[:, 0:P], func=AF.Silu)
        act.op("activation", [cq], [cq], out=cq[:, 128:128 + P], in_=cq[:, 128:128 + P], func=AF.Silu)
        act.op("activation", [cv], [cv], out=cv[:, 0:P], in_=cv[:, 0:P], func=AF.Silu)
        tr(ps_g, ps_g[0:P, 0:128], cq[:, 0:P], [cq, identf], identf[:])
        tr(ps_g, ps_g[0:P, 128:256], cq[:, 128:128 + P], [cq, identf], identf[:])
        tr(ps_g, ps_g[0:P, 256:384], cv[:, 0:P], [cv, identf], identf[:])
        act.op("activation", [ps_g], [g["Qsb"], g["bq"]], out=g["Qsb"][0:P, :], in_=ps_g[0:P, 0:128], func=AF.Square, accum_out=g["bq"][0:P, :])
        act.op("activation", [ps_g], [g["Ksb"], g["bk"]], out=g["Ksb"][0:P, :], in_=ps_g[0:P, 128:256], func=AF.Square, accum_out=g["bk"][0:P, :])
        for nm in ("bq", "bk"):
            dve.op("tensor_scalar", [g[nm]], [g[nm]], out=g[nm][0:P, :], in0=g[nm][0:P, :], scalar1=L2_EPS, scalar2=None, op0=ALU.add)
            act.op("activation", [g[nm]], [g[nm]], out=g[nm][0:P, :], in_=g[nm][0:P, :], func=AF.Sqrt)
            dve.op("reciprocal", [g[nm]], [g[nm]], out=g[nm][0:P, :], in_=g[nm][0:P, :])
        dve.op("tensor_scalar", [ps_g, g["bq"]], [g["Qb"]], out=g["Qb"][0:P, :], in0=ps_g[0:P, 0:128], scalar1=g["bq"][0:P, 0:1],
               scalar2=128.0 ** -0.5, op0=ALU.mult, op1=ALU.mult)
        dve.op("tensor_scalar", [ps_g, g["bk"]], [g["Ksb"]], out=g["Ksb"][0:P, :], in0=ps_g[0:P, 128:256], scalar1=g["bk"][0:P, 0:1],
               scalar2=None, op0=ALU.mult)
        act.op("activation", [g["Ksb"]], [g["Kb"]], out=g["Kb"][0:P, :], in_=g["Ksb"][0:P, :], func=AF.Copy)
        act.op("activation", [ps_g], [g["Vsb"]], out=g["Vsb"][0:P, :], in_=ps_g[0:P, 256:384], func=AF.Copy)
        tr(ps_tg, ps_t[:, 6, 0:P], g["Qb"][0:P, :], [g["Qb"], identb], identb[0:P, 0:P])
        tr(ps_tg, ps_t[:, 7, 0:P], g["Kb"][0:P, :], [g["Kb"], identb], identb[0:P, 0:P])
        act.op("activation", [ps_tg], [g["QTn"]], out=g["QTn"][:, 0:P], in_=ps_t[:, 6, 0:P], func=AF.Copy)
        act.op("activation", [ps_tg], [g["KTn"]], out=g["KTn"][:, 0:P], in_=ps_t[:, 7, 0:P], func=AF.Copy)
        if pz_rows_direct:
            bz_buf, bz = pZ, pZ[0:P, 128:130]
        else:
            mm(ps_h, ps_h[0:P, 384:386], identf[:, c0:c0 + P], pZ[:, 128:130], [identf, pZ])
            bz_buf, bz = ps_h, ps_h[0:P, 384:386]
        act.op("activation", [bz_buf], [g["beta"]], out=g["beta"][0:P, :], in_=bz[:, 0:1], func=AF.Sigmoid)
        act.op("activation", [bz_buf, dtbt], [g["sp_"]], out=g["sp_"][0:P, :], in_=bz[:, 1:2], func=AF.Exp, bias=dtbt[0:P, 0:1])
        act.op("activation", [g["sp_"]], [g["sp_"]], out=g["sp_"][0:P, :], in_=g["sp_"][0:P, :], func=AF.Ln, bias=1.0)
        dve.op("tensor_tensor", [g["sp_"], aneg], [g["gv"]], out=g["gv"][0:P, :], in0=g["sp_"][0:P, :], in1=aneg[0:P, :], op=ALU.mult)
        act.op("activation", [ones_f, g["gv"]], [g["grep"]], out=g["grep"][0:P, :], in_=ones_f[0:P, :], func=AF.Copy, scale=g["gv"][0:P, 0:1])
        mm(ps_h, ps_h[:, 0:P], g["grep"][0:P, :], tri[0:P, 0, 0:P], [g["grep"], tri])
        mm(ps_h, ps_h[0:P, 386:387], tri[0:P, 0, 0:P], g["gv"][0:P, 0:1], [tri, g["gv"]])
        dve.op("tensor_copy", [ps_h], [g["gcol"]], out=g["gcol"][0:P, :], in_=ps_h[0:P, 386:387])
        dve.op("tensor_scalar", [ps_h, g["gcol"]], [g["Dsb"]], out=g["Dsb"][0:P, 0:P], in0=ps_h[0:P, 0:P], scalar1=g["gcol"][0:P, 0:1],
               scalar2=0.0, op0=ALU.subtract, op1=ALU.min)
        act.op("activation", [g["Dsb"]], [g["E1"]], out=g["E1"][0:P, 0:P], in_=g["Dsb"][0:P, 0:P], func=AF.Exp)
        act.op("activation", [ps_h], [g["EG"]], out=g["EG"][:, 0:P], in_=ps_h[:, 0:P], func=AF.Exp)
        act.op("activation", [g["gcol"]], [g["egcol"]], out=g["egcol"][0:P, :], in_=g["gcol"][0:P, :], func=AF.Exp)
        pool.op("tensor_tensor", [g["E1"], tri], [g["E1i"]], out=g["E1i"][0:P, 0:P], in0=g["E1"][0:P, 0:P], in1=tri[0:P, 0, 0:P], op=ALU.mult)
        pool.op("tensor_tensor", [g["E1"], tri], [g["E1s"]], out=g["E1s"][0:P, 0:P], in0=g["E1"][0:P, 0:P], in1=tri[0:P, 1, 0:P], op=ALU.mult)
        mm(ps_g, ps_g[0:P, 384:384 + P], g["KTn"][:, 0:P], g["KTn"][:, 0:P], [g["KTn"]])
        dve.op("scalar_tensor_tensor", [ps_g, g["beta"], g["E1s"]], [g["NT"]], out=g["NT"][0:P, 0:P], in0=ps_g[0:P, 384:384 + P],
               scalar=g["beta"][0:P, 0:1], in1=g["E1s"][0:P, 0:P], op0=ALU.mult, op1=ALU.mult)
        mm(ps_g, ps_g[0:P, 0:P], g["KTn"][:, 0:P], g["QTn"][:, 0:P], [g["KTn"], g["QTn"]])
        dve.op("tensor_tensor", [ps_g, g["E1i"]], [g["attnT"]], out=g["attnT"][0:P, 0:P], in0=ps_g[0:P, 0:P], in1=g["E1i"][0:P, 0:P], op=ALU.mult)
        act.op("activation", [g["Vsb"]], [g["Xb"]], out=g["Xb"][0:P, 0:128], in_=g["Vsb"][0:P, :], func=AF.Copy)
        dve.op("tensor_scalar", [g["Ksb"], g["egcol"]], [g["Xb"]], out=g["Xb"][0:P, 128:256], in0=g["Ksb"][0:P, :],
               scalar1=g["egcol"][0:P, 0:1], scalar2=None, op0=ALU.mult)
        tr(ps_tg, ps_t[0:P, 6, 0:P], g["NT"][0:P, 0:P], [g["NT"], identb], identb[0:P, 0:P])
        act.op("activation", [ps_tg], [g["Zc"]], out=g["Zc"][0:P, 0:P], in_=ps_t[0:P, 6, 0:P], func=AF.Copy)
        W, WT = g["Y"][0], g["Y"][1]
        pool.op("tensor_tensor", [g["Zc"], bdm], [g["Zn"]], out=g["Zn"][0:P, 0:P], in0=g["Zc"][0:P, 0:P], in1=bdm[0:P, 0, 0:P], op=ALU.mult)
        pool.op("tensor_tensor", [g["NT"], bdm], [g["Yn"]], out=g["Yn"][0:P, 0:P], in0=g["NT"][0:P, 0:P], in1=bdm[0:P, 0, 0:P], op=ALU.mult)
        dve.op("tensor_tensor", [identb, g["Zn"]], [W], out=W[0:P, 0:P], in0=identb[0:P, 0:P], in1=g["Zn"][0:P, 0:P], op=ALU.subtract)
        dve.op("tensor_tensor", [identb, g["Yn"]], [WT], out=WT[0:P, 0:P], in0=identb[0:P, 0:P], in1=g["Yn"][0:P, 0:P], op=ALU.subtract)
        s_ = 2
        li = 1
        while s_ < P:
            C, CT, T1, T1p = g["Zn"], g["Yn"], g["Z"][0], g["Z"][1]
            pool.op("tensor_tensor", [g["Zc"], bdm], [C], out=C[0:P, 0:P], in0=g["Zc"][0:P, 0:P], in1=bdm[0:P, li, 0:P], op=ALU.mult)
            pool.op("tensor_tensor", [g["NT"], bdm], [CT], out=CT[0:P, 0:P], in0=g["NT"][0:P, 0:P], in1=bdm[0:P, li, 0:P], op=ALU.mult)
            mm(ps_g, ps_g[0:P, 128:128 + P], CT[0:P, 0:P], W[0:P, 0:P], [CT, W])
            mm(ps_g, ps_g[0:P, 256:256 + P], C[0:P, 0:P], WT[0:P, 0:P], [C, WT])
            act.op("activation", [ps_g], [T1], out=T1[0:P, 0:P], in_=ps_g[0:P, 128:128 + P], func=AF.Copy)
            act.op("activation", [ps_g], [T1p], out=T1p[0:P, 0:P], in_=ps_g[0:P, 256:256 + P], func=AF.Copy)
            mm(ps_h, ps_h[0:P, 0:P], WT[0:P, 0:P], T1[0:P, 0:P], [WT, T1])
            mm(ps_h, ps_h[0:P, 128:128 + P], W[0:P, 0:P], T1p[0:P, 0:P], [W, T1p])
            dve.op("tensor_tensor", [W, ps_h], [W], out=W[0:P, 0:P], in0=W[0:P, 0:P], in1=ps_h[0:P, 0:P], op=ALU.subtract)
            dve.op("tensor_tensor", [WT, ps_h], [WT], out=WT[0:P, 0:P], in0=WT[0:P, 0:P], in1=ps_h[0:P, 128:128 + P], op=ALU.subtract)
            s_ *= 2
            li += 1
        mm(ps_h, ps_h[0:P, 0:256], WT[0:P, 0:P], g["Xb"][0:P, :], [WT, g["Xb"]])
        dve.op("tensor_copy", [ps_h], [g["Xf"]], out=g["Xf"][0:P, :], in_=ps_h[0:P, 0:256])
        dve.op("tensor_scalar", [g["Xf"], g["beta"]], [g["u"]], out=g["u"][0:P, :], in0=g["Xf"][0:P, 0:128], scalar1=g["beta"][0:P, 0:1],
               scalar2=None, op0=ALU.mult)
        dve.op("tensor_scalar", [g["Xf"], g["beta"]], [g["wb"]], out=g["wb"][0:P, :], in0=g["Xf"][0:P, 128:256], scalar1=g["beta"][0:P, 0:1],
               scalar2=None, op0=ALU.mult)
        tr(ps_tg, ps_t[:, 7, 0:P], g["wb"][0:P, :], [g["wb"], identb], identb[0:P, 0:P])
        act.op("activation", [ps_tg], [g["wT"]], out=g["wT"][:, 0:P], in_=ps_t[:, 7, 0:P], func=AF.Copy)
        pool.op("tensor_tensor", [g["QTn"], g["EG"]], [g["QeT"]], out=g["QeT"][:, 0:P], in0=g["QTn"][:, 0:P], in1=g["EG"][:, 0:P], op=ALU.mult)
        dve.op("tensor_scalar", [g["Ksb"], g["E1"]], [g["Kd"]], out=g["Kd"][0:P, :], in0=g["Ksb"][0:P, :], scalar1=g["E1"][0:P, P - 1:P],
               scalar2=None, op0=ALU.mult)
        mm(ps_g, ps_g[0:P, 0:128], g["wT"][:, 0:P], g["Sb"][:, :], [g["wT"], g["Sb"]])
        dve.op("tensor_tensor", [g["u"], ps_g], [g["vnb"]], out=g["vnb"][0:P, :], in0=g["u"][0:P, :], in1=ps_g[0:P, 0:128], op=ALU.subtract)
        mm(ps_g, ps_g[0:P, 128:256], g["QeT"][:, 0:P], g["Sb"][:, :], [g["QeT"], g["Sb"]], start=True, stop=False)
        mm(ps_g, ps_g[0:P, 128:256], g["attnT"][0:P, 0:P], g["vnb"][0:P, :], [g["attnT"], g["vnb"]], start=False, stop=True)
        mm(ps_h, ps_h[:, 256:384], g["Kd"][0:P, :], g["vnb"][0:P, :], [g["Kd"], g["vnb"]])
        dve.op("scalar_tensor_tensor", [g["S"], g["EG"], ps_h], [g["S"]], out=g["S"][:, :], in0=g["S"][:, :], scalar=g["EG"][:, P - 1:P],
               in1=ps_h[:, 256:384], op0=ALU.mult, op1=ALU.add)
        act.op("activation", [g["S"]], [g["Sb"]], out=g["Sb"][:, :], in_=g["S"][:, :], func=AF.Copy)
        act.op("activation", [ps_g], [g["osb"], g["ssq"]], out=g["osb"][0:P, :], in_=ps_g[0:P, 128:256], func=AF.Square, accum_out=g["ssq"][0:P, :])
        dve.op("tensor_scalar", [g["ssq"]], [g["ssq"]], out=g["ssq"][0:P, :], in0=g["ssq"][0:P, :], scalar1=1.0 / 128.0, scalar2=RMS_EPS,
               op0=ALU.mult, op1=ALU.add)
        act.op("activation", [g["ssq"]], [g["ssq"]], out=g["ssq"][0:P, :], in_=g["ssq"][0:P, :], func=AF.Sqrt)
        dve.op("reciprocal", [g["ssq"]], [g["rn"]], out=g["rn"][0:P, :], in_=g["ssq"][0:P, :])
        dve.op("scalar_tensor_tensor", [ps_g, g["rn"], dnwt], [g["od"]], out=g["od"][0:P, :], in0=ps_g[0:P, 128:256], scalar=g["rn"][0:P, 0:1],
               in1=dnwt[0:P, :], op0=ALU.mult, op1=ALU.mult)

    def load_pass_weights(l, hg):
        for k in range(8):
            sp.dma(wstg[:], w_in[l, hg, k * 128:(k + 1) * 128, :], [w_in], [wstg], wstg)
            act.op("activation", [wstg], [wA], out=wA[:, k, :], in_=wstg[:, 0:512], func=AF.Copy)
            dve.op("tensor_copy", [wstg], [wZ], out=wZ[:, k, :], in_=wstg[:, 512:642])
            pool.op("tensor_copy", [wstg], [wD], out=wD[:, k, :], in_=wstg[:, 642:1026])
        sp.dma(cw[:], conv_w[l, hg, :, :], [conv_w], [cw], cw)
        sp.dma(aneg[:], alog[l, hg, :, :], [alog], [aneg], aneg)
        sp.dma(dtbt[:], dtb[l, hg, :, :], [dtb], [dtbt], dtbt)
        act.op("activation", [aneg], [aneg], out=aneg[:], in_=aneg[:], func=AF.Exp)
        dve.op("tensor_scalar", [aneg], [aneg], out=aneg[:], in0=aneg[:], scalar1=-1.0, scalar2=None, op0=ALU.mult)

    sp.dma(xst[:], xs[:, :], [xs], [xst], xst)

    for l in range(DEPTH):
        compute_mod(l, cTp, mod_p)
        compute_mod(l, cTs, mod_s)
        sp.dma(dnwt[:], dnw[l, :, :], [dnw], [dnwt], dnwt)
        sp.dma(lng[:], ln_g.t[l:l + 1, :].to_broadcast([128, D]), [ln_g], [lng], lng)
        sp.dma(lnb[:], ln_b.t[l:l + 1, :].to_broadcast([128, D]), [ln_b], [lnb], lnb)
        for k in range(8):
            sp.dma(wostg[:, 0:D], w_out[l, k * 128:(k + 1) * 128, :], [w_out], [wostg], wostg)
            act.op("activation", [wostg], [wo], out=wo[:, k, :], in_=wostg[:, 0:D], func=AF.Copy)
        xsrc = xp if l == 0 else x1

        for hg in range(4):
            load_pass_weights(l, hg)
            pZ = pZs[0]; xhist = xhists[0]; zdsil = zdsils[0]
            front(l, hg, xst, mod_s, coss, sins, pZ)
            sp.dma(nks[l, :, hg * 128:(hg + 1) * 128], qkr[:, 128:256], [qkr], [nks], qkr)
            sp.dma(nvs[l, :, hg * 128:(hg + 1) * 128], pA[:, 256:384], [pA], [nvs], pA)
            qT_and_gate(0, False)
            act.op("activation", [pA], [zsil], out=zsil[:, 0, :], in_=pA[:, 384:512], func=AF.Silu)
            act.op("activation", [pZ], [zdsil], out=zdsil[:], in_=pZ[:, 0:128], func=AF.Silu)
            dve.op("tensor_copy", [ps_a], [xhist], out=xhist[:, :, 3:131], in_=ps_a[:, 0:384].rearrange("p (c t) -> p c t", c=3))
            for h in range(2):
                tr(ps_t, ps_t[0:64, h, :], kvb[:, h * 64:(h + 1) * 64], [kvb, identb], identb[:])
            for h in range(2):
                act.op("activation", [ps_t], [knT[h]], out=knT[h][:], in_=ps_t[0:64, h, :], func=AF.Copy)
            knew = knT
            for j in range(NBS):
                r0 = 4 * j
                sp.dma(ptb[:], ptab.t[j:j + 1, :].to_broadcast([128, NPG]), [ptab], [ptb], ptb)
                dve.op("tensor_scalar", [ptb, iota4], [idxb], out=idxb[:], in0=ptb[:], scalar1=512.0, scalar2=iota4[:, 0:1], op0=ALU.mult, op1=ALU.add)
                dve.op("tensor_scalar", [idxb], [idx], out=idx[:], in0=idxb[:], scalar1=float(l * cfg.NPOOL * 512 + hg), scalar2=None, op0=ALU.add)
                for pg in range(NPG):
                    bk_ = pgk[pg % 2]
                    pgb = pgbs[pg % 2]
                    ptb_, so = (ps_t, 2) if pg % 2 == 0 else (ps_tg, 4)
                    pool.idma(bk_[:], cache_kv[:, :], idx[:, pg:pg + 1], [cache_kv, idx], [bk_], bk_)
                    act.op("activation", [bk_], [pgb], out=pgb[:, 0:128], in_=bk_[:, 0:128], func=AF.Copy)
                    for h in range(2):
                        tr(ptb_, ps_t[0:64, so + h, :], pgb[:, h * 64:(h + 1) * 64], [pgb, identb], identb[:])
                    for h in range(2):
                        act.op("activation", [ptb_], [KT[h]], out=KT[h][0:64, pg * 128:(pg + 1) * 128], in_=ps_t[0:64, so + h, :], func=AF.Copy)
                    dve.op("tensor_copy", [bk_], [VA], out=VA[:, pg, :, 0:64], in_=bk_[:, 128:256].rearrange("p (h f) -> p h f", h=2))
                    kmean_update(pg, pgb, lambda h: pgb[:, h * 64:(h + 1) * 64])
                for h in range(2):
                    dve.op("tensor_copy", [knew[h]], [KT[h]], out=KT[h][0:64, NK:NK + 4], in_=knew[h][0:64, r0:r0 + 4])
                mm(ps_b, ps_b[0:4, 0:128], identb[:, r0:r0 + 4], kvb[:, 128:256], [identb, kvb])
                dve.op("tensor_copy", [ps_b], [VA], out=VA[0:4, NKT, :, 0:64], in_=ps_b[0:4, 0:128].rearrange("p (h f) -> p h f", h=2))
                nb = NPG // 2
                for h in range(2):
                    mm(ps_b, ps_b[:, 256 + h * 32:256 + h * 32 + nb], qTt[h][:], kmT[h][:, 0:nb], [qTt[h], kmT[h]])
                dve.op("memset", [], [qa], qa[:, :, 64:96], 0.0)
                if nb > 3:
                    dve.op("memset", [], [gate_sb], gate_sb[:], -1e30)
                    dve.op("tensor_copy", [ps_b], [gate_sb], out=gate_sb[:, :, 0:nb],
                           in_=ps_b[:, 256:320].rearrange("p (h f) -> p h f", h=2)[:, :, 0:nb])
                    for h in range(2):
                        dve.op("max", [gate_sb], [top8], out=top8[:, h, :], in_=gate_sb[:, h, :])
                        dve.op("tensor_scalar", [gate_sb, top8], [qa], out=qa[:, h, 64:64 + nb], in0=gate_sb[:, h, 0:nb],
                               scalar1=top8[:, h, 2:3], scalar2=NEG, op0=ALU.is_lt, op1=ALU.mult)
                for h in range(2):
                    qaug_T(h, QT[h], QT[h][:, 0:128])
                    ktl = [(KT[h][:, kt * 128:(kt + 1) * 128], 128, VA[:, kt, h, :], None) for kt in range(NPG)]
                    ktl.append((KT[h][:, NK:NK + 4], 4, VA[0:4, NKT, h, :], stair[0:4, 0, 0:4]))
                    attention(h, QT[h], QT[h][:, r0:r0 + 4], 4, ktl, None)
                    dve.op("tensor_copy", [ps_o], [OTs], out=OTs[:, 0:4], in_=ps_o[0:65, 0:4])
                    tr(ps_b, ps_b[0:4, 128:193], OTs[0:65, 0:4], [OTs, identf], identf[0:65, 0:65])
                    dve.op("reciprocal", [ps_b], [rden], out=rden[0:4, :], in_=ps_b[0:4, 192:193])
                    dve.op("tensor_scalar", [ps_b, rden], [oas], out=oas[0:4, h * 64:(h + 1) * 64], in0=ps_b[0:4, 128:192], scalar1=rden[0:4, 0:1],
                           scalar2=None, op0=ALU.mult)
                sp.dma(xhs[:, :, 0:3], sconv[l, j, hg, :, :].rearrange("p (c t) -> p c t", c=3), [sconv], [xhs], xhs)
                dve.op("tensor_copy", [xhist], [xhs], out=xhs[:, :, 3:7], in_=xhist[:, :, 3 + r0:3 + r0 + 4])
                sp.dma(G["S"][:], ssm0[l, j, hg, :, :], [ssm0], [G["S"]], G["S"])
                act.op("activation", [G["S"]], [G["Sb"]], out=G["Sb"][:], in_=G["S"][:], func=AF.Copy)
                gdn_chunk(4, r0, xhs, xhs[:, :, 0:7], False, pZ)
                sp.dma(ssms[l, j, hg, :, :], G["S"][:], [G["S"]], [ssms], G["S"])
                sp.dma(convs[l, j, hg, :, :].rearrange("p (c t) -> p c t", c=3), xhs[:, :, 4:7], [xhs], [convs], xhs)
                act.op("activation", [G["od"]], [oas], out=oas[0:4, 128:256], in_=G["od"][0:4, :], func=AF.Copy)
                mm(ps_a, ps_a[:, 0:256], esel[0:4, j, :], oas[0:4, :], [esel, oas], start=(j == 0), stop=(j == NBS - 1))
            dve.op("tensor_tensor", [ps_a, zsil], [mixs], out=mixs[:, hg * 256:hg * 256 + 128], in0=ps_a[:, 0:128], in1=zsil[:, 0, :], op=ALU.mult)
            dve.op("tensor_tensor", [ps_a, zdsil], [mixs], out=mixs[:, hg * 256 + 128:hg * 256 + 256], in0=ps_a[:, 128:256], in1=zdsil[:], op=ALU.mult)

            dve.op("memset", [], [G["S"]], G["S"][:], 0.0)
            dve.op("memset", [], [G["Sb"]], G["Sb"][:], 0.0)
            dve.op("memset", [], [xhists[1]], xhists[1][:, :, 128:131], 0.0)
            dve.op("memset", [], [gate_sb], gate_sb[:], -1e30)

            def stream_b(i):
                p = i % 2
                c = i % 4
                pZ, xh, zd = pZs[p], xhists[p], zdsils[p]
                sp.dma(xt[:], xsrc[i * 128:(i + 1) * 128, :], [xsrc], [xt], xt)
                sp.dma(cost[:], c_cosp[i * 128:(i + 1) * 128, :], [c_cosp], [cost], cost)
                sp.dma(sint[:], c_sinp[i * 128:(i + 1) * 128, :], [c_sinp], [sint], sint)
                front(l, hg, xt, mod_p, cost, sint, pZ)
                sp.dma(nkp[l, i * 128:(i + 1) * 128, hg * 128:(hg + 1) * 128], qkr[:, 128:256], [qkr], [nkp], qkr)
                sp.dma(nvp[l, i * 128:(i + 1) * 128, hg * 128:(hg + 1) * 128], pA[:, 256:384], [pA], [nvp], pA)
                act.op("activation", [pA], [zsil], out=zsil[:, c, :], in_=pA[:, 384:512], func=AF.Silu)
                act.op("activation", [pZ], [zd], out=zd[:], in_=pZ[:, 0:128], func=AF.Silu)
                dve.op("tensor_copy", [ps_a], [xh], out=xh[:, :, 3:131], in_=ps_a[:, 0:384].rearrange("p (c t) -> p c t", c=3))
                dve.op("tensor_copy", [xhists[1 - p]], [xh], out=xh[:, :, 0:3], in_=xhists[1 - p][:, :, 128:131])
                store_kv_tile(i)
                qT_and_gate(i // 2, True)
                kmean_update(i, kvb, lambda h: kvb[:, h * 64:(h + 1) * 64])
                for h in range(2):
                    qaug_T(h, QT[h], QT[h][:, c * 128:(c + 1) * 128])

            def stream_b_attn(i):
                c = i % 4
                if c == 3 or i == NT - 1:
                    g0 = (i // 4) * 4
                    ng = i - g0 + 1
                    NQ = ng * 128
                    for h in range(2):
                        ktl = []
                        for kt in range(0, i + 1):
                            mask = stair[:, kt - g0, 0:NQ] if kt >= g0 else None
                            ktl.append((KT[h][:, kt * 128:(kt + 1) * 128], 128, VA[:, kt, h, :], mask))
                        attention(h, QT[h], QT[h][:, 0:NQ], NQ, ktl, None)
                        dve.op("tensor_copy", [ps_o], [OTs], out=OTs[:, 0:NQ], in_=ps_o[0:65, 0:NQ])
                        for cc in range(ng):
                            tr(ps_b, ps_b[:, 0:65], OTs[0:65, cc * 128:(cc + 1) * 128], [OTs, identf], identf[0:65, 0:65])
                            dve.op("reciprocal", [ps_b], [rden], out=rden[:], in_=ps_b[:, 64:65])
                            dve.op("scalar_tensor_tensor", [ps_b, rden, zsil], [mix], out=mix[:, h * 64:(h + 1) * 64], in0=ps_b[:, 0:64],
                                   scalar=rden[:, 0:1], in1=zsil[:, cc, h * 64:(h + 1) * 64], op0=ALU.mult, op1=ALU.mult)
                            ti = g0 + cc
                            sp.dma(mixp[ti * 128:(ti + 1) * 128, hg * 256 + h * 64:hg * 256 + (h + 1) * 64], mix[:, h * 64:(h + 1) * 64],
                                   [mix], [mixp], mix)

            def stream_a(i):
                p = i % 2
                pZ, xh, zd = pZs[p], xhists[p], zdsils[p]
                gdn_chunk(128, 0, xh, xh[:, :, :], True, pZ)
                dve.op("tensor_tensor", [G["od"], zd], [mixd], out=mixd[:, :], in0=G["od"][:, :], in1=zd[:], op=ALU.mult)
                sp.dma(mixp[i * 128:(i + 1) * 128, hg * 256 + 128:hg * 256 + 256], mixd[:, :], [mixd], [mixp], mixd)

            stream_b(0)
            stream_b_attn(0)
            MERGE_ATTN = bool(os.environ.get("K_MERGE_ATTN"))
            KSTOP = int(os.environ.get("K_STOP", "100000"))
            for i in range(NT):
                if i >= KSTOP:
                    break
                A = kb.record(lambda: stream_a(i))
                if i + 1 < NT:
                    B = kb.record(lambda: (stream_b(i + 1), stream_b_attn(i + 1) if MERGE_ATTN else None))
                else:
                    B = []
                kb.merge_emit(A, B)
                if i + 1 < NT and not MERGE_ATTN:
                    stream_b_attn(i + 1)
            xhist = xhists[(NT - 1) % 2]
            dve.op("tensor_copy", [xhist], [xhs], out=xhs[:, :, 0:3], in_=xhist[:, :, 128:131])
            sp.dma(ssmp[l, hg, :, :], G["S"][:], [G["S"]], [ssmp], G["S"])
            sp.dma(convp[l, hg, :, :].rearrange("p (c t) -> p c t", c=3), xhs[:, :, 0:3], [xhs], [convp], xhs)

        if cfg.DEBUG:
            sp.dma(dbg_mixs[l, :, :], mixs[:], [mixs], [dbg_mixs], mixs)
            out_bufs.append(dbg_mixs)

        def phase2(mix_buf, mix_ap, xtile, mod, out_buf, out_ap, keep=None):
            for rnd in range(2):
                for k in range(4):
                    tr(ps_t, ps_t[:, k, :], mix_ap[:, (rnd * 4 + k) * 128:(rnd * 4 + k + 1) * 128], [mix_buf, identb], identb[:])
                act.op("activation", [ps_t], [mT], out=mT[:, rnd * 4:rnd * 4 + 4, :], in_=ps_t[:, 0:4, :], func=AF.Copy)
            for half, pb_ in ((0, ps_a), (1, ps_b)):
                for k in range(8):
                    mm(pb_, pb_[:, :], mT[:, k, :], wo[:, k, half * 512:(half + 1) * 512], [mT, wo], start=(k == 0), stop=(k == 7))
                dve.op("tensor_tensor", [pb_, mod], [rt], out=rt[:, half * 512:(half + 1) * 512], in0=pb_[:, :],
                       in1=mod[:, 2 * D + half * 512:2 * D + (half + 1) * 512], op=ALU.mult)
            dve.op("scalar_tensor_tensor", [xtile, rt], [rt], out=rt[:], in0=xtile[:], scalar=cfg.ALPHA, in1=rt[:], op0=ALU.mult, op1=ALU.add)
            for half in range(2):
                dve.op("bn_stats", [rt], [stats], out=stats[:, half, :], in_=rt[:, half * 512:(half + 1) * 512])
            dve.op("bn_aggr", [stats], [mv], out=mv[:], in_=stats[:].rearrange("p a b -> p (a b)"))
            dve.op("tensor_scalar", [mv], [rstd], out=rstd[:], in0=mv[:, 1:2], scalar1=LN_EPS, scalar2=None, op0=ALU.add)
            act.op("activation", [rstd], [rstd], out=rstd[:], in_=rstd[:], func=AF.Sqrt)
            dve.op("reciprocal", [rstd], [rstd], out=rstd[:], in_=rstd[:])
            dve.op("tensor_scalar", [rt, mv, rstd], [rt], out=rt[:], in0=rt[:], scalar1=mv[:, 0:1], scalar2=rstd[:, 0:1], op0=ALU.subtract, op1=ALU.mult)
            pool.op("tensor_tensor", [rt, lng], [rt], out=rt[:], in0=rt[:], in1=lng[:], op=ALU.mult)
            tgt = keep if keep is not None else rt
            dve.op("tensor_tensor", [rt, lnb], [tgt], out=tgt[:], in0=rt[:], in1=lnb[:], op=ALU.add)
            if out_buf is not None:
                sp.dma(out_ap, tgt[:], [tgt], [out_buf], tgt)

        for i in range(NT):
            sp.dma(xt[:], xsrc[i * 128:(i + 1) * 128, :], [xsrc], [xt], xt)
            sp.dma(mixl[:], mixp[i * 128:(i + 1) * 128, :], [mixp], [mixl], mixl)
            if l == DEPTH - 1:
                phase2(mixl, mixl, xt, mod_p, yp, yp[i * 128:(i + 1) * 128, :])
            else:
                phase2(mixl, mixl, xt, mod_p, x1, x1[i * 128:(i + 1) * 128, :])
        if l == DEPTH - 1:
            phase2(mixs, mixs, xst, mod_s, ys, ys[:, :])
        else:
            phase2(mixs, mixs, xst, mod_s, None, None, keep=xst)

    for e in (sp, pool, act, dve, pe):
        e.wait_all(out_bufs)
    return kb


_CACHE = {}


def _consts(cfg):
    SEQ, NK = cfg.SEQ, cfg.NK
    bf = ml_dtypes.bfloat16
    c = {}
    c["c_identf"] = np.eye(128, dtype=np.float32)
    c["c_identb"] = np.eye(128, dtype=np.float32).astype(bf)
    t = np.arange(128)
    tri = np.zeros((128, 2, 128), np.float32)
    tri[:, 0, :] = (t[:, None] <= t[None, :])
    tri[:, 1, :] = (t[:, None] < t[None, :])
    c["c_tri"] = tri
    st = np.zeros((128, 4, 512), np.float32)
    for r in range(4):
        for cc in range(4):
            blk = st[:, r, cc * 128:(cc + 1) * 128]
            if cc < r:
                blk[:] = NEG
            elif cc == r:
                blk[:] = np.where(t[:, None] <= t[None, :], 0.0, NEG)
    c["c_stair"] = st.astype(bf)
    e = np.zeros((32, NK + 128), np.float32)
    pos = np.arange(NK)
    for j in range(32):
        e[j, :NK] = (pos // 256 == j)
    c["c_eblk"] = e.astype(bf)
    half = 32
    inv = (10000.0 ** (-np.arange(half, dtype=np.float32) / half)).astype(np.float32)

    def tab(pos):
        ang = pos.astype(np.float32)[:, None] * inv[None, :]
        return np.tile(np.cos(ang).astype(np.float32), (1, 4)), np.tile(np.sin(ang).astype(np.float32), (1, 4))
    c["c_cosp"], c["c_sinp"] = tab(np.arange(SEQ))
    ps = cfg.PAST + (np.arange(128) % 4)
    cs, sn = tab(ps)
    c["c_coss"], c["c_sins"] = cs, sn
    c["c_iota4"] = (4 * np.arange(128, dtype=np.float32)).reshape(128, 1)
    es = np.zeros((4, cfg.NBS, 128), np.float32)
    for j in range(cfg.NBS):
        for tt in range(4):
            es[tt, j, 4 * j + tt] = 1.0
    c["c_esel"] = es.astype(bf)
    bd = np.zeros((128, 7, 128), np.float32)
    prev = None
    for li, sz in enumerate([2, 4, 8, 16, 32, 64, 128]):
        m_ = (t[:, None] // sz == t[None, :] // sz).astype(np.float32)
        bd[:, li, :] = m_ if prev is None else m_ - prev
        prev = m_
    c["c_bdm"] = bd.astype(bf)
    return c


def _run(cfg, x_prompt, x_sample, c_prompt, c_sample, cache_k, cache_v, page_table,
         state_ssm, state_conv, w_ada, b_ada, w_in, conv_w, a_log, dt_bias,
         dn_norm_w, w_out, ln_g, ln_b):
    f = np.float32
    key = (cfg.SEQ, cfg.PAST, cfg.DEPTH, cfg.DECB)
    if key not in _CACHE:
        _CACHE[key] = build(cfg)
    kb = _CACHE[key]
    DEPTH = cfg.DEPTH
    A, Dq = 512, 1536
    w_in = np.asarray(w_in, f)
    win_r = np.zeros((DEPTH, 4, D, 1026), f)
    for hg in range(4):
        a0 = hg * 128
        cols = []
        for g in range(4):
            cols += list(range(g * 512 + a0, g * 512 + a0 + 128))
        cols += list(range(2048 + 1536 + a0, 2048 + 1536 + a0 + 128))
        cols += [2048 + 1536 + 512 + hg, 2048 + 1536 + 512 + 4 + hg]
        for g in range(3):
            cols += list(range(2048 + g * 512 + a0, 2048 + g * 512 + a0 + 128))
        win_r[:, hg] = w_in[:, :, cols]
    conv_w = np.asarray(conv_w, f)
    cw_r = np.zeros((DEPTH, 4, 128, 12), f)
    for hg in range(4):
        for g in range(3):
            blk = conv_w[:, :, g * 512 + hg * 128: g * 512 + (hg + 1) * 128]
            cw_r[:, hg, :, g * 4:(g + 1) * 4] = np.transpose(blk, (0, 2, 1))
    alog_r = np.broadcast_to(np.asarray(a_log, f)[:, :, None, None], (DEPTH, 4, 128, 1)).copy()
    dtb_r = np.broadcast_to(np.asarray(dt_bias, f)[:, :, None, None], (DEPTH, 4, 128, 1)).copy()
    dnw_r = np.broadcast_to(np.asarray(dn_norm_w, f)[:, None, :], (DEPTH, 128, 128)).copy()
    rows = []
    for hg in range(4):
        rows += list(range(hg * 128, (hg + 1) * 128)) + list(range(512 + hg * 128, 512 + (hg + 1) * 128))
    wout_r = np.ascontiguousarray(np.asarray(w_out, f)[:, rows, :])
    ckv = np.empty((np.asarray(cache_k).size // 128, 256), f)
    ckv[:, 0:128] = np.asarray(cache_k, f).reshape(-1, 128)
    ckv[:, 128:256] = np.asarray(cache_v, f).reshape(-1, 128)
    consts = _consts(cfg)
    state_conv = np.asarray(state_conv, f)
    state_ssm = np.asarray(state_ssm, f)
    in_maps = []
    NC_, NBS = cfg.NCORES, cfg.NBS
    cpb = NC_ // cfg.BATCH
    for c in range(NC_):
        b = c // cpb
        m = dict(consts)
        m["xp"] = np.ascontiguousarray(np.asarray(x_prompt, f)[b])
        xs_ = np.zeros((128, D), f)
        cTs = np.zeros((128, 8, 128), f)
        sc = np.zeros((DEPTH, NBS, 4, 128, 9), f)
        for j in range(NBS):
            sb_ = NBS * c + j
            xs_[4 * j:4 * j + 4] = np.asarray(x_sample, f)[sb_]
            cTs[:, :, 4 * j:4 * j + 4] = np.asarray(c_sample, f)[sb_].reshape(8, 128).T[:, :, None]
            for hg in range(4):
                for g in range(3):
                    blk = state_conv[:, sb_, :, g * 512 + hg * 128:g * 512 + (hg + 1) * 128]
                    sc[:, j, hg, :, g * 3:(g + 1) * 3] = np.transpose(blk, (0, 2, 1))
        m["xs"] = xs_
        m["cTs"] = cTs
        m["cTp"] = np.broadcast_to(np.asarray(c_prompt, f)[b].reshape(8, 128).T[:, :, None], (128, 8, 128)).copy()
        m["w_ada"] = np.asarray(w_ada, f); m["b_ada"] = np.asarray(b_ada, f)
        m["w_in"] = win_r; m["conv_w"] = cw_r; m["alog"] = alog_r; m["dtb"] = dtb_r; m["dnw"] = dnw_r
        m["w_out"] = wout_r; m["ln_g"] = np.asarray(ln_g, f); m["ln_b"] = np.asarray(ln_b, f)
        m["cache_kv"] = ckv
        m["ptab"] = np.ascontiguousarray(np.asarray(page_table, np.int32)[NBS * c:NBS * c + NBS])
        m["ssm0"] = np.ascontiguousarray(state_ssm[:, NBS * c:NBS * c + NBS])
        m["sconv"] = sc
        in_maps.append(m)
    res = run_bass_kernel_spmd(kb.nc, in_maps, core_ids=list(range(NC_)))
    R = res.results
    B, SEQ, DECB = cfg.BATCH, cfg.SEQ, cfg.DECB
    y_prompt = np.stack([R[cpb * b]["yp"] for b in range(B)]).astype(f)
    y_sample = np.zeros((DECB, 4, D), f)
    nk_p = np.zeros((DEPTH, B, SEQ, 8, 64), f); nv_p = np.zeros_like(nk_p)
    ssm_p = np.zeros((DEPTH, B, 4, 128, 128), f); conv_p = np.zeros((DEPTH, B, 3, 1536), f)
    nk_s = np.zeros((DEPTH, DECB, 4, 8, 64), f); nv_s = np.zeros_like(nk_s)
    ssm_s = np.zeros((DEPTH, DECB, 4, 128, 128), f); conv_s = np.zeros((DEPTH, DECB, 3, 1536), f)
    for b in range(B):
        r = R[cpb * b]
        nk_p[:, b] = r["nkp"].reshape(DEPTH, SEQ, 8, 64)
        nv_p[:, b] = r["nvp"].reshape(DEPTH, SEQ, 8, 64)
        ssm_p[:, b] = r["ssmp"]
        cp = r["convp"].reshape(DEPTH, 4, 128, 3, 3)
        for hg in range(4):
            for g in range(3):
                conv_p[:, b, :, g * 512 + hg * 128:g * 512 + (hg + 1) * 128] = np.transpose(cp[:, hg, :, g, :], (0, 2, 1))
    for c in range(NC_):
        r = R[c]
        for j in range(NBS):
            sb_ = NBS * c + j
            y_sample[sb_] = r["ys"][4 * j:4 * j + 4]
            nk_s[:, sb_] = r["nks"][:, 4 * j:4 * j + 4].reshape(DEPTH, 4, 8, 64)
            nv_s[:, sb_] = r["nvs"][:, 4 * j:4 * j + 4].reshape(DEPTH, 4, 8, 64)
            ssm_s[:, sb_] = r["ssms"][:, j]
            cs = r["convs"][:, j].reshape(DEPTH, 4, 128, 3, 3)
            for hg in range(4):
                for g in range(3):
                    conv_s[:, sb_, :, g * 512 + hg * 128:g * 512 + (hg + 1) * 128] = np.transpose(cs[:, hg, :, g, :], (0, 2, 1))
    return (y_prompt, y_sample, nk_p, nv_p, ssm_p, conv_p, nk_s, nv_s, ssm_s, conv_s)


def kernel(**inputs):
    cfg = Cfg()
    return _run(cfg, **inputs)
```

```python
import math
import numpy as np
import ml_dtypes
import concourse.bass as bass
import concourse.mybir as mybir
from concourse.bass_utils import run_bass_kernel_spmd

F32 = mybir.dt.float32
BF16 = mybir.dt.bfloat16
I32 = mybir.dt.int32
AF = mybir.ActivationFunctionType
ALU = mybir.AluOpType

import os
ROLL = int(os.environ.get('K_ROLL', '30000'))
D = 1024
NEG = -30000.0
L2_EPS = 1e-6
RMS_EPS = 1e-6
LN_EPS = 1e-5


class Cfg:
    def __init__(self, seq=8192, past=8192, depth=2, dec_batch=32, batch=2, ncores=4):
        self.SEQ = seq
        self.PAST = past
        self.DEPTH = depth
        self.DECB = dec_batch
        self.BATCH = batch
        self.NT = seq // 128
        self.NPAGES = past // 128
        self.NPOOL = (dec_batch * self.NPAGES * 5) // 4
        self.NK = max(seq, past)
        self.NKT = self.NK // 128
        self.ALPHA = (2 * depth) ** 0.25
        self.DEBUG = False
        self.NCORES = ncores
        self.NBS = dec_batch // ncores


class Buf:
    __slots__ = ("t", "name", "w", "r", "dsem", "dcnt")

    def __init__(self, t, name):
        self.t = t
        self.name = name
        self.w = None
        self.r = {}
        self.dsem = None
        self.dcnt = 0

    def __getitem__(self, k):
        return self.t[k]


class Eng:
    def __init__(self, kb, eng, name, selfsync=True):
        self.kb = kb
        self.eng = eng
        self.name = name
        self.selfsync = selfsync
        self.sem = kb.nc.alloc_semaphore(f"p_{name}_0")
        self.nsem = 1
        self.cnt = 0
        self.waited = {}
        self.n_ins = 0

    def _wait(self, tok):
        if tok is None:
            return
        sem, val = tok
        if (not self.selfsync) and sem is self.sem:
            return
        k = id(sem)
        if self.waited.get(k, 0) >= val:
            return
        self.eng.wait_ge(sem, val)
        self.waited[k] = val
        self.n_ins += 1
        self.n_wait = getattr(self, "n_wait", 0) + 1

    def _pre(self, reads, writes):
        for b in reads:
            self._wait(b.w)
        for b in writes:
            self._wait(b.w)
            for t in b.r.values():
                self._wait(t)

    def _post(self, tok, reads, writes):
        sem, val = tok
        for b in reads:
            b.r[id(sem)] = tok
        for b in writes:
            b.w = tok
            b.r = {}

    def op(self, name, reads, writes, *a, **kw):
        if self.kb.rec is not None:
            self.kb.rec.append((self, "op", name, reads, writes, a, kw))
            return None
        self._pre(reads, writes)
        ins = getattr(self.eng, name)(*a, **kw)
        if self.cnt >= ROLL:
            self.kb.all_sems.append(self.sem)
            self.sem = self.kb.nc.alloc_semaphore(f"p_{self.name}_{self.nsem}")
            self.nsem += 1
            self.cnt = 0
        self.cnt += 1
        ins.then_inc(self.sem, 1)
        self.n_ins += 1
        tok = (self.sem, self.cnt)
        self._post(tok, reads, writes)
        return tok

    def _dsem(self, sb):
        if sb.dsem is None:
            sb.dsem = self.kb.nc.alloc_semaphore(f"d_{sb.name}")
        return sb.dsem

    def dma(self, out, in_, reads, writes, sb, **kw):
        if self.kb.rec is not None:
            self.kb.rec.append((self, "dma", None, reads, writes, (out, in_, sb), kw))
            return None
        self._pre(reads, writes)
        sem = self._dsem(sb)
        ins = self.eng.dma_start(out=out, in_=in_, **kw)
        sb.dcnt += 1
        ins.then_inc(sem, 16)
        self.n_ins += 1
        tok = (sem, 16 * sb.dcnt)
        self._post(tok, reads, writes)
        return tok

    def idma(self, out, in_, idx_ap, reads, writes, sb):
        self._pre(reads, writes)
        sem = self._dsem(sb)
        ins = self.eng.indirect_dma_start(
            out=out, out_offset=None, in_=in_,
            in_offset=bass.IndirectOffsetOnAxis(ap=idx_ap, axis=0))
        sb.dcnt += 1
        ins.then_inc(sem, 16)
        self.n_ins += 1
        tok = (sem, 16 * sb.dcnt)
        self._post(tok, reads, writes)
        return tok

    def wait_all(self, bufs):
        for b in bufs:
            self._wait(b.w)
            for t in b.r.values():
                self._wait(t)


class KB:
    def __init__(self):
        self.nc = bass.Bass("TRN2", target_bir_lowering=False)
        nc = self.nc
        self.pe = Eng(self, nc.tensor, "pe", selfsync=False)
        self.act = Eng(self, nc.scalar, "act")
        self.dve = Eng(self, nc.vector, "dve")
        self.pool = Eng(self, nc.gpsimd, "pool")
        self.sp = Eng(self, nc.sync, "sp")
        self.ext_in = {}
        self.ext_out = {}
        self.rec = None
        self.all_sems = []

    def record(self, fn):
        assert self.rec is None
        self.rec = []
        fn()
        r, self.rec = self.rec, None
        return r

    def emit(self, item):
        eng, kind, name, reads, writes, a, kw = item
        if kind == "op":
            eng.op(name, reads, writes, *a, **kw)
        else:
            out, in_, sb = a
            eng.dma(out, in_, reads, writes, sb, **kw)

    def merge_emit(self, A, B):
        import os
        if os.environ.get("K_NOMERGE"):
            for it in A:
                self.emit(it)
            for it in B:
                self.emit(it)
            return
        na, nb = len(A), len(B)
        ia = ib = 0
        while ia < na or ib < nb:
            if ib >= nb or (ia < na and ia * nb <= ib * na):
                self.emit(A[ia]); ia += 1
            else:
                self.emit(B[ib]); ib += 1

    def sb(self, name, shape, dt=F32):
        return Buf(self.nc.alloc_sbuf_tensor(name, list(shape), dt), name)

    def ps(self, name, shape, dt=F32):
        return Buf(self.nc.alloc_psum_tensor(name, list(shape), dt), name)

    def dram(self, name, shape, dt=F32, kind="Internal"):
        b = Buf(self.nc.dram_tensor(name, list(shape), dt, kind=kind).ap(), name)
        if kind == "ExternalInput":
            self.ext_in[name] = b
        elif kind == "ExternalOutput":
            self.ext_out[name] = b
        return b

    def alias(self, buf, name):
        return Buf(buf.t, name)


def build(cfg):
    kb = KB()
    pe, act, dve, pool, sp = kb.pe, kb.act, kb.dve, kb.pool, kb.sp
    SEQ, NT, NK, NKT, NPG, DEPTH = cfg.SEQ, cfg.NT, cfg.NK, cfg.NKT, cfg.NPAGES, cfg.DEPTH
    NPOOLROWS = DEPTH * cfg.NPOOL * 128 * 4

    def IN(name, shape, dt=F32):
        return kb.dram(name, shape, dt, kind="ExternalInput")

    def OUT(name, shape, dt=F32):
        return kb.dram(name, shape, dt, kind="ExternalOutput")

    xp = IN("xp", [SEQ, D]); xs = IN("xs", [128, D])
    cTp = IN("cTp", [128, 8, 128]); cTs = IN("cTs", [128, 8, 128])
    w_ada = IN("w_ada", [DEPTH, D, 3 * D]); b_ada = IN("b_ada", [DEPTH, 3 * D])
    w_in = IN("w_in", [DEPTH, 4, D, 1026])
    conv_w = IN("conv_w", [DEPTH, 4, 128, 12])
    alog = IN("alog", [DEPTH, 4, 128, 1]); dtb = IN("dtb", [DEPTH, 4, 128, 1])
    dnw = IN("dnw", [DEPTH, 128, 128])
    w_out = IN("w_out", [DEPTH, D, D])
    ln_g = IN("ln_g", [DEPTH, D]); ln_b = IN("ln_b", [DEPTH, D])
    cache_kv = IN("cache_kv", [NPOOLROWS, 256])
    NBS = cfg.NBS
    ptab = IN("ptab", [NBS, NPG], I32)
    ssm0 = IN("ssm0", [DEPTH, NBS, 4, 128, 128]); sconv = IN("sconv", [DEPTH, NBS, 4, 128, 9])
    c_identf = IN("c_identf", [128, 128]); c_identb = IN("c_identb", [128, 128], BF16)
    c_tri = IN("c_tri", [128, 2, 128])
    c_stair = IN("c_stair", [128, 4, 512], BF16)
    c_eblk = IN("c_eblk", [32, NK + 128], BF16)
    c_cosp = IN("c_cosp", [SEQ, 128]); c_sinp = IN("c_sinp", [SEQ, 128])
    c_coss = IN("c_coss", [128, 128]); c_sins = IN("c_sins", [128, 128])
    c_iota4 = IN("c_iota4", [128, 1])
    c_esel = IN("c_esel", [4, NBS, 128], BF16)
    c_bdm = IN("c_bdm", [128, 7, 128], BF16)

    yp = OUT("yp", [SEQ, D]); ys = OUT("ys", [128, D])
    nkp = OUT("nkp", [DEPTH, SEQ, 512]); nvp = OUT("nvp", [DEPTH, SEQ, 512])
    ssmp = OUT("ssmp", [DEPTH, 4, 128, 128]); convp = OUT("convp", [DEPTH, 4, 128, 9])
    nks = OUT("nks", [DEPTH, 128, 512]); nvs = OUT("nvs", [DEPTH, 128, 512])
    ssms = OUT("ssms", [DEPTH, NBS, 4, 128, 128]); convs = OUT("convs", [DEPTH, NBS, 4, 128, 9])
    out_bufs = [yp, ys, nkp, nvp, ssmp, convp, nks, nvs, ssms, convs]

    mixp = kb.dram("mixp", [SEQ, D], BF16, kind="ExternalOutput" if cfg.DEBUG else "Internal")
    dbg_mixs = kb.dram("dbg_mixs", [DEPTH, 128, D], BF16, kind="ExternalOutput") if cfg.DEBUG else None
    x1 = kb.dram("x1", [SEQ, D], F32)

    identf = kb.sb("identf", [128, 128]); identb = kb.sb("identb", [128, 128], BF16)
    tri = kb.sb("tri", [128, 2, 128])
    stair = kb.sb("stair", [128, 4, 512], BF16)
    iota4 = kb.sb("iota4", [128, 1]); esel = kb.sb("esel", [4, NBS, 128], BF16)
    ones_f = kb.sb("ones_f", [128, 128]); ones_b = kb.sb("ones_b", [128, 1], BF16)
    coss = kb.sb("coss", [128, 128]); sins = kb.sb("sins", [128, 128])
    for dst, src in ((identf, c_identf), (identb, c_identb), (iota4, c_iota4), (coss, c_coss), (sins, c_sins)):
        sp.dma(dst[:], src[:, :], [src], [dst], dst)
    sp.dma(tri[:], c_tri[:, :, :], [c_tri], [tri], tri)
    sp.dma(stair[:], c_stair[:, :, :], [c_stair], [stair], stair)
    sp.dma(esel[:], c_esel[:, :, :], [c_esel], [esel], esel)
    bdm = kb.sb("bdm", [128, 7, 128], BF16)
    sp.dma(bdm[:], c_bdm[:, :, :], [c_bdm], [bdm], bdm)
    dve.op("memset", [], [ones_f], ones_f[:], 1.0)
    dve.op("memset", [], [ones_b], ones_b[:], 1.0)

    KT = [kb.sb(f"KT{h}", [96, NK + 128], BF16) for h in range(2)]
    VA = kb.sb("VA", [128, NKT + 1, 2, 65], BF16)
    for h in range(2):
        sp.dma(KT[h][64:96, :], c_eblk[:, :], [c_eblk], [KT[h]], KT[h])
    dve.op("memset", [], [VA], VA[:], 1.0)
    kmT = [kb.sb(f"kmT{h}", [64, 32], BF16) for h in range(2)]
    ksum = kb.sb("ksum", [64, 2])
    QT = [kb.sb(f"QT{h}", [96, 512], BF16) for h in range(2)]
    mod_p = kb.sb("mod_p", [128, 3 * D]); mod_s = kb.sb("mod_s", [128, 3 * D])
    wA = kb.sb("wA", [128, 8, 512], BF16); wZ = kb.sb("wZ", [128, 8, 130], BF16); wD = kb.sb("wD", [128, 8, 384], BF16)
    wstg = kb.sb("wstg", [128, 1026])
    wo = kb.sb("wo", [128, 8, D], BF16); wostg = wstg
    cw = kb.sb("cw", [128, 12]); aneg = kb.sb("aneg", [128, 1]); dtbt = kb.sb("dtbt", [128, 1]); dnwt = kb.sb("dnwt", [128, 128])
    lng = kb.sb("lng", [128, D]); lnb = kb.sb("lnb", [128, D])
    xt = kb.sb("xt", [128, D]); xst = kb.sb("xst", [128, D])
    hb = kb.sb("hb", [128, D], BF16); hT = kb.sb("hT", [128, 8, 128], BF16)
    pA = kb.sb("pA", [128, 512]); pZs = [kb.sb(f"pZ{i}", [128, 130]) for i in range(2)]
    xhists = [kb.sb(f"xhist{i}", [128, 3, 131]) for i in range(2)]; xhs = kb.sb("xhs", [128, 3, 8])
    cost = kb.sb("cost", [128, 128]); sint = kb.sb("sint", [128, 128])
    rtmp = kb.sb("rtmp", [128, 4, 128]); qkr = kb.sb("qkr", [128, 256])
    kvb = kb.sb("kvb", [128, 256], BF16)
    qa = kb.sb("qa", [128, 2, 96], BF16)
    qTt = [kb.sb(f"qTt{h}", [64, 128], BF16) for h in range(2)]
    gate_sb = kb.sb("gate_sb", [128, 2, 32]); top8 = kb.sb("top8", [128, 2, 8])
    zsil = kb.sb("zsil", [128, 4, 128]); zdsils = [kb.sb(f"zdsil{i}", [128, 128]) for i in range(2)]
    PTb = [kb.sb(f"PTb{i}", [128, 512], BF16) for i in range(2)]
    OTs = kb.sb("OTs", [65, 512]); rden = kb.sb("rden", [128, 1])
    mix = kb.sb("mix", [128, 128], BF16); mixd = kb.sb("mixd", [128, 128], BF16); mixs = kb.sb("mixs", [128, D], BF16)
    oas = kb.sb("oas", [4, 256], BF16)
    mixl = kb.sb("mixl", [128, D], BF16); mT = kb.sb("mT", [128, 8, 128], BF16)
    rt = kb.sb("rt", [128, D]); stats = kb.sb("stats", [128, 2, 6]); mv = kb.sb("mv", [128, 2]); rstd = kb.sb("rstd", [128, 1])
    cst = kb.sb("cst", [128, 8, 128]); scb = kb.sb("scb", [128, 8, 128], BF16)
    wastg = kb.sb("wastg", [128, 8, 128]); wab = kb.sb("wab", [128, 8, 128], BF16); bab = kb.sb("bab", [128, 128])
    ptb = kb.sb("ptb", [128, NPG], I32); idxb = kb.sb("idxb", [128, NPG], I32); idx = kb.sb("idx", [128, NPG], I32)
    pgk = [kb.sb(f"pgk{i}", [128, 256]) for i in range(2)]
    pgbs = [kb.sb(f"pgb{i}", [128, 256], BF16) for i in range(2)]
    knT = [kb.sb(f"knT{h}", [64, 128], BF16) for h in range(2)]
    G = {}
    for nm in ["grep", "Dsb", "E1", "E1i", "E1s", "EG", "Xf", "Ksb", "Qsb", "Vsb", "u", "osb", "od", "S", "qkraw"]:
        G[nm] = kb.sb("g_" + nm, [128, 256] if nm in ("Xf", "qkraw") else [128, 128])
    for nm in ["NT", "attnT", "Zc", "Yn", "Zn", "Xb", "wb", "wT", "QTn", "KTn", "QeT", "Kd", "vnb", "Sb", "Kb", "Qb"]:
        G[nm] = kb.sb("g_" + nm, [128, 256] if nm == "Xb" else [128, 128], BF16)
    G["Y"] = [kb.sb(f"g_Y{i}", [128, 128], BF16) for i in range(2)]
    G["Z"] = [kb.sb(f"g_Z{i}", [128, 128], BF16) for i in range(2)]
    for nm in ["gcol", "egcol", "beta", "gv", "ssq", "rn", "kds", "sp_", "bq", "bk"]:
        G[nm] = kb.sb("g_" + nm, [128, 1])

    ps_t = kb.ps("ps_t", [128, 8, 128], BF16)
    ps_tg = kb.alias(ps_t, "ps_tg")
    ps_a = kb.ps("ps_a", [128, 512])
    ps_b = kb.ps("ps_b", [128, 512])
    ps_ss = [kb.ps(f"ps_s{i}", [128, 512]) for i in range(2)]
    ps_o = kb.ps("ps_o", [128, 512])
    ps_g = kb.ps("ps_g", [128, 512])
    ps_h = kb.ps("ps_h", [128, 512])

    def mm(out_buf, out_ap, lhsT, rhs, reads, start=True, stop=True):
        pe.op("matmul", reads, [out_buf], out_ap, lhsT=lhsT, rhs=rhs, start=start, stop=stop)

    def tr(out_buf, out_ap, in_ap, reads, ident_ap):
        pe.op("transpose", reads, [out_buf], out_ap, in_ap, ident_ap)

    def compute_mod(l, cT_src, mod):
        sp.dma(cst[:], cT_src[:, :, :], [cT_src], [cst], cst)
        act.op("activation", [cst], [scb], out=scb[:], in_=cst[:], func=AF.Silu)
        for cc in range(24):
            c0 = cc * 128
            sp.dma(wastg[:], w_ada[l, :, c0:c0 + 128].rearrange("(k p) c -> p k c", p=128), [w_ada], [wastg], wastg)
            sp.dma(bab[:], b_ada.t[l:l + 1, c0:c0 + 128].to_broadcast([128, 128]), [b_ada], [bab], bab)
            act.op("activation", [wastg], [wab], out=wab[:], in_=wastg[:], func=AF.Copy)
            for k in range(8):
                mm(ps_a, ps_a[:, 0:128], scb[:, k, :], wab[:, k, :], [scb, wab], start=(k == 0), stop=(k == 7))
            dve.op("tensor_tensor", [ps_a, bab], [mod], out=mod[:, c0:c0 + 128], in0=ps_a[:, 0:128], in1=bab[:], op=ALU.add)
        dve.op("tensor_scalar", [mod], [mod], out=mod[:, D:3 * D], in0=mod[:, D:3 * D], scalar1=1.0, scalar2=None, op0=ALU.add)

    def front(l, hg, xtile, mod, cos_t, sin_t, pZ):
        dve.op("tensor_tensor", [xtile, mod], [rt], out=rt[:], in0=xtile[:], in1=mod[:, D:2 * D], op=ALU.mult)
        dve.op("tensor_tensor", [rt, mod], [hb], out=hb[:], in0=rt[:], in1=mod[:, 0:D], op=ALU.add)
        for rnd in range(2):
            for k in range(4):
                tr(ps_t, ps_t[:, k, :], hb[:, (rnd * 4 + k) * 128:(rnd * 4 + k + 1) * 128], [hb, identb], identb[:])
            act.op("activation", [ps_t], [hT], out=hT[:, rnd * 4:rnd * 4 + 4, :], in_=ps_t[:, 0:4, :], func=AF.Copy)
        for k in range(8):
            mm(ps_a, ps_a[:, :], hT[:, k, :], wA[:, k, :], [hT, wA], start=(k == 0), stop=(k == 7))
        act.op("activation", [ps_a], [pA], out=pA[:], in_=ps_a[:], func=AF.Copy)
        for k in range(8):
            mm(ps_b, ps_b[:, 0:130], hT[:, k, :], wZ[:, k, :], [hT, wZ], start=(k == 0), stop=(k == 7))
        dve.op("tensor_copy", [ps_b], [pZ], out=pZ[:], in_=ps_b[:, 0:130])
        for c in range(3):
            for k in range(8):
                mm(ps_a, ps_a[:, c * 128:(c + 1) * 128], wD[:, k, c * 128:(c + 1) * 128], hT[:, k, :], [hT, wD],
                   start=(k == 0), stop=(k == 7))
        v4 = pA[:, 0:256].rearrange("p (g two f) -> p g two f", g=4, two=2)
        A_ = v4[:, :, 0, :]
        B_ = v4[:, :, 1, :]
        c4 = cos_t[:].rearrange("p (g f) -> p g f", g=4)
        s4 = sin_t[:].rearrange("p (g f) -> p g f", g=4)
        o4 = qkr[:].rearrange("p (g two f) -> p g two f", g=4, two=2)
        r4 = rtmp[:]
        dve.op("tensor_tensor", [pA, cos_t], [rtmp], out=r4[:, :, 0:32], in0=A_, in1=c4, op=ALU.mult)
        pool.op("tensor_tensor", [pA, sin_t], [rtmp], out=r4[:, :, 32:64], in0=B_, in1=s4, op=ALU.mult)
        dve.op("tensor_tensor", [rtmp], [qkr], out=o4[:, :, 0, :], in0=r4[:, :, 0:32], in1=r4[:, :, 32:64], op=ALU.subtract)
        pool.op("tensor_tensor", [pA, sin_t], [rtmp], out=r4[:, :, 64:96], in0=A_, in1=s4, op=ALU.mult)
        dve.op("tensor_tensor", [pA, cos_t], [rtmp], out=r4[:, :, 96:128], in0=B_, in1=c4, op=ALU.mult)
        dve.op("tensor_tensor", [rtmp], [qkr], out=o4[:, :, 1, :], in0=r4[:, :, 64:96], in1=r4[:, :, 96:128], op=ALU.add)
        act.op("activation", [qkr], [kvb], out=kvb[:, 0:128], in_=qkr[:, 128:256], func=AF.Copy)
        act.op("activation", [pA], [kvb], out=kvb[:, 128:256], in_=pA[:, 256:384], func=AF.Copy)
        act.op("activation", [qkr], [qa], out=qa[:, :, 0:64], in_=qkr[:, 0:128].rearrange("p (h f) -> p h f", h=2),
               func=AF.Copy, scale=0.125)

    def qT_and_gate(nblk_valid, topk):
        for h in range(2):
            tr(ps_t, ps_t[0:64, h, :], qa[:, h, 0:64], [qa, identb], identb[:])
        for h in range(2):
            act.op("activation", [ps_t], [qTt[h]], out=qTt[h][:], in_=ps_t[0:64, h, :], func=AF.Copy)
        dve.op("memset", [], [qa], qa[:, :, 64:96], 0.0)
        if topk and nblk_valid > 3:
            nb = nblk_valid
            for h in range(2):
                mm(ps_b, ps_b[:, 256 + h * 32:256 + h * 32 + nb], qTt[h][:], kmT[h][:, 0:nb], [qTt[h], kmT[h]])
            dve.op("tensor_copy", [ps_b], [gate_sb], out=gate_sb[:, :, 0:nb],
                   in_=ps_b[:, 256:320].rearrange("p (h f) -> p h f", h=2)[:, :, 0:nb])
            for h in range(2):
                dve.op("max", [gate_sb], [top8], out=top8[:, h, :], in_=gate_sb[:, h, :])
                dve.op("tensor_scalar", [gate_sb, top8], [qa], out=qa[:, h, 64:64 + nb], in0=gate_sb[:, h, 0:nb],
                       scalar1=top8[:, h, 2:3], scalar2=NEG, op0=ALU.is_lt, op1=ALU.mult)

    def qaug_T(h, dst_buf, dst_ap):
        tr(ps_t, ps_t[0:96, 2 + h, :], qa[:, h, :], [qa, identb], identb[:])
        act.op("activation", [ps_t], [dst_buf], out=dst_ap, in_=ps_t[0:96, 2 + h, :], func=AF.Copy)

    def store_kv_tile(kt, rows=128):
        for h in range(2):
            tr(ps_t, ps_t[0:64, h, :], kvb[:, h * 64:(h + 1) * 64], [kvb, identb], identb[:])
        for h in range(2):
            act.op("activation", [ps_t], [KT[h]], out=KT[h][0:64, kt * 128:(kt + 1) * 128], in_=ps_t[0:64, h, :], func=AF.Copy)
        dve.op("tensor_copy", [kvb], [VA], out=VA[:, kt, :, 0:64], in_=kvb[:, 128:256].rearrange("p (h f) -> p h f", h=2))

    def kmean_update(kt, src_kb, src_ap_fn):
        for h in range(2):
            mm(ps_b, ps_b[0:64, 384 + h:385 + h], src_ap_fn(h), ones_b[:, 0:1], [src_kb, ones_b])
        if kt % 2 == 0:
            dve.op("tensor_copy", [ps_b], [ksum], out=ksum[:], in_=ps_b[0:64, 384:386])
        else:
            j = kt // 2
            for h in range(2):
                dve.op("tensor_scalar", [ps_b, ksum], [kmT[h]], out=kmT[h][:, j:j + 1], in0=ps_b[0:64, 384 + h:385 + h],
                       scalar1=ksum[:, h:h + 1], scalar2=1.0 / 256.0, op0=ALU.add, op1=ALU.mult)

    def attention(h, q_buf, q_ap, NQ, ktiles, o_ap):
        per = max(1, 512 // NQ)
        n = len(ktiles)
        gi = 0
        first = True
        i = 0
        while i < n:
            grp = []
            while i < n and len(grp) < per and (not grp or (ktiles[i][1] == 128 and grp[0][1] == 128)):
                grp.append(ktiles[i]); i += 1
            nk = grp[0][1]
            pb = PTb[gi % 2]; ps_s = ps_ss[gi % 2]; gi += 1
            for j, (kap, nk_, vap, mask) in enumerate(grp):
                mm(ps_s, ps_s[0:nk, j * NQ:(j + 1) * NQ], kap, q_ap, [KT[h], q_buf], start=True, stop=(mask is None))
                if mask is not None:
                    mm(ps_s, ps_s[0:nk, j * NQ:(j + 1) * NQ], identb[0:nk, 0:nk], mask, [identb, stair], start=False, stop=True)
            act.op("activation", [ps_s], [pb], out=pb[0:nk, 0:len(grp) * NQ], in_=ps_s[0:nk, 0:len(grp) * NQ], func=AF.Exp)
            for j, (kap, nk_, vap, mask) in enumerate(grp):
                last = (i == n and j == len(grp) - 1)
                mm(ps_o, ps_o[0:65, 0:NQ], vap, pb[0:nk, j * NQ:(j + 1) * NQ], [VA, pb], start=first, stop=last)
                first = False

    def gdn_chunk(P, c0, conv_in_buf, conv_in_ap, pz_rows_direct, pZ):
        g = G
        Pn = P
        conv = g["Xf"]
        cq = g["qkraw"]
        cv = g["u"]
        dsts = [cq[:, 0:P], cq[:, 128:128 + P], cv[:, 0:P]]
        dbuf = [cq, cq, cv]
        for c in range(3):
            e = dve
            e.op("tensor_scalar", [conv_in_buf, cw], [dbuf[c]], out=dsts[c], in0=conv_in_ap[:, c, 0:P],
                 scalar1=cw[:, c * 4:c * 4 + 1], scalar2=None, op0=ALU.mult)
            for tap in range(1, 4):
                e.op("scalar_tensor_tensor", [conv_in_buf, cw, dbuf[c]], [dbuf[c]], out=dsts[c], in0=conv_in_ap[:, c, tap:tap + P],
                     scalar=cw[:, c * 4 + tap:c * 4 + tap + 1], in1=dsts[c], op0=ALU.mult, op1=ALU.add)
        act.op("activation", [cq], [cq], out=cq[:, 0:P], in_=cq[:, 0:P], func=AF.Silu)
        act.op("activation", [cq], [cq], out=cq[:, 128:128 + P], in_=cq[:, 128:128 + P], func=AF.Silu)
        act.op("activation", [cv], [cv], out=cv[:, 0:P], in_=cv[:, 0:P], func=AF.Silu)
        tr(ps_g, ps_g[0:P, 0:128], cq[:, 0:P], [cq, identf], identf[:])
        tr(ps_g, ps_g[0:P, 128:256], cq[:, 128:128 + P], [cq, identf], identf[:])
        tr(ps_g, ps_g[0:P, 256:384], cv[:, 0:P], [cv, identf], identf[:])
        act.op("activation", [ps_g], [g["Qsb"], g["bq"]], out=g["Qsb"][0:P, :], in_=ps_g[0:P, 0:128], func=AF.Square, accum_out=g["bq"][0:P, :])
        act.op("activation", [ps_g], [g["Ksb"], g["bk"]], out=g["Ksb"][0:P, :], in_=ps_g[0:P, 128:256], func=AF.Square, accum_out=g["bk"][0:P, :])
        for nm in ("bq", "bk"):
            dve.op("tensor_scalar", [g[nm]], [g[nm]], out=g[nm][0:P, :], in0=g[nm][0:P, :], scalar1=L2_EPS, scalar2=None, op0=ALU.add)
            act.op("activation", [g[nm]], [g[nm]], out=g[nm][0:P, :], in_=g[nm][0:P, :], func=AF.Sqrt)
            dve.op("reciprocal", [g[nm]], [g[nm]], out=g[nm][0:P, :], in_=g[nm][0:P, :])
        dve.op("tensor_scalar", [ps_g, g["bq"]], [g["Qb"]], out=g["Qb"][0:P, :], in0=ps_g[0:P, 0:128], scalar1=g["bq"][0:P, 0:1],
               scalar2=128.0 ** -0.5, op0=ALU.mult, op1=ALU.mult)
        dve.op("tensor_scalar", [ps_g, g["bk"]], [g["Ksb"]], out=g["Ksb"][0:P, :], in0=ps_g[0:P, 128:256], scalar1=g["bk"][0:P, 0:1],
               scalar2=None, op0=ALU.mult)
        act.op("activation", [g["Ksb"]], [g["Kb"]], out=g["Kb"][0:P, :], in_=g["Ksb"][0:P, :], func=AF.Copy)
        act.op("activation", [ps_g], [g["Vsb"]], out=g["Vsb"][0:P, :], in_=ps_g[0:P, 256:384], func=AF.Copy)
        tr(ps_tg, ps_t[:, 6, 0:P], g["Qb"][0:P, :], [g["Qb"], identb], identb[0:P, 0:P])
        tr(ps_tg, ps_t[:, 7, 0:P], g["Kb"][0:P, :], [g["Kb"], identb], identb[0:P, 0:P])
        act.op("activation", [ps_tg], [g["QTn"]], out=g["QTn"][:, 0:P], in_=ps_t[:, 6, 0:P], func=AF.Copy)
        act.op("activation", [ps_tg], [g["KTn"]], out=g["KTn"][:, 0:P], in_=ps_t[:, 7, 0:P], func=AF.Copy)
        if pz_rows_direct:
            bz_buf, bz = pZ, pZ[0:P, 128:130]
        else:
            mm(ps_h, ps_h[0:P, 384:386], identf[:, c0:c0 + P], pZ[:, 128:130], [identf, pZ])
            bz_buf, bz = ps_h, ps_h[0:P, 384:386]
        act.op("activation", [bz_buf], [g["beta"]], out=g["beta"][0:P, :], in_=bz[:, 0:1], func=AF.Sigmoid)
        act.op("activation", [bz_buf, dtbt], [g["sp_"]], out=g["sp_"][0:P, :], in_=bz[:, 1:2], func=AF.Exp, bias=dtbt[0:P, 0:1])
        act.op("activation", [g["sp_"]], [g["sp_"]], out=g["sp_"][0:P, :], in_=g["sp_"][0:P, :], func=AF.Ln, bias=1.0)
        dve.op("tensor_tensor", [g["sp_"], aneg], [g["gv"]], out=g["gv"][0:P, :], in0=g["sp_"][0:P, :], in1=aneg[0:P, :], op=ALU.mult)
        act.op("activation", [ones_f, g["gv"]], [g["grep"]], out=g["grep"][0:P, :], in_=ones_f[0:P, :], func=AF.Copy, scale=g["gv"][0:P, 0:1])
        mm(ps_h, ps_h[:, 0:P], g["grep"][0:P, :], tri[0:P, 0, 0:P], [g["grep"], tri])
        mm(ps_h, ps_h[0:P, 386:387], tri[0:P, 0, 0:P], g["gv"][0:P, 0:1], [tri, g["gv"]])
        dve.op("tensor_copy", [ps_h], [g["gcol"]], out=g["gcol"][0:P, :], in_=ps_h[0:P, 386:387])
        dve.op("tensor_scalar", [ps_h, g["gcol"]], [g["Dsb"]], out=g["Dsb"][0:P, 0:P], in0=ps_h[0:P, 0:P], scalar1=g["gcol"][0:P, 0:1],
               scalar2=0.0, op0=ALU.subtract, op1=ALU.min)
        act.op("activation", [g["Dsb"]], [g["E1"]], out=g["E1"][0:P, 0:P], in_=g["Dsb"][0:P, 0:P], func=AF.Exp)
        act.op("activation", [ps_h], [g["EG"]], out=g["EG"][:, 0:P], in_=ps_h[:, 0:P], func=AF.Exp)
        act.op("activation", [g["gcol"]], [g["egcol"]], out=g["egcol"][0:P, :], in_=g["gcol"][0:P, :], func=AF.Exp)
        pool.op("tensor_tensor", [g["E1"], tri], [g["E1i"]], out=g["E1i"][0:P, 0:P], in0=g["E1"][0:P, 0:P], in1=tri[0:P, 0, 0:P], op=ALU.mult)
        pool.op("tensor_tensor", [g["E1"], tri], [g["E1s"]], out=g["E1s"][0:P, 0:P], in0=g["E1"][0:P, 0:P], in1=tri[0:P, 1, 0:P], op=ALU.mult)
        mm(ps_g, ps_g[0:P, 384:384 + P], g["KTn"][:, 0:P], g["KTn"][:, 0:P], [g["KTn"]])
        dve.op("scalar_tensor_tensor", [ps_g, g["beta"], g["E1s"]], [g["NT"]], out=g["NT"][0:P, 0:P], in0=ps_g[0:P, 384:384 + P],
               scalar=g["beta"][0:P, 0:1], in1=g["E1s"][0:P, 0:P], op0=ALU.mult, op1=ALU.mult)
        mm(ps_g, ps_g[0:P, 0:P], g["KTn"][:, 0:P], g["QTn"][:, 0:P], [g["KTn"], g["QTn"]])
        dve.op("tensor_tensor", [ps_g, g["E1i"]], [g["attnT"]], out=g["attnT"][0:P, 0:P], in0=ps_g[0:P, 0:P], in1=g["E1i"][0:P, 0:P], op=ALU.mult)
        act.op("activation", [g["Vsb"]], [g["Xb"]], out=g["Xb"][0:P, 0:128], in_=g["Vsb"][0:P, :], func=AF.Copy)
        dve.op("tensor_scalar", [g["Ksb"], g["egcol"]], [g["Xb"]], out=g["Xb"][0:P, 128:256], in0=g["Ksb"][0:P, :],
               scalar1=g["egcol"][0:P, 0:1], scalar2=None, op0=ALU.mult)
        tr(ps_tg, ps_t[0:P, 6, 0:P], g["NT"][0:P, 0:P], [g["NT"], identb], identb[0:P, 0:P])
        act.op("activation", [ps_tg], [g["Zc"]], out=g["Zc"][0:P, 0:P], in_=ps_t[0:P, 6, 0:P], func=AF.Copy)
        W, WT = g["Y"][0], g["Y"][1]
        pool.op("tensor_tensor", [g["Zc"], bdm], [g["Zn"]], out=g["Zn"][0:P, 0:P], in0=g["Zc"][0:P, 0:P], in1=bdm[0:P, 0, 0:P], op=ALU.mult)
        pool.op("tensor_tensor", [g["NT"], bdm], [g["Yn"]], out=g["Yn"][0:P, 0:P], in0=g["NT"][0:P, 0:P], in1=bdm[0:P, 0, 0:P], op=ALU.mult)
        dve.op("tensor_tensor", [identb, g["Zn"]], [W], out=W[0:P, 0:P], in0=identb[0:P, 0:P], in1=g["Zn"][0:P, 0:P], op=ALU.subtract)
        dve.op("tensor_tensor", [identb, g["Yn"]], [WT], out=WT[0:P, 0:P], in0=identb[0:P, 0:P], in1=g["Yn"][0:P, 0:P], op=ALU.subtract)
        s_ = 2
        li = 1
        while s_ < P:
            C, CT, T1, T1p = g["Zn"], g["Yn"], g["Z"][0], g["Z"][1]
            pool.op("tensor_tensor", [g["Zc"], bdm], [C], out=C[0:P, 0:P], in0=g["Zc"][0:P, 0:P], in1=bdm[0:P, li, 0:P], op=ALU.mult)
            pool.op("tensor_tensor", [g["NT"], bdm], [CT], out=CT[0:P, 0:P], in0=g["NT"][0:P, 0:P], in1=bdm[0:P, li, 0:P], op=ALU.mult)
            mm(ps_g, ps_g[0:P, 128:128 + P], CT[0:P, 0:P], W[0:P, 0:P], [CT, W])
            mm(ps_g, ps_g[0:P, 256:256 + P], C[0:P, 0:P], WT[0:P, 0:P], [C, WT])
            act.op("activation", [ps_g], [T1], out=T1[0:P, 0:P], in_=ps_g[0:P, 128:128 + P], func=AF.Copy)
            act.op("activation", [ps_g], [T1p], out=T1p[0:P, 0:P], in_=ps_g[0:P, 256:256 + P], func=AF.Copy)
            mm(ps_h, ps_h[0:P, 0:P], WT[0:P, 0:P], T1[0:P, 0:P], [WT, T1])
            mm(ps_h, ps_h[0:P, 128:128 + P], W[0:P, 0:P], T1p[0:P, 0:P], [W, T1p])
            dve.op("tensor_tensor", [W, ps_h], [W], out=W[0:P, 0:P], in0=W[0:P, 0:P], in1=ps_h[0:P, 0:P], op=ALU.subtract)
            dve.op("tensor_tensor", [WT, ps_h], [WT], out=WT[0:P, 0:P], in0=WT[0:P, 0:P], in1=ps_h[0:P, 128:128 + P], op=ALU.subtract)
            s_ *= 2
            li += 1
        mm(ps_h, ps_h[0:P, 0:256], WT[0:P, 0:P], g["Xb"][0:P, :], [WT, g["Xb"]])
        dve.op("tensor_copy", [ps_h], [g["Xf"]], out=g["Xf"][0:P, :], in_=ps_h[0:P, 0:256])
        dve.op("tensor_scalar", [g["Xf"], g["beta"]], [g["u"]], out=g["u"][0:P, :], in0=g["Xf"][0:P, 0:128], scalar1=g["beta"][0:P, 0:1],
               scalar2=None, op0=ALU.mult)
        dve.op("tensor_scalar", [g["Xf"], g["beta"]], [g["wb"]], out=g["wb"][0:P, :], in0=g["Xf"][0:P, 128:256], scalar1=g["beta"][0:P, 0:1],
               scalar2=None, op0=ALU.mult)
        tr(ps_tg, ps_t[:, 7, 0:P], g["wb"][0:P, :], [g["wb"], identb], identb[0:P, 0:P])
        act.op("activation", [ps_tg], [g["wT"]], out=g["wT"][:, 0:P], in_=ps_t[:, 7, 0:P], func=AF.Copy)
        pool.op("tensor_tensor", [g["QTn"], g["EG"]], [g["QeT"]], out=g["QeT"][:, 0:P], in0=g["QTn"][:, 0:P], in1=g["EG"][:, 0:P], op=ALU.mult)
        dve.op("tensor_scalar", [g["Ksb"], g["E1"]], [g["Kd"]], out=g["Kd"][0:P, :], in0=g["Ksb"][0:P, :], scalar1=g["E1"][0:P, P - 1:P],
               scalar2=None, op0=ALU.mult)
        mm(ps_g, ps_g[0:P, 0:128], g["wT"][:, 0:P], g["Sb"][:, :], [g["wT"], g["Sb"]])
        dve.op("tensor_tensor", [g["u"], ps_g], [g["vnb"]], out=g["vnb"][0:P, :], in0=g["u"][0:P, :], in1=ps_g[0:P, 0:128], op=ALU.subtract)
        mm(ps_g, ps_g[0:P, 128:256], g["QeT"][:, 0:P], g["Sb"][:, :], [g["QeT"], g["Sb"]], start=True, stop=False)
        mm(ps_g, ps_g[0:P, 128:256], g["attnT"][0:P, 0:P], g["vnb"][0:P, :], [g["attnT"], g["vnb"]], start=False, stop=True)
        mm(ps_h, ps_h[:, 256:384], g["Kd"][0:P, :], g["vnb"][0:P, :], [g["Kd"], g["vnb"]])
        dve.op("scalar_tensor_tensor", [g["S"], g["EG"], ps_h], [g["S"]], out=g["S"][:, :], in0=g["S"][:, :], scalar=g["EG"][:, P - 1:P],
               in1=ps_h[:, 256:384], op0=ALU.mult, op1=ALU.add)
        act.op("activation", [g["S"]], [g["Sb"]], out=g["Sb"][:, :], in_=g["S"][:, :], func=AF.Copy)
        act.op("activation", [ps_g], [g["osb"], g["ssq"]], out=g["osb"][0:P, :], in_=ps_g[0:P, 128:256], func=AF.Square, accum_out=g["ssq"][0:P, :])
        dve.op("tensor_scalar", [g["ssq"]], [g["ssq"]], out=g["ssq"][0:P, :], in0=g["ssq"][0:P, :], scalar1=1.0 / 128.0, scalar2=RMS_EPS,
               op0=ALU.mult, op1=ALU.add)
        act.op("activation", [g["ssq"]], [g["ssq"]], out=g["ssq"][0:P, :], in_=g["ssq"][0:P, :], func=AF.Sqrt)
        dve.op("reciprocal", [g["ssq"]], [g["rn"]], out=g["rn"][0:P, :], in_=g["ssq"][0:P, :])
        dve.op("scalar_tensor_tensor", [ps_g, g["rn"], dnwt], [g["od"]], out=g["od"][0:P, :], in0=ps_g[0:P, 128:256], scalar=g["rn"][0:P, 0:1],
               in1=dnwt[0:P, :], op0=ALU.mult, op1=ALU.mult)

    def load_pass_weights(l, hg):
        for k in range(8):
            sp.dma(wstg[:], w_in[l, hg, k * 128:(k + 1) * 128, :], [w_in], [wstg], wstg)
            act.op("activation", [wstg], [wA], out=wA[:, k, :], in_=wstg[:, 0:512], func=AF.Copy)
            dve.op("tensor_copy", [wstg], [wZ], out=wZ[:, k, :], in_=wstg[:, 512:642])
            pool.op("tensor_copy", [wstg], [wD], out=wD[:, k, :], in_=wstg[:, 642:1026])
        sp.dma(cw[:], conv_w[l, hg, :, :], [conv_w], [cw], cw)
        sp.dma(aneg[:], alog[l, hg, :, :], [alog], [aneg], aneg)
        sp.dma(dtbt[:], dtb[l, hg, :, :], [dtb], [dtbt], dtbt)
        act.op("activation", [aneg], [aneg], out=aneg[:], in_=aneg[:], func=AF.Exp)
        dve.op("tensor_scalar", [aneg], [aneg], out=aneg[:], in0=aneg[:], scalar1=-1.0, scalar2=None, op0=ALU.mult)

    sp.dma(xst[:], xs[:, :], [xs], [xst], xst)

    for l in range(DEPTH):
        compute_mod(l, cTp, mod_p)
        compute_mod(l, cTs, mod_s)
        sp.dma(dnwt[:], dnw[l, :, :], [dnw], [dnwt], dnwt)
        sp.dma(lng[:], ln_g.t[l:l + 1, :].to_broadcast([128, D]), [ln_g], [lng], lng)
        sp.dma(lnb[:], ln_b.t[l:l + 1, :].to_broadcast([128, D]), [ln_b], [lnb], lnb)
        for k in range(8):
            sp.dma(wostg[:, 0:D], w_out[l, k * 128:(k + 1) * 128, :], [w_out], [wostg], wostg)
            act.op("activation", [wostg], [wo], out=wo[:, k, :], in_=wostg[:, 0:D], func=AF.Copy)
        xsrc = xp if l == 0 else x1

        for hg in range(4):
            load_pass_weights(l, hg)
            pZ = pZs[0]; xhist = xhists[0]; zdsil = zdsils[0]
            front(l, hg, xst, mod_s, coss, sins, pZ)
            sp.dma(nks[l, :, hg * 128:(hg + 1) * 128], qkr[:, 128:256], [qkr], [nks], qkr)
            sp.dma(nvs[l, :, hg * 128:(hg + 1) * 128], pA[:, 256:384], [pA], [nvs], pA)
            qT_and_gate(0, False)
            act.op("activation", [pA], [zsil], out=zsil[:, 0, :], in_=pA[:, 384:512], func=AF.Silu)
            act.op("activation", [pZ], [zdsil], out=zdsil[:], in_=pZ[:, 0:128], func=AF.Silu)
            dve.op("tensor_copy", [ps_a], [xhist], out=xhist[:, :, 3:131], in_=ps_a[:, 0:384].rearrange("p (c t) -> p c t", c=3))
            for h in range(2):
                tr(ps_t, ps_t[0:64, h, :], kvb[:, h * 64:(h + 1) * 64], [kvb, identb], identb[:])
            for h in range(2):
                act.op("activation", [ps_t], [knT[h]], out=knT[h][:], in_=ps_t[0:64, h, :], func=AF.Copy)
            knew = knT
            for j in range(NBS):
                r0 = 4 * j
                sp.dma(ptb[:], ptab.t[j:j + 1, :].to_broadcast([128, NPG]), [ptab], [ptb], ptb)
                dve.op("tensor_scalar", [ptb, iota4], [idxb], out=idxb[:], in0=ptb[:], scalar1=512.0, scalar2=iota4[:, 0:1], op0=ALU.mult, op1=ALU.add)
                dve.op("tensor_scalar", [idxb], [idx], out=idx[:], in0=idxb[:], scalar1=float(l * cfg.NPOOL * 512 + hg), scalar2=None, op0=ALU.add)
                for pg in range(NPG):
                    bk_ = pgk[pg % 2]
                    pgb = pgbs[pg % 2]
                    ptb_, so = (ps_t, 2) if pg % 2 == 0 else (ps_tg, 4)
                    pool.idma(bk_[:], cache_kv[:, :], idx[:, pg:pg + 1], [cache_kv, idx], [bk_], bk_)
                    act.op("activation", [bk_], [pgb], out=pgb[:, 0:128], in_=bk_[:, 0:128], func=AF.Copy)
                    for h in range(2):
                        tr(ptb_, ps_t[0:64, so + h, :], pgb[:, h * 64:(h + 1) * 64], [pgb, identb], identb[:])
                    for h in range(2):
                        act.op("activation", [ptb_], [KT[h]], out=KT[h][0:64, pg * 128:(pg + 1) * 128], in_=ps_t[0:64, so + h, :], func=AF.Copy)
                    dve.op("tensor_copy", [bk_], [VA], out=VA[:, pg, :, 0:64], in_=bk_[:, 128:256].rearrange("p (h f) -> p h f", h=2))
                    kmean_update(pg, pgb, lambda h: pgb[:, h * 64:(h + 1) * 64])
                for h in range(2):
                    dve.op("tensor_copy", [knew[h]], [KT[h]], out=KT[h][0:64, NK:NK + 4], in_=knew[h][0:64, r0:r0 + 4])
                mm(ps_b, ps_b[0:4, 0:128], identb[:, r0:r0 + 4], kvb[:, 128:256], [identb, kvb])
                dve.op("tensor_copy", [ps_b], [VA], out=VA[0:4, NKT, :, 0:64], in_=ps_b[0:4, 0:128].rearrange("p (h f) -> p h f", h=2))
                nb = NPG // 2
                for h in range(2):
                    mm(ps_b, ps_b[:, 256 + h * 32:256 + h * 32 + nb], qTt[h][:], kmT[h][:, 0:nb], [qTt[h], kmT[h]])
                dve.op("memset", [], [qa], qa[:, :, 64:96], 0.0)
                if nb > 3:
                    dve.op("memset", [], [gate_sb], gate_sb[:], -1e30)
                    dve.op("tensor_copy", [ps_b], [gate_sb], out=gate_sb[:, :, 0:nb],
                           in_=ps_b[:, 256:320].rearrange("p (h f) -> p h f", h=2)[:, :, 0:nb])
                    for h in range(2):
                        dve.op("max", [gate_sb], [top8], out=top8[:, h, :], in_=gate_sb[:, h, :])
                        dve.op("tensor_scalar", [gate_sb, top8], [qa], out=qa[:, h, 64:64 + nb], in0=gate_sb[:, h, 0:nb],
                               scalar1=top8[:, h, 2:3], scalar2=NEG, op0=ALU.is_lt, op1=ALU.mult)
                for h in range(2):
                    qaug_T(h, QT[h], QT[h][:, 0:128])
                    ktl = [(KT[h][:, kt * 128:(kt + 1) * 128], 128, VA[:, kt, h, :], None) for kt in range(NPG)]
                    ktl.append((KT[h][:, NK:NK + 4], 4, VA[0:4, NKT, h, :], stair[0:4, 0, 0:4]))
                    attention(h, QT[h], QT[h][:, r0:r0 + 4], 4, ktl, None)
                    dve.op("tensor_copy", [ps_o], [OTs], out=OTs[:, 0:4], in_=ps_o[0:65, 0:4])
                    tr(ps_b, ps_b[0:4, 128:193], OTs[0:65, 0:4], [OTs, identf], identf[0:65, 0:65])
                    dve.op("reciprocal", [ps_b], [rden], out=rden[0:4, :], in_=ps_b[0:4, 192:193])
                    dve.op("tensor_scalar", [ps_b, rden], [oas], out=oas[0:4, h * 64:(h + 1) * 64], in0=ps_b[0:4, 128:192], scalar1=rden[0:4, 0:1],
                           scalar2=None, op0=ALU.mult)
                sp.dma(xhs[:, :, 0:3], sconv[l, j, hg, :, :].rearrange("p (c t) -> p c t", c=3), [sconv], [xhs], xhs)
                dve.op("tensor_copy", [xhist], [xhs], out=xhs[:, :, 3:7], in_=xhist[:, :, 3 + r0:3 + r0 + 4])
                sp.dma(G["S"][:], ssm0[l, j, hg, :, :], [ssm0], [G["S"]], G["S"])
                act.op("activation", [G["S"]], [G["Sb"]], out=G["Sb"][:], in_=G["S"][:], func=AF.Copy)
                gdn_chunk(4, r0, xhs, xhs[:, :, 0:7], False, pZ)
                sp.dma(ssms[l, j, hg, :, :], G["S"][:], [G["S"]], [ssms], G["S"])
                sp.dma(convs[l, j, hg, :, :].rearrange("p (c t) -> p c t", c=3), xhs[:, :, 4:7], [xhs], [convs], xhs)
                act.op("activation", [G["od"]], [oas], out=oas[0:4, 128:256], in_=G["od"][0:4, :], func=AF.Copy)
                mm(ps_a, ps_a[:, 0:256], esel[0:4, j, :], oas[0:4, :], [esel, oas], start=(j == 0), stop=(j == NBS - 1))
            dve.op("tensor_tensor", [ps_a, zsil], [mixs], out=mixs[:, hg * 256:hg * 256 + 128], in0=ps_a[:, 0:128], in1=zsil[:, 0, :], op=ALU.mult)
            dve.op("tensor_tensor", [ps_a, zdsil], [mixs], out=mixs[:, hg * 256 + 128:hg * 256 + 256], in0=ps_a[:, 128:256], in1=zdsil[:], op=ALU.mult)

            dve.op("memset", [], [G["S"]], G["S"][:], 0.0)
            dve.op("memset", [], [G["Sb"]], G["Sb"][:], 0.0)
            dve.op("memset", [], [xhists[1]], xhists[1][:, :, 128:131], 0.0)
            dve.op("memset", [], [gate_sb], gate_sb[:], -1e30)

            def stream_b(i):
                p = i % 2
                c = i % 4
                pZ, xh, zd = pZs[p], xhists[p], zdsils[p]
                sp.dma(xt[:], xsrc[i * 128:(i + 1) * 128, :], [xsrc], [xt], xt)
                sp.dma(cost[:], c_cosp[i * 128:(i + 1) * 128, :], [c_cosp], [cost], cost)
                sp.dma(sint[:], c_sinp[i * 128:(i + 1) * 128, :], [c_sinp], [sint], sint)
                front(l, hg, xt, mod_p, cost, sint, pZ)
                sp.dma(nkp[l, i * 128:(i + 1) * 128, hg * 128:(hg + 1) * 128], qkr[:, 128:256], [qkr], [nkp], qkr)
                sp.dma(nvp[l, i * 128:(i + 1) * 128, hg * 128:(hg + 1) * 128], pA[:, 256:384], [pA], [nvp], pA)
                act.op("activation", [pA], [zsil], out=zsil[:, c, :], in_=pA[:, 384:512], func=AF.Silu)
                act.op("activation", [pZ], [zd], out=zd[:], in_=pZ[:, 0:128], func=AF.Silu)
                dve.op("tensor_copy", [ps_a], [xh], out=xh[:, :, 3:131], in_=ps_a[:, 0:384].rearrange("p (c t) -> p c t", c=3))
                dve.op("tensor_copy", [xhists[1 - p]], [xh], out=xh[:, :, 0:3], in_=xhists[1 - p][:, :, 128:131])
                store_kv_tile(i)
                qT_and_gate(i // 2, True)
                kmean_update(i, kvb, lambda h: kvb[:, h * 64:(h + 1) * 64])
                for h in range(2):
                    qaug_T(h, QT[h], QT[h][:, c * 128:(c + 1) * 128])

            def stream_b_attn(i):
                c = i % 4
                if c == 3 or i == NT - 1:
                    g0 = (i // 4) * 4
                    ng = i - g0 + 1
                    NQ = ng * 128
                    for h in range(2):
                        ktl = []
                        for kt in range(0, i + 1):
                            mask = stair[:, kt - g0, 0:NQ] if kt >= g0 else None
                            ktl.append((KT[h][:, kt * 128:(kt + 1) * 128], 128, VA[:, kt, h, :], mask))
                        attention(h, QT[h], QT[h][:, 0:NQ], NQ, ktl, None)
                        dve.op("tensor_copy", [ps_o], [OTs], out=OTs[:, 0:NQ], in_=ps_o[0:65, 0:NQ])
                        for cc in range(ng):
                            tr(ps_b, ps_b[:, 0:65], OTs[0:65, cc * 128:(cc + 1) * 128], [OTs, identf], identf[0:65, 0:65])
                            dve.op("reciprocal", [ps_b], [rden], out=rden[:], in_=ps_b[:, 64:65])
                            dve.op("scalar_tensor_tensor", [ps_b, rden, zsil], [mix], out=mix[:, h * 64:(h + 1) * 64], in0=ps_b[:, 0:64],
                                   scalar=rden[:, 0:1], in1=zsil[:, cc, h * 64:(h + 1) * 64], op0=ALU.mult, op1=ALU.mult)
                            ti = g0 + cc
                            sp.dma(mixp[ti * 128:(ti + 1) * 128, hg * 256 + h * 64:hg * 256 + (h + 1) * 64], mix[:, h * 64:(h + 1) * 64],
                                   [mix], [mixp], mix)

            def stream_a(i):
                p = i % 2
                pZ, xh, zd = pZs[p], xhists[p], zdsils[p]
                gdn_chunk(128, 0, xh, xh[:, :, :], True, pZ)
                dve.op("tensor_tensor", [G["od"], zd], [mixd], out=mixd[:, :], in0=G["od"][:, :], in1=zd[:], op=ALU.mult)
                sp.dma(mixp[i * 128:(i + 1) * 128, hg * 256 + 128:hg * 256 + 256], mixd[:, :], [mixd], [mixp], mixd)

            stream_b(0)
            stream_b_attn(0)
            MERGE_ATTN = bool(os.environ.get("K_MERGE_ATTN"))
            KSTOP = int(os.environ.get("K_STOP", "100000"))
            for i in range(NT):
                if i >= KSTOP:
                    break
                A = kb.record(lambda: stream_a(i))
                if i + 1 < NT:
                    B = kb.record(lambda: (stream_b(i + 1), stream_b_attn(i + 1) if MERGE_ATTN else None))
                else:
                    B = []
                kb.merge_emit(A, B)
                if i + 1 < NT and not MERGE_ATTN:
                    stream_b_attn(i + 1)
            xhist = xhists[(NT - 1) % 2]
            dve.op("tensor_copy", [xhist], [xhs], out=xhs[:, :, 0:3], in_=xhist[:, :, 128:131])
            sp.dma(ssmp[l, hg, :, :], G["S"][:], [G["S"]], [ssmp], G["S"])
            sp.dma(convp[l, hg, :, :].rearrange("p (c t) -> p c t", c=3), xhs[:, :, 0:3], [xhs], [convp], xhs)

        if cfg.DEBUG:
            sp.dma(dbg_mixs[l, :, :], mixs[:], [mixs], [dbg_mixs], mixs)
            out_bufs.append(dbg_mixs)

        def phase2(mix_buf, mix_ap, xtile, mod, out_buf, out_ap, keep=None):
            for rnd in range(2):
                for k in range(4):
                    tr(ps_t, ps_t[:, k, :], mix_ap[:, (rnd * 4 + k) * 128:(rnd * 4 + k + 1) * 128], [mix_buf, identb], identb[:])
                act.op("activation", [ps_t], [mT], out=mT[:, rnd * 4:rnd * 4 + 4, :], in_=ps_t[:, 0:4, :], func=AF.Copy)
            for half, pb_ in ((0, ps_a), (1, ps_b)):
                for k in range(8):
                    mm(pb_, pb_[:, :], mT[:, k, :], wo[:, k, half * 512:(half + 1) * 512], [mT, wo], start=(k == 0), stop=(k == 7))
                dve.op("tensor_tensor", [pb_, mod], [rt], out=rt[:, half * 512:(half + 1) * 512], in0=pb_[:, :],
                       in1=mod[:, 2 * D + half * 512:2 * D + (half + 1) * 512], op=ALU.mult)
            dve.op("scalar_tensor_tensor", [xtile, rt], [rt], out=rt[:], in0=xtile[:], scalar=cfg.ALPHA, in1=rt[:], op0=ALU.mult, op1=ALU.add)
            for half in range(2):
                dve.op("bn_stats", [rt], [stats], out=stats[:, half, :], in_=rt[:, half * 512:(half + 1) * 512])
            dve.op("bn_aggr", [stats], [mv], out=mv[:], in_=stats[:].rearrange("p a b -> p (a b)"))
            dve.op("tensor_scalar", [mv], [rstd], out=rstd[:], in0=mv[:, 1:2], scalar1=LN_EPS, scalar2=None, op0=ALU.add)
            act.op("activation", [rstd], [rstd], out=rstd[:], in_=rstd[:], func=AF.Sqrt)
            dve.op("reciprocal", [rstd], [rstd], out=rstd[:], in_=rstd[:])
            dve.op("tensor_scalar", [rt, mv, rstd], [rt], out=rt[:], in0=rt[:], scalar1=mv[:, 0:1], scalar2=rstd[:, 0:1], op0=ALU.subtract, op1=ALU.mult)
            pool.op("tensor_tensor", [rt, lng], [rt], out=rt[:], in0=rt[:], in1=lng[:], op=ALU.mult)
            tgt = keep if keep is not None else rt
            dve.op("tensor_tensor", [rt, lnb], [tgt], out=tgt[:], in0=rt[:], in1=lnb[:], op=ALU.add)
            if out_buf is not None:
                sp.dma(out_ap, tgt[:], [tgt], [out_buf], tgt)

        for i in range(NT):
            sp.dma(xt[:], xsrc[i * 128:(i + 1) * 128, :], [xsrc], [xt], xt)
            sp.dma(mixl[:], mixp[i * 128:(i + 1) * 128, :], [mixp], [mixl], mixl)
            if l == DEPTH - 1:
                phase2(mixl, mixl, xt, mod_p, yp, yp[i * 128:(i + 1) * 128, :])
            else:
                phase2(mixl, mixl, xt, mod_p, x1, x1[i * 128:(i + 1) * 128, :])
        if l == DEPTH - 1:
            phase2(mixs, mixs, xst, mod_s, ys, ys[:, :])
        else:
            phase2(mixs, mixs, xst, mod_s, None, None, keep=xst)

    for e in (sp, pool, act, dve, pe):
        e.wait_all(out_bufs)
    return kb


_CACHE = {}


def _consts(cfg):
    SEQ, NK = cfg.SEQ, cfg.NK
    bf = ml_dtypes.bfloat16
    c = {}
    c["c_identf"] = np.eye(128, dtype=np.float32)
    c["c_identb"] = np.eye(128, dtype=np.float32).astype(bf)
    t = np.arange(128)
    tri = np.zeros((128, 2, 128), np.float32)
    tri[:, 0, :] = (t[:, None] <= t[None, :])
    tri[:, 1, :] = (t[:, None] < t[None, :])
    c["c_tri"] = tri
    st = np.zeros((128, 4, 512), np.float32)
    for r in range(4):
        for cc in range(4):
            blk = st[:, r, cc * 128:(cc + 1) * 128]
            if cc < r:
                blk[:] = NEG
            elif cc == r:
                blk[:] = np.where(t[:, None] <= t[None, :], 0.0, NEG)
    c["c_stair"] = st.astype(bf)
    e = np.zeros((32, NK + 128), np.float32)
    pos = np.arange(NK)
    for j in range(32):
        e[j, :NK] = (pos // 256 == j)
    c["c_eblk"] = e.astype(bf)
    half = 32
    inv = (10000.0 ** (-np.arange(half, dtype=np.float32) / half)).astype(np.float32)

    def tab(pos):
        ang = pos.astype(np.float32)[:, None] * inv[None, :]
        return np.tile(np.cos(ang).astype(np.float32), (1, 4)), np.tile(np.sin(ang).astype(np.float32), (1, 4))
    c["c_cosp"], c["c_sinp"] = tab(np.arange(SEQ))
    ps = cfg.PAST + (np.arange(128) % 4)
    cs, sn = tab(ps)
    c["c_coss"], c["c_sins"] = cs, sn
    c["c_iota4"] = (4 * np.arange(128, dtype=np.float32)).reshape(128, 1)
    es = np.zeros((4, cfg.NBS, 128), np.float32)
    for j in range(cfg.NBS):
        for tt in range(4):
            es[tt, j, 4 * j + tt] = 1.0
    c["c_esel"] = es.astype(bf)
    bd = np.zeros((128, 7, 128), np.float32)
    prev = None
    for li, sz in enumerate([2, 4, 8, 16, 32, 64, 128]):
        m_ = (t[:, None] // sz == t[None, :] // sz).astype(np.float32)
        bd[:, li, :] = m_ if prev is None else m_ - prev
        prev = m_
    c["c_bdm"] = bd.astype(bf)
    return c


def _run(cfg, x_prompt, x_sample, c_prompt, c_sample, cache_k, cache_v, page_table,
         state_ssm, state_conv, w_ada, b_ada, w_in, conv_w, a_log, dt_bias,
         dn_norm_w, w_out, ln_g, ln_b):
    f = np.float32
    key = (cfg.SEQ, cfg.PAST, cfg.DEPTH, cfg.DECB)
    if key not in _CACHE:
        _CACHE[key] = build(cfg)
    kb = _CACHE[key]
    DEPTH = cfg.DEPTH
    A, Dq = 512, 1536
    w_in = np.asarray(w_in, f)
    win_r = np.zeros((DEPTH, 4, D, 1026), f)
    for hg in range(4):
        a0 = hg * 128
        cols = []
        for g in range(4):
            cols += list(range(g * 512 + a0, g * 512 + a0 + 128))
        cols += list(range(2048 + 1536 + a0, 2048 + 1536 + a0 + 128))
        cols += [2048 + 1536 + 512 + hg, 2048 + 1536 + 512 + 4 + hg]
        for g in range(3):
            cols += list(range(2048 + g * 512 + a0, 2048 + g * 512 + a0 + 128))
        win_r[:, hg] = w_in[:, :, cols]
    conv_w = np.asarray(conv_w, f)
    cw_r = np.zeros((DEPTH, 4, 128, 12), f)
    for hg in range(4):
        for g in range(3):
            blk = conv_w[:, :, g * 512 + hg * 128: g * 512 + (hg + 1) * 128]
            cw_r[:, hg, :, g * 4:(g + 1) * 4] = np.transpose(blk, (0, 2, 1))
    alog_r = np.broadcast_to(np.asarray(a_log, f)[:, :, None, None], (DEPTH, 4, 128, 1)).copy()
    dtb_r = np.broadcast_to(np.asarray(dt_bias, f)[:, :, None, None], (DEPTH, 4, 128, 1)).copy()
    dnw_r = np.broadcast_to(np.asarray(dn_norm_w, f)[:, None, :], (DEPTH, 128, 128)).copy()
    rows = []
    for hg in range(4):
        rows += list(range(hg * 128, (hg + 1) * 128)) + list(range(512 + hg * 128, 512 + (hg + 1) * 128))
    wout_r = np.ascontiguousarray(np.asarray(w_out, f)[:, rows, :])
    ckv = np.empty((np.asarray(cache_k).size // 128, 256), f)
    ckv[:, 0:128] = np.asarray(cache_k, f).reshape(-1, 128)
    ckv[:, 128:256] = np.asarray(cache_v, f).reshape(-1, 128)
    consts = _consts(cfg)
    state_conv = np.asarray(state_conv, f)
    state_ssm = np.asarray(state_ssm, f)
    in_maps = []
    NC_, NBS = cfg.NCORES, cfg.NBS
    cpb = NC_ // cfg.BATCH
    for c in range(NC_):
        b = c // cpb
        m = dict(consts)
        m["xp"] = np.ascontiguousarray(np.asarray(x_prompt, f)[b])
        xs_ = np.zeros((128, D), f)
        cTs = np.zeros((128, 8, 128), f)
        sc = np.zeros((DEPTH, NBS, 4, 128, 9), f)
        for j in range(NBS):
            sb_ = NBS * c + j
            xs_[4 * j:4 * j + 4] = np.asarray(x_sample, f)[sb_]
            cTs[:, :, 4 * j:4 * j + 4] = np.asarray(c_sample, f)[sb_].reshape(8, 128).T[:, :, None]
            for hg in range(4):
                for g in range(3):
                    blk = state_conv[:, sb_, :, g * 512 + hg * 128:g * 512 + (hg + 1) * 128]
                    sc[:, j, hg, :, g * 3:(g + 1) * 3] = np.transpose(blk, (0, 2, 1))
        m["xs"] = xs_
        m["cTs"] = cTs
        m["cTp"] = np.broadcast_to(np.asarray(c_prompt, f)[b].reshape(8, 128).T[:, :, None], (128, 8, 128)).copy()
        m["w_ada"] = np.asarray(w_ada, f); m["b_ada"] = np.asarray(b_ada, f)
        m["w_in"] = win_r; m["conv_w"] = cw_r; m["alog"] = alog_r; m["dtb"] = dtb_r; m["dnw"] = dnw_r
        m["w_out"] = wout_r; m["ln_g"] = np.asarray(ln_g, f); m["ln_b"] = np.asarray(ln_b, f)
        m["cache_kv"] = ckv
        m["ptab"] = np.ascontiguousarray(np.asarray(page_table, np.int32)[NBS * c:NBS * c + NBS])
        m["ssm0"] = np.ascontiguousarray(state_ssm[:, NBS * c:NBS * c + NBS])
        m["sconv"] = sc
        in_maps.append(m)
    res = run_bass_kernel_spmd(kb.nc, in_maps, core_ids=list(range(NC_)))
    R = res.results
    B, SEQ, DECB = cfg.BATCH, cfg.SEQ, cfg.DECB
    y_prompt = np.stack([R[cpb * b]["yp"] for b in range(B)]).astype(f)
    y_sample = np.zeros((DECB, 4, D), f)
    nk_p = np.zeros((DEPTH, B, SEQ, 8, 64), f); nv_p = np.zeros_like(nk_p)
    ssm_p = np.zeros((DEPTH, B, 4, 128, 128), f); conv_p = np.zeros((DEPTH, B, 3, 1536), f)
    nk_s = np.zeros((DEPTH, DECB, 4, 8, 64), f); nv_s = np.zeros_like(nk_s)
    ssm_s = np.zeros((DEPTH, DECB, 4, 128, 128), f); conv_s = np.zeros((DEPTH, DECB, 3, 1536), f)
    for b in range(B):
        r = R[cpb * b]
        nk_p[:, b] = r["nkp"].reshape(DEPTH, SEQ, 8, 64)
        nv_p[:, b] = r["nvp"].reshape(DEPTH, SEQ, 8, 64)
        ssm_p[:, b] = r["ssmp"]
        cp = r["convp"].reshape(DEPTH, 4, 128, 3, 3)
        for hg in range(4):
            for g in range(3):
                conv_p[:, b, :, g * 512 + hg * 128:g * 512 + (hg + 1) * 128] = np.transpose(cp[:, hg, :, g, :], (0, 2, 1))
    for c in range(NC_):
        r = R[c]
        for j in range(NBS):
            sb_ = NBS * c + j
            y_sample[sb_] = r["ys"][4 * j:4 * j + 4]
            nk_s[:, sb_] = r["nks"][:, 4 * j:4 * j + 4].reshape(DEPTH, 4, 8, 64)
            nv_s[:, sb_] = r["nvs"][:, 4 * j:4 * j + 4].reshape(DEPTH, 4, 8, 64)
            ssm_s[:, sb_] = r["ssms"][:, j]
            cs = r["convs"][:, j].reshape(DEPTH, 4, 128, 3, 3)
            for hg in range(4):
                for g in range(3):
                    conv_s[:, sb_, :, g * 512 + hg * 128:g * 512 + (hg + 1) * 128] = np.transpose(cs[:, hg, :, g, :], (0, 2, 1))
    return (y_prompt, y_sample, nk_p, nv_p, ssm_p, conv_p, nk_s, nv_s, ssm_s, conv_s)


def kernel(**inputs):
    cfg = Cfg()
    return _run(cfg, **inputs)
```
